# Optimizing a Trainium2 kernel written in Bass

```python
import math
import jax
import jax.numpy as jnp
from jax import lax
import numpy as np

D_MODEL = 2048
BATCH = 4
SEQ = 4096
DEPTH = 2
DEC_BATCH = 16
DEC_SEQ = 64
PAST_LEN = 2048

CHUNK = 64
Q_BLOCK = 128
N_EVEN = (DEPTH + 1) // 2
N_ODD = DEPTH // 2
EPS = 1e-6
NEG_INF = -1e30

S5_WIDTH = D_MODEL // 2
S5_GROUP = 16
S5_GROUPS = S5_WIDTH // S5_GROUP
S5_STATE = 64
DIFF_WIDTH = D_MODEL - S5_WIDTH
DIFF_DK = 64
DIFF_DV = 2 * DIFF_DK
DIFF_HEADS = DIFF_WIDTH // DIFF_DV
EVEN_IN = S5_WIDTH + 3 * DIFF_WIDTH

MLA_HEADS = D_MODEL // 128
MLA_NOPE = 128
MLA_ROPE = 64
MLA_V = 128
MLA_Q_RANK = D_MODEL // 4
MLA_KV_RANK = D_MODEL // 4
ODD_IN = MLA_Q_RANK + MLA_KV_RANK + MLA_ROPE
ROPE_BASE = 10000.0

D_FF = 11 * D_MODEL // 4
CONV_W = 3

kernel_name = 'hybrid_s5_diffattn_mla_convffn_stream_step'


def rmsnorm(x, g):
    x32 = x.astype(jnp.float32)
    y = x32 * lax.rsqrt(jnp.mean(x32 * x32, axis=-1, keepdims=True) + EPS)
    return (y * g.astype(jnp.float32)).astype(x.dtype)


def chunk_mask(q_pos, k_pos):
    return (k_pos[None, :] // CHUNK) <= (q_pos[:, None] // CHUNK)


def masked_softmax(s, mask):
    return jax.nn.softmax(jnp.where(mask, s, NEG_INF), axis=-1)


def rope(x, pos):
    half = x.shape[-1] // 2
    inv = ROPE_BASE ** (-jnp.arange(half, dtype=jnp.float32) / half)
    ang = pos.astype(jnp.float32)[:, None] * inv[None, :]
    shape = (1, pos.shape[0]) + (1,) * (x.ndim - 3) + (half,)
    cos = jnp.cos(ang).reshape(shape)
    sin = jnp.sin(ang).reshape(shape)
    x32 = x.astype(jnp.float32)
    x1, x2 = x32[..., :half], x32[..., half:]
    return jnp.concatenate([x1 * cos - x2 * sin, x2 * cos + x1 * sin], axis=-1).astype(x.dtype)


def sweep_query_blocks(fn, qs, q_pos):
    t = q_pos.shape[0]
    if t <= Q_BLOCK:
        return fn(qs, q_pos)
    n_blk = t // Q_BLOCK

    def body(i):
        start = i * Q_BLOCK
        blk = tuple(lax.dynamic_slice_in_dim(q, start, Q_BLOCK, axis=1) for q in qs)
        return fn(blk, lax.dynamic_slice_in_dim(q_pos, start, Q_BLOCK))

    out = lax.map(body, jnp.arange(n_blk, dtype=jnp.int32))
    out = jnp.moveaxis(out, 0, 1)
    return out.reshape((out.shape[0], t) + out.shape[3:])


def diff_attention(q1, q2, k1, k2, v, q_pos, k_pos, lam):
    scale = DIFF_DK ** -0.5

    def block(qs, qp):
        b1, b2 = qs
        mask = chunk_mask(qp, k_pos)
        s1 = jnp.einsum('bqhd,bkhd->bhqk', b1, k1, preferred_element_type=jnp.float32) * scale
        s2 = jnp.einsum('bqhd,bkhd->bhqk', b2, k2, preferred_element_type=jnp.float32) * scale
        p = masked_softmax(s1, mask) - lam * masked_softmax(s2, mask)
        return jnp.einsum('bhqk,bkhd->bqhd', p.astype(v.dtype), v)

    return sweep_query_blocks(block, (q1, q2), q_pos)


def mla_attention(q_nope, q_pe, ckv, kpe, w_uk, w_uv, q_pos, k_pos):
    scale = (MLA_NOPE + MLA_ROPE) ** -0.5

    def block(qs, qp):
        qn, qr = qs
        q_lat = jnp.einsum('bqhn,rhn->bqhr', qn, w_uk)
        s = (jnp.einsum('bqhr,bkr->bhqk', q_lat, ckv, preferred_element_type=jnp.float32)
             + jnp.einsum('bqhp,bkp->bhqk', qr, kpe, preferred_element_type=jnp.float32)) * scale
        p = masked_softmax(s, chunk_mask(qp, k_pos)).astype(ckv.dtype)
        o_lat = jnp.einsum('bhqk,bkr->bqhr', p, ckv)
        return jnp.einsum('bqhr,rhv->bqhv', o_lat, w_uv)

    return sweep_query_blocks(block, (q_nope, q_pe), q_pos)


def s5_mix(u, h0_re, h0_im, w, i):
    f32 = jnp.float32
    b, t, _ = u.shape
    u32 = u.astype(f32).reshape(b, t, S5_GROUPS, S5_GROUP)
    lam_re = jnp.minimum(w['s5_a_re'][i].astype(f32), -1e-4)
    lam_im = w['s5_a_im'][i].astype(f32)
    dt = jnp.exp(w['s5_log_dt'][i].astype(f32))[:, None]
    mag = jnp.exp(lam_re * dt)
    ab_re = mag * jnp.cos(lam_im * dt)
    ab_im = mag * jnp.sin(lam_im * dt)
    den = lam_re * lam_re + lam_im * lam_im
    co_re = ((ab_re - 1.0) * lam_re + ab_im * lam_im) / den
    co_im = (ab_im * lam_re - (ab_re - 1.0) * lam_im) / den
    b_re = w['s5_b_re'][i].astype(f32)
    b_im = w['s5_b_im'][i].astype(f32)
    bb_re = co_re[..., None] * b_re - co_im[..., None] * b_im
    bb_im = co_re[..., None] * b_im + co_im[..., None] * b_re
    x_re = jnp.einsum('gnp,btgp->tbgn', bb_re, u32)
    x_im = jnp.einsum('gnp,btgp->tbgn', bb_im, u32)
    a_re_t = jnp.broadcast_to(ab_re, (t, 1) + ab_re.shape)
    a_im_t = jnp.broadcast_to(ab_im, (t, 1) + ab_im.shape)

    def combine(e, l):
        ar, ai, xr, xi = e
        br, bi, yr, yi = l
        return (br * ar - bi * ai, br * ai + bi * ar,
                br * xr - bi * xi + yr, br * xi + bi * xr + yi)

    p_re, p_im, h_re, h_im = lax.associative_scan(combine, (a_re_t, a_im_t, x_re, x_im), axis=0)
    h0r = h0_re.astype(f32)
    h0i = h0_im.astype(f32)
    h_re = h_re + p_re * h0r - p_im * h0i
    h_im = h_im + p_re * h0i + p_im * h0r
    c_re = w['s5_c_re'][i].astype(f32)
    c_im = w['s5_c_im'][i].astype(f32)
    y = (jnp.einsum('gpn,tbgn->btgp', c_re, h_re) - jnp.einsum('gpn,tbgn->btgp', c_im, h_im)
         + w['s5_d'][i].astype(f32) * u32)
    y = y.reshape(b, t, S5_WIDTH)
    z = jax.nn.gelu(y)
    out = z * jax.nn.sigmoid(z @ w['s5_w_glu'][i].astype(f32) + w['s5_b_glu'][i].astype(f32))
    return out.astype(u.dtype), h_re[-1], h_im[-1]


def even_mixer(h, h0_re, h0_im, k_past, v_past, q_pos, k_pos, w, i, lam_init):
    b, t, _ = h.shape
    proj = h @ w['w_in_even'][i]
    u, q, k, v = jnp.split(proj, [S5_WIDTH, S5_WIDTH + DIFF_WIDTH, S5_WIDTH + 2 * DIFF_WIDTH], axis=-1)
    q = q.reshape(b, t, DIFF_HEADS, 2 * DIFF_DK)
    k = k.reshape(b, t, DIFF_HEADS, 2 * DIFF_DK)
    v = v.reshape(b, t, DIFF_HEADS, DIFF_DV)
    y_s5, hT_re, hT_im = s5_mix(u, h0_re, h0_im, w, i)
    k_all = k if k_past is None else jnp.concatenate([k_past, k], axis=1)
    v_all = v if v_past is None else jnp.concatenate([v_past, v], axis=1)
    f32 = jnp.float32
    lam = (jnp.exp(jnp.sum(w['diff_lambda_q1'][i].astype(f32) * w['diff_lambda_k1'][i].astype(f32)))
           - jnp.exp(jnp.sum(w['diff_lambda_q2'][i].astype(f32) * w['diff_lambda_k2'][i].astype(f32)))
           + lam_init)
    o = diff_attention(q[..., :DIFF_DK], q[..., DIFF_DK:], k_all[..., :DIFF_DK], k_all[..., DIFF_DK:],
                       v_all, q_pos, k_pos, lam)
    o = rmsnorm(o, w['diff_subln'][i]) * (1.0 - lam_init)
    mixed = jnp.concatenate([y_s5, o.reshape(b, t, DIFF_WIDTH)], axis=-1)
    return mixed @ w['w_out_even'][i], hT_re, hT_im, k, v


def odd_mixer(h, ckv_past, kpe_past, q_pos, k_pos, w, i):
    b, t, _ = h.shape
    cq, ckv, kpe = jnp.split(h @ w['w_in_odd'][i], [MLA_Q_RANK, MLA_Q_RANK + MLA_KV_RANK], axis=-1)
    q = jnp.einsum('btr,rhd->bthd', rmsnorm(cq, w['mla_q_norm'][i]), w['mla_w_uq'][i])
    q_nope = q[..., :MLA_NOPE]
    q_pe = rope(q[..., MLA_NOPE:], q_pos)
    ckv = rmsnorm(ckv, w['mla_kv_norm'][i])
    kpe = rope(kpe, q_pos)
    ckv_all = ckv if ckv_past is None else jnp.concatenate([ckv_past, ckv], axis=1)
    kpe_all = kpe if kpe_past is None else jnp.concatenate([kpe_past, kpe], axis=1)
    w_ukv = w['mla_w_ukv'][i]
    o = mla_attention(q_nope, q_pe, ckv_all, kpe_all, w_ukv[..., :MLA_NOPE], w_ukv[..., MLA_NOPE:],
                      q_pos, k_pos)
    return o.reshape(b, t, MLA_HEADS * MLA_V) @ w['w_out_odd'][i], ckv, kpe


def conv_ffn(h, conv_state, w_in, conv_w, conv_b, w_down):
    val, gate = jnp.split(h @ w_in, 2, axis=-1)
    t = gate.shape[1]
    g = jnp.concatenate([conv_state.astype(gate.dtype), gate], axis=1)
    c = conv_b
    for j in range(CONV_W):
        c = c + conv_w[j] * g[:, j:j + t]
    out = (jax.nn.silu(c) * val) @ w_down
    return out, g[:, t:]


def trunk(x, s5_re0, s5_im0, k_past, v_past, ckv_past, kpe_past, conv0, q_pos, k_pos, w):
    s5_re, s5_im, ks, vs, ckvs, kpes, convs = [], [], [], [], [], [], []
    for layer in range(DEPTH):
        i = layer // 2
        h = rmsnorm(x, w['norm_mix'][layer])
        if layer % 2 == 0:
            lam_init = 0.8 - 0.6 * math.exp(-0.3 * layer)
            out, hr, hi, k, v = even_mixer(
                h, s5_re0[i], s5_im0[i],
                None if k_past is None else k_past[i],
                None if v_past is None else v_past[i],
                q_pos, k_pos, w, i, lam_init)
            s5_re.append(hr)
            s5_im.append(hi)
            ks.append(k)
            vs.append(v)
        else:
            out, ckv, kpe = odd_mixer(
                h,
                None if ckv_past is None else ckv_past[i],
                None if kpe_past is None else kpe_past[i],
                q_pos, k_pos, w, i)
            ckvs.append(ckv)
            kpes.append(kpe)
        x = x + out
        f, cs = conv_ffn(rmsnorm(x, w['norm_ffn'][layer]), conv0[layer], w['ffn_w_in'][layer],
                         w['ffn_conv_w'][layer], w['ffn_conv_b'][layer], w['ffn_w_down'][layer])
        x = x + f
        convs.append(cs)
    return (rmsnorm(x, w['norm_final']), jnp.stack(s5_re), jnp.stack(s5_im), jnp.stack(ks),
            jnp.stack(vs), jnp.stack(ckvs), jnp.stack(kpes), jnp.stack(convs))


def setup_inputs(seed: int = 0) -> dict:
    key = jax.random.key(seed)
    keys = iter(jax.random.split(key, 48))
    f32 = jnp.float32

    def normal(shape, scale):
        return scale * jax.random.normal(next(keys), shape, f32)

    def gain(shape):
        return 1.0 + 0.02 * jax.random.normal(next(keys), shape, f32)

    a_im0 = math.pi * jnp.arange(S5_STATE, dtype=f32)
    return {
        'x_prompt': normal((BATCH, SEQ, D_MODEL), 1.0),
        'x_sample': normal((DEC_BATCH, DEC_SEQ, D_MODEL), 1.0),
        'state_s5_re': normal((N_EVEN, DEC_BATCH, S5_GROUPS, S5_STATE), 0.5),
        'state_s5_im': normal((N_EVEN, DEC_BATCH, S5_GROUPS, S5_STATE), 0.5),
        'cache_diff_k': normal((N_EVEN, DEC_BATCH, PAST_LEN, DIFF_HEADS, 2 * DIFF_DK), 1.0),
        'cache_diff_v': normal((N_EVEN, DEC_BATCH, PAST_LEN, DIFF_HEADS, DIFF_DV), 1.0),
        'cache_mla_ckv': normal((N_ODD, DEC_BATCH, PAST_LEN, MLA_KV_RANK), 1.0),
        'cache_mla_kpe': normal((N_ODD, DEC_BATCH, PAST_LEN, MLA_ROPE), 1.0),
        'state_ffn_conv': normal((DEPTH, DEC_BATCH, CONV_W - 1, D_FF), 1.0),
        'norm_mix': gain((DEPTH, D_MODEL)),
        'norm_ffn': gain((DEPTH, D_MODEL)),
        'norm_final': gain((D_MODEL,)),
        'w_in_even': normal((N_EVEN, D_MODEL, EVEN_IN), D_MODEL ** -0.5),
        'w_out_even': normal((N_EVEN, D_MODEL, D_MODEL), D_MODEL ** -0.5),
        's5_a_re': -0.5 + normal((N_EVEN, S5_GROUPS, S5_STATE), 0.01),
        's5_a_im': a_im0 + normal((N_EVEN, S5_GROUPS, S5_STATE), 0.01),
        's5_b_re': normal((N_EVEN, S5_GROUPS, S5_STATE, S5_GROUP), (2 * S5_GROUP) ** -0.5),
        's5_b_im': normal((N_EVEN, S5_GROUPS, S5_STATE, S5_GROUP), (2 * S5_GROUP) ** -0.5),
        's5_c_re': normal((N_EVEN, S5_GROUPS, S5_GROUP, S5_STATE), (2 * S5_STATE) ** -0.5),
        's5_c_im': normal((N_EVEN, S5_GROUPS, S5_GROUP, S5_STATE), (2 * S5_STATE) ** -0.5),
        's5_d': normal((N_EVEN, S5_GROUPS, S5_GROUP), 1.0),
        's5_log_dt': jax.random.uniform(next(keys), (N_EVEN, S5_GROUPS), f32,
                                        math.log(1e-3), math.log(1e-1)),
        's5_w_glu': normal((N_EVEN, S5_WIDTH, S5_WIDTH), S5_WIDTH ** -0.5),
        's5_b_glu': normal((N_EVEN, S5_WIDTH), 0.01),
        'diff_lambda_q1': normal((N_EVEN, DIFF_DK), 0.1),
        'diff_lambda_k1': normal((N_EVEN, DIFF_DK), 0.1),
        'diff_lambda_q2': normal((N_EVEN, DIFF_DK), 0.1),
        'diff_lambda_k2': normal((N_EVEN, DIFF_DK), 0.1),
        'diff_subln': gain((N_EVEN, DIFF_DV)),
        'w_in_odd': normal((N_ODD, D_MODEL, ODD_IN), D_MODEL ** -0.5),
        'mla_q_norm': gain((N_ODD, MLA_Q_RANK)),
        'mla_kv_norm': gain((N_ODD, MLA_KV_RANK)),
        'mla_w_uq': normal((N_ODD, MLA_Q_RANK, MLA_HEADS, MLA_NOPE + MLA_ROPE), MLA_Q_RANK ** -0.5),
        'mla_w_ukv': normal((N_ODD, MLA_KV_RANK, MLA_HEADS, MLA_NOPE + MLA_V), MLA_KV_RANK ** -0.5),
        'w_out_odd': normal((N_ODD, MLA_HEADS * MLA_V, D_MODEL), (MLA_HEADS * MLA_V) ** -0.5),
        'ffn_w_in': normal((DEPTH, D_MODEL, 2 * D_FF), D_MODEL ** -0.5),
        'ffn_conv_w': normal((DEPTH, CONV_W, D_FF), CONV_W ** -0.5),
        'ffn_conv_b': normal((DEPTH, D_FF), 0.01),
        'ffn_w_down': normal((DEPTH, D_FF, D_MODEL), D_FF ** -0.5),
    }


def reference(x_prompt, x_sample, state_s5_re, state_s5_im, cache_diff_k, cache_diff_v,
              cache_mla_ckv, cache_mla_kpe, state_ffn_conv, norm_mix, norm_ffn, norm_final,
              w_in_even, w_out_even, s5_a_re, s5_a_im, s5_b_re, s5_b_im, s5_c_re, s5_c_im,
              s5_d, s5_log_dt, s5_w_glu, s5_b_glu, diff_lambda_q1, diff_lambda_k1,
              diff_lambda_q2, diff_lambda_k2, diff_subln, w_in_odd, mla_q_norm, mla_kv_norm,
              mla_w_uq, mla_w_ukv, w_out_odd, ffn_w_in, ffn_conv_w, ffn_conv_b, ffn_w_down):
    w = {
        'norm_mix': norm_mix, 'norm_ffn': norm_ffn, 'norm_final': norm_final,
        'w_in_even': w_in_even, 'w_out_even': w_out_even,
        's5_a_re': s5_a_re, 's5_a_im': s5_a_im, 's5_b_re': s5_b_re, 's5_b_im': s5_b_im,
        's5_c_re': s5_c_re, 's5_c_im': s5_c_im, 's5_d': s5_d, 's5_log_dt': s5_log_dt,
        's5_w_glu': s5_w_glu, 's5_b_glu': s5_b_glu,
        'diff_lambda_q1': diff_lambda_q1, 'diff_lambda_k1': diff_lambda_k1,
        'diff_lambda_q2': diff_lambda_q2, 'diff_lambda_k2': diff_lambda_k2,
        'diff_subln': diff_subln,
        'w_in_odd': w_in_odd, 'mla_q_norm': mla_q_norm, 'mla_kv_norm': mla_kv_norm,
        'mla_w_uq': mla_w_uq, 'mla_w_ukv': mla_w_ukv, 'w_out_odd': w_out_odd,
        'ffn_w_in': ffn_w_in, 'ffn_conv_w': ffn_conv_w, 'ffn_conv_b': ffn_conv_b,
        'ffn_w_down': ffn_w_down,
    }
    b_p, t_p = x_prompt.shape[0], x_prompt.shape[1]
    q_pos_p = jnp.arange(t_p, dtype=jnp.int32)
    s5_zero = jnp.zeros((N_EVEN, b_p, S5_GROUPS, S5_STATE), jnp.float32)
    conv_zero = jnp.zeros((DEPTH, b_p, CONV_W - 1, D_FF), x_prompt.dtype)
    (y_prompt, s5_re_prompt, s5_im_prompt, diff_k_prompt, diff_v_prompt,
     mla_ckv_prompt, mla_kpe_prompt, ffn_conv_prompt) = trunk(
        x_prompt, s5_zero, s5_zero, None, None, None, None, conv_zero, q_pos_p, q_pos_p, w)
    past_len = cache_diff_k.shape[2]
    t_s = x_sample.shape[1]
    q_pos_s = past_len + jnp.arange(t_s, dtype=jnp.int32)
    k_pos_s = jnp.arange(past_len + t_s, dtype=jnp.int32)
    (y_sample, s5_re_sample, s5_im_sample, diff_k_sample, diff_v_sample,
     mla_ckv_sample, mla_kpe_sample, ffn_conv_sample) = trunk(
        x_sample, state_s5_re, state_s5_im, cache_diff_k, cache_diff_v, cache_mla_ckv,
        cache_mla_kpe, state_ffn_conv, q_pos_s, k_pos_s, w)
    return (y_prompt, y_sample, s5_re_prompt, s5_im_prompt, s5_re_sample, s5_im_sample,
            diff_k_prompt, diff_v_prompt, diff_k_sample, diff_v_sample,
            mla_ckv_prompt, mla_kpe_prompt, mla_ckv_sample, mla_kpe_sample,
            ffn_conv_prompt, ffn_conv_sample)
```

```python
import math
import numpy as np
import concourse.bass as bass
import concourse.mybir as mybir
from concourse.bass_utils import run_bass_kernel_spmd

F32 = mybir.dt.float32
BF16 = mybir.dt.bfloat16
I32 = mybir.dt.int32
ALU = mybir.AluOpType
AF = mybir.ActivationFunctionType
AX = mybir.AxisListType

ENGS = ("pe", "act", "dve", "pool", "sp")
RECYCLE_SEMS = True

D = 2048
DFF = 5632
NJ = DFF // 128
EPS = 1e-6
TWO_PI = 2.0 * math.pi


class T:
    __slots__ = ("name", "t", "ws", "rs", "sem", "cnt", "transient", "xw", "sem_sw", "cnt_sw")

    def __init__(self, name, t=None, transient=False):
        self.name = name
        self.t = t
        self.ws = {}
        self.rs = {}
        self.sem = None
        self.cnt = 0
        self.transient = transient
        self.xw = None
        self.sem_sw = None
        self.cnt_sw = 0

    def __getitem__(self, idx):
        return self.t[idx]


class _Frozen:
    __slots__ = ("sem",)

    def __init__(self, sem):
        self.sem = sem


class Op:
    __slots__ = ("eng", "fn", "deps", "signal", "tick", "dma", "sem", "semval", "sem_t", "key")

    def __init__(self, eng, fn, dma):
        self.eng = eng
        self.fn = fn
        self.deps = None
        self.signal = False
        self.tick = 0
        self.dma = dma
        self.sem = None
        self.semval = 0
        self.sem_t = None
        self.key = eng


class Prog:
    def __init__(self, nc):
        self.nc = nc
        self.ops = {e: [] for e in ENGS}
        self.stack = []
        self.engsem = {}
        self.last = {}
        self.semtiles = []
        self.pending_bar = {}
        self.nops = 0
        self.free_sems = []
        self.swtiles = []

    def sb(self, name, shape, dtype):
        cm = self.nc.sbuf_tensor(name, list(shape), dtype)
        h = cm.__enter__()
        self.stack.append(cm)
        return T(name, h)

    def ps(self, name, shape, dtype):
        cm = self.nc.psum_tensor(name, list(shape), dtype)
        h = cm.__enter__()
        self.stack.append(cm)
        return T(name, h)

    def new_sem(self, name):
        cm = self.nc.semaphore(name)
        h = cm.__enter__()
        self.stack.append(cm)
        return h

    def dram(self, name, shape, dtype, kind="Internal"):
        h = self.nc.dram_tensor(name, list(shape), dtype, kind=kind)
        return T(name, h.ap())

    def add(self, eng, fn, reads=(), writes=(), pwrites=(), dma=False, semtile=None):
        op = Op(eng, fn, dma)
        deps = []
        for t in reads:
            deps.extend(t.ws.values())
        for t in writes:
            deps.extend(t.ws.values())
            deps.extend(t.rs.values())
        for t in pwrites:
            deps.extend(t.rs.values())
            if t.xw is not None:
                deps.append(t.xw)
        res = []
        seen = set()
        for p in deps:
            if p is op or id(p) in seen:
                continue
            seen.add(id(p))
            if p.dma:
                if p.key.startswith("dmasw"):
                    res.append((p.sem, p.sem_t.cnt_sw))
                elif p.sem_t.sem is p.sem:
                    res.append((p.sem, p.sem_t.cnt))
                else:
                    res.append((p.sem, p.semval))
            else:
                if p.eng == eng and eng == "pe":
                    continue
                p.signal = True
                res.append(p)
        if eng in self.pending_bar:
            for p in self.pending_bar.pop(eng):
                if isinstance(p, tuple):
                    res.append(p)
                elif not (p.eng == eng and eng == "pe"):
                    p.signal = True
                    res.append(p)
        op.deps = res
        if dma and eng == "pool":
            st = semtile
            if st.sem_sw is None:
                st.sem_sw = self.new_sem("w_" + st.name)
                self.swtiles.append(st)
            st.cnt_sw += 16
            op.sem = st.sem_sw
            op.sem_t = st
            op.semval = st.cnt_sw
            op.key = "dmasw%d" % id(st)
        elif dma:
            st = semtile
            if st.sem is None:
                if self.free_sems and RECYCLE_SEMS:
                    st.sem, st.cnt = self.free_sems.pop()
                else:
                    st.sem = self.new_sem("s_" + st.name)
                self.semtiles.append(st)
            st.cnt += 16
            op.sem = st.sem
            op.sem_t = st
            op.semval = st.cnt
            op.key = "dma%d" % id(st)
        else:
            self.last[eng] = op
        k = op.key
        for t in reads:
            t.rs[k] = op
        for t in writes:
            t.ws[k] = op
            t.xw = op
        for t in pwrites:
            t.ws[k] = op
        self.ops[eng].append(op)
        self.nops += 1
        return op

    def barrier(self):
        deps = list(self.last.values()) + [(st.sem, st.cnt) for st in self.semtiles] + [(st.sem_sw, st.cnt_sw) for st in self.swtiles]
        for e in ENGS:
            self.pending_bar[e] = self.pending_bar.get(e, []) + deps
        keep = []
        for st in self.semtiles:
            if st.transient:
                self.free_sems.append((st.sem, st.cnt))
                st.sem = None
                st.cnt = 0
            else:
                keep.append(st)
        self.semtiles = keep

    def emit(self):
        nc = self.nc
        for e in ENGS:
            if e != "sp":
                self.engsem[e] = self.new_sem("eng_" + e)
        for e in ENGS:
            c = 0
            for op in self.ops[e]:
                if op.signal:
                    c += 1
                    op.tick = c
        prog = self

        def run(e, engobj):
            waited = {}
            for op in prog.ops[e]:
                for d in op.deps:
                    if isinstance(d, tuple):
                        sem, val = d
                    else:
                        sem, val = prog.engsem[d.eng], d.tick
                    key = id(sem)
                    if waited.get(key, 0) >= val:
                        continue
                    waited[key] = val
                    engobj.wait_ge(sem, val)
                ins = op.fn(engobj)
                if op.dma:
                    ins.then_inc(op.sem, 16)
                elif op.signal:
                    ins.then_inc(prog.engsem[e], 1)
            done = {}
            for op in prog.ops[e]:
                if op.dma:
                    done[id(op.sem)] = (op.sem, max(op.semval, done.get(id(op.sem), (None, 0))[1]))
            for sem, val in done.values():
                engobj.wait_ge(sem, val)

        with nc.Block() as block:
            @block.sync
            def _(eng):
                run("sp", eng)

            @block.scalar
            def _(eng):
                run("act", eng)

            @block.vector
            def _(eng):
                run("dve", eng)

            @block.gpsimd
            def _(eng):
                run("pool", eng)

            @block.tensor
            def _(eng):
                run("pe", eng)

    def close(self):
        while self.stack:
            self.stack.pop().__exit__(None, None, None)


class Ring:
    def __init__(self, tiles):
        self.tiles = tiles
        self.i = 0

    def next(self):
        t = self.tiles[self.i % len(self.tiles)]
        self.i += 1
        return t


class Cfg:
    def __init__(self, p_len=4096, past=2048, ns=2, sl=64):
        self.P_LEN = p_len
        self.PAST = past
        self.NS = ns
        self.SL = sl
        self.NTOK = p_len + ns * sl
        self.KSEQ = past + sl
        self.NKEY = p_len + ns * self.KSEQ
        self.macros = []
        for m in range(p_len // 512):
            self.macros.append((m * 512, 512, [(0, 512, 0)]))
        assert ns * sl == 128
        self.macros.append((p_len, 128, [(i * sl, sl, 1 + i) for i in range(ns)]))
        self.NSEQ = 1 + ns

    def koff(self, i):
        return self.P_LEN + i * self.KSEQ


class Builder:
    def __init__(self, nc, cfg, debug=False, stop_after=None):
        self.nc = nc
        self.cfg = cfg
        self.debug = debug
        self.stop_after = stop_after
        self.P = Prog(nc)
        self.uid = 0

    def add(self, *a, **k):
        return self.P.add(*a, **k)

    def nm(self, s):
        self.uid += 1
        return "%s_%d" % (s, self.uid)

    def alloc(self, name, shape, dtype):
        n = int(np.prod(shape[1:]))
        nb = n * (2 if dtype == BF16 else 4)
        nw = (nb + 3) // 4
        assert self.aoff + nw <= self.arena_words, ("arena overflow", name, self.aoff, nw)
        ap = self.arena.t[0:shape[0], self.aoff:self.aoff + nw]
        self.aoff += nw
        if dtype != F32:
            ap = ap.bitcast(dtype)
        if len(shape) == 3:
            ap = ap.rearrange("p (a b) -> p a b", b=shape[2])
        elif len(shape) == 4:
            ap = ap.rearrange("p (a b c) -> p a b c", b=shape[2], c=shape[3])
        return T(self.nm(name), ap, transient=True)

    def phase(self):
        self.P.barrier()
        self.aoff = 0
        self.psr = Ring(self.psum)

    def dma(self, eng, out, in_, reads=(), writes=(), pwrites=(), semtile=None, slow=False):
        if slow:
            fn = lambda e, o=out, i=in_: e.dma_start(out=o, in_=i, allow_slow_non_contiguous=True)
        else:
            fn = lambda e, o=out, i=in_: e.dma_start(out=o, in_=i)
        return self.add(eng, fn, reads=reads, writes=writes, pwrites=pwrites, dma=True, semtile=semtile)

    def load(self, dst_t, dst_ap, src_t, src_ap, slow=False, part=False):
        return self.dma("sp", dst_ap, src_ap, reads=[src_t], writes=([] if part else [dst_t]),
                        pwrites=([dst_t] if part else []), semtile=dst_t, slow=slow)

    def store(self, dst_t, dst_ap, src_t, src_ap, slow=False, eng="pool"):
        return self.dma(eng, dst_ap, src_ap, reads=[src_t], pwrites=[dst_t], semtile=src_t, slow=slow)

    def mm(self, ps_t, ps_ap, lhsT, rhs, start, stop, reads, excl=None):
        fn = lambda e, o=ps_ap, l=lhsT, r=rhs, s=start, p=stop: e.matmul(o, lhsT=l, rhs=r, start=s, stop=p)
        if excl is None:
            excl = start
        if excl:
            return self.add("pe", fn, reads=reads, writes=[ps_t])
        return self.add("pe", fn, reads=reads, pwrites=[ps_t])

    def tr(self, ps_t, ps_ap, in_ap, ident_ap, reads, first=True):
        fn = lambda e, o=ps_ap, i=in_ap, d=ident_ap: e.transpose(o, i, d)
        if first:
            return self.add("pe", fn, reads=reads, writes=[ps_t])
        return self.add("pe", fn, reads=reads, pwrites=[ps_t])

    def act(self, out_ap, in_ap, func, reads, writes=(), pwrites=(), bias=None, scale=None, accum=None):
        kw = {}
        if bias is not None:
            kw["bias"] = bias
        if scale is not None:
            kw["scale"] = scale
        if accum is not None:
            kw["accum_out"] = accum
        return self.add("act", lambda e, o=out_ap, i=in_ap, f=func, kw=kw: e.activation(out=o, in_=i, func=f, **kw),
                        reads=reads, writes=writes, pwrites=pwrites)

    def ts(self, eng, out_ap, in_ap, s1, s2, op0, op1, reads, writes=(), pwrites=()):
        if op1 is None:
            fn = lambda e, o=out_ap, i=in_ap, a=s1, p0=op0: e.tensor_scalar(o, i, a, None, p0)
        else:
            fn = lambda e, o=out_ap, i=in_ap, a=s1, b=s2, p0=op0, p1=op1: e.tensor_scalar(o, i, a, b, p0, p1)
        return self.add(eng, fn, reads=reads, writes=writes, pwrites=pwrites)

    def tt(self, eng, out_ap, a_ap, b_ap, op, reads, writes=(), pwrites=()):
        return self.add(eng, lambda e, o=out_ap, a=a_ap, b=b_ap, p=op: e.tensor_tensor(out=o, in0=a, in1=b, op=p),
                        reads=reads, writes=writes, pwrites=pwrites)

    def stt(self, eng, out_ap, in0, scalar, in1, op0, op1, reads, writes=(), pwrites=()):
        return self.add(eng, lambda e, o=out_ap, a=in0, s=scalar, b=in1, p0=op0, p1=op1:
                        e.scalar_tensor_tensor(out=o, in0=a, scalar=s, in1=b, op0=p0, op1=p1),
                        reads=reads, writes=writes, pwrites=pwrites)

    def cp(self, eng, out_ap, in_ap, reads, writes=(), pwrites=()):
        if eng == "act":
            return self.add("act", lambda e, o=out_ap, i=in_ap: e.copy(out=o, in_=i), reads=reads, writes=writes, pwrites=pwrites)
        return self.add(eng, lambda e, o=out_ap, i=in_ap: e.tensor_copy(out=o, in_=i), reads=reads, writes=writes, pwrites=pwrites)

    def memset(self, eng, t, ap, val, part=False):
        return self.add(eng, lambda e, o=ap, v=val: e.memset(o, v), writes=([] if part else [t]), pwrites=([t] if part else []))

    def declare(self):
        P, c = self.P, self.cfg
        IN, OUT = "ExternalInput", "ExternalOutput"
        SCR = OUT if self.debug else "Internal"
        d = {}
        d["x_all"] = P.dram("x_all", [c.NTOK, D], F32, IN)
        d["s5re0"] = P.dram("s5re0", [c.NS, 64, 64], F32, IN)
        d["s5im0"] = P.dram("s5im0", [c.NS, 64, 64], F32, IN)
        d["ck"] = P.dram("ck", [c.NS, c.PAST, 1024], F32, IN)
        d["cv"] = P.dram("cv", [c.NS, c.PAST, 1024], F32, IN)
        d["cckv"] = P.dram("cckv", [c.NS, c.PAST, 512], F32, IN)
        d["ckpe"] = P.dram("ckpe", [c.NS, c.PAST, 64], F32, IN)
        d["convst"] = P.dram("convst", [2, c.NS, 2, DFF], F32, IN)
        wshapes = {
            "norm_mix": [2, D], "norm_ffn": [2, D], "norm_final": [D],
            "w_in_even": [D, 4096], "w_out_even": [D, D],
            "s5_a_re": [64, 64], "s5_a_im": [64, 64], "s5_b_re": [64, 64, 16], "s5_b_im": [64, 64, 16],
            "s5_c_re": [64, 16, 64], "s5_c_im": [64, 16, 64], "s5_d": [1024], "s5_log_dt": [64],
            "s5_w_glu": [1024, 1024], "s5_b_glu": [1024],
            "diff_lambda_q1": [64], "diff_lambda_k1": [64], "diff_lambda_q2": [64], "diff_lambda_k2": [64],
            "diff_subln": [128],
            "w_in_odd": [D, 1088], "mla_q_norm": [512], "mla_kv_norm": [512],
            "mla_w_uq": [512, 3072], "mla_w_ukv": [512, 4096], "w_out_odd": [D, D],
            "ffn_w_in": [2, D, 2 * DFF], "ffn_conv_w": [2, 3, DFF], "ffn_conv_b": [2, DFF],
            "ffn_w_down": [2, DFF, D],
        }
        self.wshapes = wshapes
        for k, s in wshapes.items():
            d[k] = P.dram(k, s, F32, IN)
        d["y_all"] = P.dram("y_all", [c.NTOK, D], F32, OUT)
        d["s5re_o"] = P.dram("s5re_o", [c.NSEQ, 64, 64], F32, OUT)
        d["s5im_o"] = P.dram("s5im_o", [c.NSEQ, 64, 64], F32, OUT)
        d["dk_o"] = P.dram("dk_o", [c.NTOK, 1024], F32, OUT)
        d["dv_o"] = P.dram("dv_o", [c.NTOK, 1024], F32, OUT)
        d["ckv_o"] = P.dram("ckv_o", [c.NTOK, 512], F32, OUT)
        d["kpe_o"] = P.dram("kpe_o", [c.NTOK, 64], F32, OUT)
        d["conv_o"] = P.dram("conv_o", [2, c.NSEQ, 2, DFF], F32, OUT)
        for k in ("w_in_even", "w_out_even", "s5_w_glu", "w_in_odd", "mla_w_uq", "mla_w_ukv", "w_out_odd"):
            d[k + "_b"] = P.dram(k + "_b", wshapes[k], BF16, "Internal")
        for l in range(2):
            d["ffn_w_in_b%d" % l] = P.dram("ffn_w_in_b%d" % l, [D, 2 * DFF], BF16, "Internal")
            d["ffn_w_down_b%d" % l] = P.dram("ffn_w_down_b%d" % l, [DFF, D], BF16, "Internal")
        d["uT_s"] = P.dram("uT_s", [1024, c.NTOK], BF16, SCR)
        d["qT_s"] = P.dram("qT_s", [1024, c.NTOK], BF16, SCR)
        d["kT_s"] = P.dram("kT_s", [1024, c.NKEY], BF16, SCR)
        d["v_s"] = P.dram("v_s", [c.NKEY, 1024], BF16, SCR)
        d["mixT_s"] = P.dram("mixT_s", [D, c.NTOK], BF16, SCR)
        d["x_s"] = P.dram("x_s", [c.NTOK, D], F32, SCR)
        d["qnT_s"] = P.dram("qnT_s", [D, c.NTOK], BF16, SCR)
        d["qpeT_s"] = P.dram("qpeT_s", [1024, c.NTOK], BF16, SCR)
        d["knT_s"] = P.dram("knT_s", [D, c.NKEY], BF16, SCR)
        d["kpeT_s"] = P.dram("kpeT_s", [64, c.NKEY], BF16, SCR)
        d["v1_s"] = P.dram("v1_s", [c.NKEY, D], BF16, SCR)
        d["oT_s"] = P.dram("oT_s", [D, c.NTOK], BF16, SCR)
        self.d = d

    def setup_consts(self):
        P, c, d = self.P, self.cfg, self.d
        self.psum = [P.ps("ps%d" % i, [128, 512], F32) for i in range(8)]
        self.psr = Ring(self.psum)
        sb = P.sb
        self.identf = sb("identf", [128, 128], F32)
        self.identb = sb("identb", [128, 128], BF16)
        self.onesf = sb("onesf", [128, 128], F32)
        self.onesb = sb("onesb", [128, 128], BF16)
        self.eps_t = sb("eps_t", [128, 1], F32)
        self.mask = sb("maskd", [128, 512], BF16)
        self.memset("pool", self.identf, self.identf[:], 1.0)
        self.add("pool", lambda e: e.affine_select(out=self.identf[:], in_=self.identf[:], pattern=[[1, 128]],
                                                   compare_op=ALU.is_equal, fill=0.0, base=0, channel_multiplier=-1),
                 reads=[self.identf], writes=[self.identf])
        self.cp("dve", self.identb[:], self.identf[:], [self.identf], [self.identb])
        self.memset("dve", self.onesf, self.onesf[:], 1.0)
        self.memset("dve", self.onesb, self.onesb[:], 1.0)
        self.memset("dve", self.eps_t, self.eps_t[:], EPS)
        self.memset("dve", self.mask, self.mask[:], 1.0)
        self.memset("dve", self.mask, self.mask[64:128, 0:64], 0.0, part=True)

    def load_featmajor(self, dst_t, dst_ap3, src_t, src_rows_ap, R, F, stage_t):
        nch = F // 128
        self.load(stage_t, stage_t[0:R, 0:F], src_t, src_rows_ap)
        per = 512 // R
        c0 = 0
        while c0 < nch:
            n = min(per, nch - c0)
            ps = self.psr.next()
            for i in range(n):
                cc = c0 + i
                self.tr(ps, ps[:, i * R:(i + 1) * R], stage_t[0:R, cc * 128:(cc + 1) * 128], self.identf[0:R, 0:R],
                        [stage_t, self.identf], first=(i == 0))
            self.cp("dve", dst_ap3[:, c0:c0 + n, :], ps[:, 0:n * R].rearrange("p (c r) -> p c r", r=R), [ps], pwrites=[dst_t])
            c0 += n

    def setup_params(self):
        P, c, d = self.P, self.cfg, self.d
        sb = P.sb
        nsub = c.NTOK // 128
        self.nsub = nsub
        self.gmix = sb("gmix", [128, 2, 16, 1], F32)
        self.gffn = sb("gffn", [128, 2, 16, 1], F32)
        self.gfin = sb("gfin", [128, D], F32)
        self.s5d = sb("s5d", [128, 8, 1], F32)
        self.bglu = sb("bglu", [128, 8, 1], F32)
        self.qng = sb("qng", [128, 4, 1], F32)
        self.kvng = sb("kvng", [128, 4, 1], F32)
        self.kvgb = sb("kvgb", [128, 512], F32)
        self.subg = sb("subg", [128, 1, 1], F32)
        self.convw = sb("convw", [128, 2, NJ, 3], F32)
        self.convb = sb("convb", [128, 2, NJ, 1], F32)
        self.halo = sb("halo", [128, 2, c.NSEQ, NJ, 2], F32)
        self.neglam = sb("neglam", [128, 1], F32)
        self.ropes = sb("ropesin", [128, nsub, 32], F32)
        self.ropec = sb("ropecos", [128, nsub, 32], F32)
        self.arena_words = (nc_free_bytes(self.nc) - 1024) // 4
        self.arena = sb("arena", [128, self.arena_words], F32)
        self.aoff = 0
        self.phase()
        A = self.alloc
        stage = A("pstage", [4, DFF], F32)
        for l in range(2):
            self.load_featmajor(self.gmix, self.gmix[:, l], d["norm_mix"], d["norm_mix"].t[l:l + 1, :], 1, D, stage)
            self.load_featmajor(self.gffn, self.gffn[:, l], d["norm_ffn"], d["norm_ffn"].t[l:l + 1, :], 1, D, stage)
        self.load(self.gfin, self.gfin[:], d["norm_final"], d["norm_final"].t.partition_broadcast(128))
        self.load_featmajor(self.s5d, self.s5d[:], d["s5_d"], d["s5_d"].t.unsqueeze(0), 1, 1024, stage)
        self.load_featmajor(self.bglu, self.bglu[:], d["s5_b_glu"], d["s5_b_glu"].t.unsqueeze(0), 1, 1024, stage)
        self.load_featmajor(self.qng, self.qng[:], d["mla_q_norm"], d["mla_q_norm"].t.unsqueeze(0), 1, 512, stage)
        self.load_featmajor(self.kvng, self.kvng[:], d["mla_kv_norm"], d["mla_kv_norm"].t.unsqueeze(0), 1, 512, stage)
        self.load(self.kvgb, self.kvgb[:], d["mla_kv_norm"], d["mla_kv_norm"].t.partition_broadcast(128))
        self.load_featmajor(self.subg, self.subg[:], d["diff_subln"], d["diff_subln"].t.unsqueeze(0), 1, 128, stage)
        self.memset("pool", self.halo, self.halo[:], 0.0)
        for l in range(2):
            self.load_featmajor(self.convw, self.convw[:, l], d["ffn_conv_w"], d["ffn_conv_w"].t[l], 3, DFF, stage)
            self.load_featmajor(self.convb, self.convb[:, l], d["ffn_conv_b"], d["ffn_conv_b"].t[l:l + 1, :], 1, DFF, stage)
            for i in range(c.NS):
                self.load_featmajor(self.halo, self.halo[:, l, 1 + i], d["convst"], d["convst"].t[l, i], 2, DFF, stage)
        lam_init = 0.8 - 0.6 * math.exp(-0.3 * 0)
        lq = A("lamtmp", [128, 4, 64], F32)
        for i, k in enumerate(("diff_lambda_q1", "diff_lambda_k1", "diff_lambda_q2", "diff_lambda_k2")):
            self.load(lq, lq[:, i, :], d[k], d[k].t.partition_broadcast(128), part=True)
        lsum = A("lamsum", [128, 2], F32)
        lprod = A("lamprod", [128, 2, 64], F32)
        self.tt("dve", lprod[:, 0, :], lq[:, 0, :], lq[:, 1, :], ALU.mult, [lq], pwrites=[lprod])
        self.tt("dve", lprod[:, 1, :], lq[:, 2, :], lq[:, 3, :], ALU.mult, [lq], pwrites=[lprod])
        self.add("dve", lambda e: e.reduce_sum(out=lsum[:], in_=lprod[:], axis=AX.X), reads=[lprod], writes=[lsum])
        self.act(lsum[:], lsum[:], AF.Exp, [lsum], [lsum])
        self.tt("dve", self.neglam[:], lsum[:, 1:2], lsum[:, 0:1], ALU.subtract, [lsum], [self.neglam])
        self.ts("dve", self.neglam[:], self.neglam[:], -lam_init, None, ALU.add, None, [self.neglam], [self.neglam])
        self.ts("dve", self.subg[:, 0, :], self.subg[:, 0, :], 1.0 - lam_init, None, ALU.mult, None, [self.subg], [self.subg])
        pos = A("ropepos", [128, nsub], F32)
        npr = c.P_LEN // 128
        self.add("pool", lambda e: e.iota(pos[:, 0:npr], pattern=[[128, npr]], base=0, channel_multiplier=1,
                                          allow_small_or_imprecise_dtypes=True), writes=[pos])
        self.add("pool", lambda e: e.iota(pos[0:64, npr:npr + 1], pattern=[[0, 1]], base=c.PAST, channel_multiplier=1,
                                          allow_small_or_imprecise_dtypes=True), pwrites=[pos])
        self.add("pool", lambda e: e.iota(pos[64:128, npr:npr + 1], pattern=[[0, 1]], base=c.PAST, channel_multiplier=1,
                                          allow_small_or_imprecise_dtypes=True), pwrites=[pos])
        invf = A("ropeinv", [128, 32], F32)
        self.add("pool", lambda e: e.iota(invf[:], pattern=[[1, 32]], base=0, channel_multiplier=0,
                                          allow_small_or_imprecise_dtypes=True), writes=[invf])
        self.act(invf[:], invf[:], AF.Exp, [invf], [invf], scale=-math.log(10000.0) / 32.0)
        ang = A("ropeang", [128, nsub, 32], F32)
        self.tt("dve", ang[:], pos[:, :].unsqueeze(2).to_broadcast([128, nsub, 32]),
                invf[:, :].unsqueeze(1).to_broadcast([128, nsub, 32]), ALU.mult, [pos, invf], [ang])
        t1 = A("ropet1", [128, nsub * 32], F32)
        t2 = A("ropet2", [128, nsub * 32], I32)
        angf = ang[:, :, :].rearrange("p a b -> p (a b)")
        self.sin_of(self.ropes, self.ropes[:, :, :].rearrange("p a b -> p (a b)"), ang, angf, 0.0, t1, t2)
        self.sin_of(self.ropec, self.ropec[:, :, :].rearrange("p a b -> p (a b)"), ang, angf, math.pi / 2, t1, t2)

    def sin_of(self, out_t, out_ap, arg_t, arg_ap, shift, t1, t2):
        n = out_ap.shape[-1]
        a1, a2 = t1[:, 0:n], t2[:, 0:n]
        self.ts("dve", a1, arg_ap, 1.0 / TWO_PI, shift / TWO_PI, ALU.mult, ALU.add, [arg_t], [t1])
        self.cp("dve", a2, a1, [t1], [t2])
        self.cp("dve", a1, a2, [t2], [t1])
        self.stt("dve", a1, a1, -TWO_PI, arg_ap, ALU.mult, ALU.add, [t1, arg_t], [t1])
        self.ts("dve", a1, a1, shift, None, ALU.add, None, [t1], [t1])
        self.ts("dve", a1, a1, -math.pi, math.pi, ALU.max, ALU.min, [t1], [t1])
        self.act(out_ap, a1, AF.Sin, [t1], [out_t])

    def conv_jobs(self, keys, CB):
        d = self.d
        jobs = []
        specs = []
        for k in keys:
            if k.startswith("ffn_w_in"):
                l = int(k[-1]); specs.append((d["ffn_w_in"], d["ffn_w_in"].t[l], d["ffn_w_in_b%d" % l], [D, 2 * DFF]))
            elif k.startswith("ffn_w_down"):
                l = int(k[-1]); specs.append((d["ffn_w_down"], d["ffn_w_down"].t[l], d["ffn_w_down_b%d" % l], [DFF, D]))
            else:
                specs.append((d[k], d[k].t, d[k + "_b"], self.wshapes[k]))
        for src_t, src_ap, dst_t, (K, N) in specs:
            for r in range(K // 128):
                for c0 in range(0, N, CB):
                    jobs.append((src_t, src_ap, dst_t, r, c0, min(CB, N - c0)))
        return jobs

    def conv_emit(self, job, s, b, cast_eng, store_eng):
        src_t, src_ap, dst_t, r, c0, n = job
        self.load(s, s[:, 0:n], src_t, src_ap[r * 128:(r + 1) * 128, c0:c0 + n])
        self.cp(cast_eng, b[:, 0:n], s[:, 0:n], [s], [b])
        self.store(dst_t, dst_t.t[r * 128:(r + 1) * 128, c0:c0 + n], b, b[:, 0:n], eng=store_eng)

    def prologue(self):
        self.phase()
        CB = 2048
        st = Ring([self.alloc("wst", [128, CB], F32) for _ in range(4)])
        bt = Ring([self.alloc("wbt", [128, CB], BF16) for _ in range(4)])
        for i, job in enumerate(self.conv_jobs(["w_in_even", "s5_w_glu"], CB)):
            self.conv_emit(job, st.next(), bt.next(), ("dve", "act")[i % 2], "pool")
        self.bg_jobs = self.conv_jobs(["w_out_even", "ffn_w_in0", "ffn_w_down0", "w_in_odd", "mla_w_uq", "mla_w_ukv", "w_out_odd",
                                       "ffn_w_in1", "ffn_w_down1"], 256)

    def slab(self, wt, k0, kc, n0, n):
        s = self.slabs.next()
        v = s.t[:, 0:kc * n].rearrange("p (k n) -> p k n", n=n)
        self.load(s, v, wt, wt.t[k0 * 128:(k0 + kc) * 128, n0:n0 + n].rearrange("(k p) n -> p k n", p=128))
        return s, v

    def norm_T(self, xts, Dw, gain_ap3, hT, evac_scale=None):
        nch = Dw // 128
        S = len(xts)
        rstds, xns = [], []
        for s, xt in enumerate(xts):
            ss = self.ssr.next()
            xn = self.xnr.next()
            self.act(xn[:, 0:Dw], xt[:, 0:Dw], AF.Square, [xt], [xn, ss], accum=ss[:])
            self.act(ss[:], ss[:], AF.Sqrt, [ss, self.eps_t], [ss], bias=self.eps_t[:], scale=1.0 / Dw)
            self.add("dve", lambda e, o=ss[:]: e.reciprocal(o, o), reads=[ss], writes=[ss])
            if s % 2 == 0:
                self.act(xn[:, 0:Dw], xt[:, 0:Dw], AF.Copy, [xt, ss], [xn], scale=ss[:])
            else:
                self.ts("dve", xn[:, 0:Dw], xt[:, 0:Dw], ss[:], None, ALU.mult, None, [xt, ss], [xn])
            rstds.append(ss)
            xns.append(xn)
        for c in range(nch):
            ps = self.psr.next()
            pb = ps.t[:, :].bitcast(BF16)
            for s in range(S):
                self.tr(ps, pb[:, s * 128:(s + 1) * 128], xns[s][:, c * 128:(c + 1) * 128], self.identb[:],
                        [xns[s], self.identb], first=(s == 0))
            if gain_ap3 is None:
                self.cp("dve" if c % 2 else "act", hT[:, c, 0:S * 128], pb[:, 0:S * 128], [ps], pwrites=[hT])
            elif c % 2 == 0:
                self.ts("dve", hT[:, c, 0:S * 128], pb[:, 0:S * 128], gain_ap3[:, c, :], None, ALU.mult, None,
                        [ps], pwrites=[hT])
            else:
                self.act(hT[:, c, 0:S * 128], pb[:, 0:S * 128], AF.Copy, [ps], pwrites=[hT], scale=gain_ap3[:, c, :])
        return rstds

    def formA(self, sl_t, sl_v, oc, kc, hT, nt):
        ps = self.psr.next()
        for k in range(kc):
            self.mm(ps, ps[:, 0:nt], sl_v[:, k, oc * 128:(oc + 1) * 128], hT[:, k, 0:nt], k == 0, k == kc - 1, [sl_t, hT])
        return ps

    def formB(self, sl_t, rhs_fn, kc, hT, s, n):
        ps = self.psr.next()
        for k in range(kc):
            self.mm(ps, ps[:, 0:n], hT[:, k, s * 128:(s + 1) * 128], rhs_fn(k), k == 0, k == kc - 1, [sl_t, hT])
        return ps

    def l0_a(self):
        c, d = self.cfg, self.d
        self.phase()
        self.slabs = Ring([self.alloc("slab", [128, 16 * 512], BF16) for _ in range(3)])
        xr = Ring([self.alloc("x", [128, D], F32) for _ in range(5)])
        self.xnr = Ring([self.alloc("xn", [128, D], BF16) for _ in range(4)])
        self.ssr = Ring([self.alloc("ss", [128, 1], F32) for _ in range(8)])
        hTr = Ring([self.alloc("hT", [128, 16, 512], BF16) for _ in range(2)])
        stf = Ring([self.alloc("stf", [128, 4, 512], F32) for _ in range(2)])
        stb = Ring([self.alloc("stb", [128, 4, 512], BF16) for _ in range(3)])
        for (tok0, nt, segs) in c.macros:
            S = nt // 128
            xts = []
            for s in range(S):
                xt = xr.next()
                self.load(xt, xt[:], d["x_all"], d["x_all"].t[tok0 + s * 128: tok0 + (s + 1) * 128, :])
                xts.append(xt)
            hT = hTr.next()
            self.norm_T(xts, D, self.gmix[:, 0], hT)
            koffs = self.key_cols(tok0, nt, segs)
            for si in range(8):
                sl, sv = self.slab(d["w_in_even_b"], 0, 16, si * 512, 512)
                if si < 6:
                    so = stb.next()
                    for oc in range(4):
                        ps = self.formA(sl, sv, oc, 16, hT, nt)
                        if si < 2:
                            self.cp("act" if oc % 2 else "dve", so[:, oc, 0:nt], ps[:, 0:nt], [ps], pwrites=[so])
                        elif si < 4:
                            self.act(so[:, oc, 0:nt], ps[:, 0:nt], AF.Copy, [ps], pwrites=[so], scale=0.125)
                        else:
                            self.cp("act" if oc % 2 else "dve", so[:, oc, 0:nt], ps[:, 0:nt], [ps], pwrites=[so])
                    if si < 2:
                        self.store(d["uT_s"], d["uT_s"].t[si * 512:(si + 1) * 512, tok0:tok0 + nt].rearrange("(c p) t -> p c t", p=128),
                                   so, so[:, :, 0:nt])
                    elif si < 4:
                        r0 = (si - 2) * 512
                        self.store(d["qT_s"], d["qT_s"].t[r0:r0 + 512, tok0:tok0 + nt].rearrange("(c p) t -> p c t", p=128),
                                   so, so[:, :, 0:nt])
                    else:
                        r0 = (si - 4) * 512
                        for (col0, ln, k0) in koffs:
                            self.store(d["kT_s"], d["kT_s"].t[r0:r0 + 512, k0:k0 + ln].rearrange("(c p) t -> p c t", p=128),
                                       so, so[:, :, col0:col0 + ln])
                if si >= 4:
                    so = stf.next()
                    for s in range(S):
                        ps = self.formB(sl, lambda k, sv=sv: sv[:, k, :], 16, hT, s, 512)
                        self.cp("act" if s % 2 else "dve", so[:, s, :], ps[:, :], [ps], pwrites=[so])
                    dst = d["dk_o"] if si < 6 else d["dv_o"]
                    cb = (si - 4) % 2 * 512
                    self.store(dst, dst.t[tok0:tok0 + nt, cb:cb + 512].rearrange("(s p) n -> p s n", p=128), so, so[:, 0:S, :])
                    if si >= 6:
                        sb2 = stb.next()
                        self.cp("pool", sb2[:, 0:S, :], so[:, 0:S, :], [so], pwrites=[sb2])
                        for (col0, ln, k0) in koffs:
                            for s in range(S):
                                a, b = max(col0, s * 128), min(col0 + ln, (s + 1) * 128)
                                if a >= b:
                                    continue
                                self.store(d["v_s"], d["v_s"].t[k0 + a - col0:k0 + b - col0, cb:cb + 512],
                                           sb2, sb2[a - s * 128:b - s * 128, s, :])

    def key_cols(self, tok0, nt, segs):
        c = self.cfg
        out = []
        for (col0, ln, seq) in segs:
            if seq == 0:
                out.append((col0, ln, tok0 + col0))
            else:
                out.append((col0, ln, c.koff(seq - 1) + c.PAST))
        return out

    def l0_past(self):
        c, d = self.cfg, self.d
        self.phase()
        fr = Ring([self.alloc("pf", [128, 1024], F32) for _ in range(3)])
        br = Ring([self.alloc("pb", [128, 1024], BF16) for _ in range(3)])
        ktr = Ring([self.alloc("pkt", [128, 8, 512], BF16) for _ in range(2)])
        for i in range(c.NS):
            k0 = c.koff(i)
            for t0 in range(0, c.PAST, 512):
                nsub = min(4, (c.PAST - t0) // 128)
                kt = ktr.next()
                for s in range(nsub):
                    r0 = t0 + s * 128
                    f, b = fr.next(), br.next()
                    self.load(f, f[:], d["ck"], d["ck"].t[i, r0:r0 + 128, :])
                    self.cp("dve", b[:], f[:], [f], [b])
                    for hp in range(2):
                        ps = self.psr.next()
                        pb = ps.t[:, :].bitcast(BF16)
                        for hh in range(4):
                            h = hp * 4 + hh
                            self.tr(ps, pb[:, hh * 128:(hh + 1) * 128], b[:, h * 128:(h + 1) * 128], self.identb[:],
                                    [b, self.identb], first=(hh == 0))
                        self.cp("act", kt[:, hp * 4:(hp + 1) * 4, s * 128:(s + 1) * 128],
                                pb[:, 0:512].rearrange("p (h t) -> p h t", t=128), [ps], pwrites=[kt])
                    f2, b2 = fr.next(), br.next()
                    self.load(f2, f2[:], d["cv"], d["cv"].t[i, r0:r0 + 128, :])
                    self.cp("pool", b2[:], f2[:], [f2], [b2])
                    self.store(d["v_s"], d["v_s"].t[k0 + r0:k0 + r0 + 128, :], b2, b2[:])
                n = nsub * 128
                self.store(d["kT_s"], d["kT_s"].t[:, k0 + t0:k0 + t0 + n].rearrange("(h p) t -> p h t", p=128), kt, kt[:, :, 0:n])

    def attention(self, maps, vfn, nq, ktiles, fin, den_modes):
        M = len(maps)
        O = [self.psr.next() for _ in range(M)]
        Dpe = {m: self.psr.next() for m in range(M) if den_modes[m] == "pe"}
        held = O + list(Dpe.values())
        sring = Ring([b for b in self.psum if all(b is not a for a in held)])
        accs = [self.accr.next() for _ in range(M)]
        S = {}
        nt_ = len(ktiles)

        def emit_S(ti):
            kt, nk, q0, diag = ktiles[ti]
            n = nq - q0
            for m in range(M):
                ps = sring.next()
                ops = maps[m]
                for oi, (lf, rf, rd) in enumerate(ops):
                    self.mm(ps, ps[0:nk, q0:nq], lf(kt, nk), rf(q0, n), oi == 0, oi == len(ops) - 1, rd)
                S[(ti, m)] = ps

        LA = 2 if M == 1 else 1
        self.cur_sring = sring
        for t0_ in range(min(LA, nt_)):
            emit_S(t0_)
        if self.deferred is not None:
            fn_, self.deferred = self.deferred, None
            fn_()
        for ti, (kt, nk, q0, diag) in enumerate(ktiles):
            if ti + LA < nt_:
                emit_S(ti + LA)
            n = nq - q0
            for m in range(M):
                ps = S.pop((ti, m))
                e = self.er.next()
                self.act(e[0:nk, q0:nq], ps[0:nk, q0:nq], AF.Exp, [ps], [e])
                if diag:
                    self.tt("dve", e[0:nk, q0:nq], e[0:nk, q0:nq], self.mask[0:nk, 0:n], ALU.mult, [e, self.mask], [e])
                va, vr = vfn(kt, nk)
                self.mm(O[m], O[m][:, q0:nq], va, e[0:nk, q0:nq], ti == 0, ti == nt_ - 1, [e] + vr)
                eng = "dve"
                acc = accs[m]
                if den_modes[m] == "pe":
                    self.mm(Dpe[m], Dpe[m][:, q0:nq], self.onesb[0:nk, :], e[0:nk, q0:nq], ti == 0, ti == nt_ - 1, [e, self.onesb])
                elif ti == 0:
                    self.cp(eng, acc[:, 0:nq], e[:, 0:nq], [e], [acc])
                else:
                    self.tt(eng, acc[0:nk, q0:nq], acc[0:nk, q0:nq], e[0:nk, q0:nq], ALU.add, [acc, e], [acc])
        Dn = []
        for m in range(M):
            if den_modes[m] == "pe":
                Dn.append(Dpe[m])
                continue
            ps = sring.next()
            self.mm(ps, ps[:, 0:nq], self.onesf[:, :], accs[m][:, 0:nq], True, True, [accs[m], self.onesf])
            Dn.append(ps)
        self.deferred = fin(O, Dn)

    def l0_attn(self):
        c, d = self.cfg, self.d
        self.phase()
        seqs = [(0, c.P_LEN, 0, c.P_LEN)]
        for i in range(c.NS):
            seqs.append((c.koff(i), c.KSEQ, c.P_LEN + i * c.SL, c.SL))
        maxk = max(c.P_LEN, c.KSEQ)
        nkt_max = (maxk + 127) // 128
        KTr = Ring([self.alloc("KT", [128, maxk], BF16) for _ in range(2)])
        QTr = Ring([self.alloc("QT", [128, max(c.P_LEN, c.SL)], BF16) for _ in range(2)])
        Vr = Ring([self.alloc("V", [128, nkt_max, 128], BF16) for _ in range(2)])
        self.er = Ring([self.alloc("E", [128, 512], BF16) for _ in range(6)])
        self.accr = Ring([self.alloc("dacc", [128, 512], F32) for _ in range(4)])
        tmp = Ring([self.alloc("atmp", [128, 512], F32) for _ in range(12)])
        ost = Ring([self.alloc("aost", [128, 512], BF16) for _ in range(3)])
        self.deferred = None
        for (key0, nkeys, tok0, ntok) in seqs:
            nfull, rem = nkeys // 128, nkeys % 128
            for h in range(8):
                KT, QT, V = KTr.next(), QTr.next(), Vr.next()
                self.load(KT, KT[:, 0:nkeys], d["kT_s"], d["kT_s"].t[h * 128:(h + 1) * 128, key0:key0 + nkeys])
                self.load(QT, QT[:, 0:ntok], d["qT_s"], d["qT_s"].t[h * 128:(h + 1) * 128, tok0:tok0 + ntok])
                if nfull:
                    self.load(V, V[:, 0:nfull, :], d["v_s"],
                              d["v_s"].t[key0:key0 + nfull * 128, h * 128:(h + 1) * 128].rearrange("(t p) d -> p t d", p=128), part=True)
                if rem:
                    self.load(V, V[0:rem, nfull, :], d["v_s"], d["v_s"].t[key0 + nfull * 128:key0 + nkeys, h * 128:(h + 1) * 128], part=True)
                causal = (ntok == nkeys)
                for qm in range((ntok + 511) // 512):
                    nq = min(512, ntok - qm * 512)
                    qc0 = qm * 512
                    if causal:
                        ktiles = [(kt, 128, 0, False) for kt in range(4 * qm)] + \
                                 [(4 * qm + j, 128, 128 * j, True) for j in range(nq // 128)]
                    else:
                        ktiles = [(kt, 128, 0, False) for kt in range(nfull)] + ([(nfull, rem, 0, False)] if rem else [])
                    maps = []
                    for m in range(2):
                        pr = slice(m * 64, (m + 1) * 64)
                        maps.append([(lambda kt, nk, KT=KT, pr=pr: KT[pr, kt * 128:kt * 128 + nk],
                                      lambda q0, n, QT=QT, pr=pr, qc0=qc0: QT[pr, qc0 + q0:qc0 + q0 + n], [KT, QT])])
                    vfn = lambda kt, nk, V=V: (V[0:nk, kt, :], [V])

                    def fin(O, Dn, h=h, tok0=tok0, qc0=qc0, nq=nq):
                        r = [tmp.next(), tmp.next()]
                        for m in range(2):
                            self.act(r[m][:, 0:nq], Dn[m][:, 0:nq], AF.Ln, [Dn[m]], [r[m]])
                            self.act(r[m][:, 0:nq], r[m][:, 0:nq], AF.Exp, [r[m]], [r[m]], scale=-1.0)
                        o1, o2 = tmp.next(), tmp.next()
                        self.tt("dve", o1[:, 0:nq], O[0][:, 0:nq], r[0][:, 0:nq], ALU.mult, [O[0], r[0]], [o1])
                        self.tt("dve", o2[:, 0:nq], O[1][:, 0:nq], r[1][:, 0:nq], ALU.mult, [O[1], r[1]], [o2])
                        self.stt("dve", o1[:, 0:nq], o2[:, 0:nq], self.neglam[:, 0:1], o1[:, 0:nq], ALU.mult, ALU.add,
                                 [o1, o2, self.neglam], [o1])
                        sq = tmp.next()
                        self.tt("pool", sq[:, 0:nq], o1[:, 0:nq], o1[:, 0:nq], ALU.mult, [o1], [sq])

                        def fin_b():
                            ps = self.cur_sring.next()
                            self.mm(ps, ps[:, 0:nq], self.onesf[:, :], sq[:, 0:nq], True, True, [sq, self.onesf])
                            rs = tmp.next()
                            self.act(rs[:, 0:nq], ps[:, 0:nq], AF.Ln, [ps, self.eps_t], [rs], bias=self.eps_t[:], scale=1.0 / 128)
                            self.act(rs[:, 0:nq], rs[:, 0:nq], AF.Exp, [rs], [rs], scale=-0.5)
                            ob = ost.next()
                            self.stt("dve", ob[:, 0:nq], o1[:, 0:nq], self.subg[:, 0, :], rs[:, 0:nq], ALU.mult, ALU.mult,
                                     [o1, rs, self.subg], [ob])
                            self.store(d["mixT_s"], d["mixT_s"].t[1024 + h * 128:1024 + (h + 1) * 128, tok0 + qc0:tok0 + qc0 + nq],
                                       ob, ob[:, 0:nq])
                        return fin_b
                    self.attention(maps, vfn, nq, ktiles, fin, ["dve", "pe"])
        if self.deferred is not None:
            self.cur_sring = self.psr
            fn_, self.deferred = self.deferred, None
            fn_()

    def l0_s5(self):
        c, d = self.cfg, self.d
        self.phase()
        A = self.alloc
        L = 128
        lre, lim, dtv = A("lre", [128, 32], F32), A("lim", [128, 32], F32), A("dtv", [128, 32], F32)
        self.load(lre, lre[:], d["s5_a_re"], d["s5_a_re"].t.rearrange("(b two) n -> (two n) b", two=2), slow=True)
        self.load(lim, lim[:], d["s5_a_im"], d["s5_a_im"].t.rearrange("(b two) n -> (two n) b", two=2), slow=True)
        ldt = d["s5_log_dt"].t.rearrange("(b two) -> two b", two=2)
        for two in range(2):
            self.load(dtv, dtv[two * 64:(two + 1) * 64, :], d["s5_log_dt"], ldt[two].partition_broadcast(64), part=True, slow=True)
        self.ts("dve", lre[:], lre[:], -1e-4, None, ALU.min, None, [lre], [lre])
        self.act(dtv[:], dtv[:], AF.Exp, [dtv], [dtv])
        mag, ang = A("mag", [128, 32], F32), A("ang", [128, 32], F32)
        self.tt("dve", mag[:], lre[:], dtv[:], ALU.mult, [lre, dtv], [mag])
        self.act(mag[:], mag[:], AF.Exp, [mag], [mag])
        self.tt("dve", ang[:], lim[:], dtv[:], ALU.mult, [lim, dtv], [ang])
        cs1, sn1 = A("cs1", [128, 32], F32), A("sn1", [128, 32], F32)
        abre, abim = A("abre", [128, 32], F32), A("abim", [128, 32], F32)
        den, w1, w2 = A("den", [128, 32], F32), A("w1", [128, 32], F32), A("w2", [128, 32], F32)
        core, coim = A("core", [128, 32], F32), A("coim", [128, 32], F32)
        io = A("iota", [128, L], F32)
        ct, st = A("ct", [128, 32, L], F32), A("st", [128, 32, L], F32)
        XBT = A("XBT", [128, 32, 2, 128], BF16)
        YT = A("YT", [128, 32, 2, 128], BF16)
        wg = A("wglu", [128, 8, 1024], BF16)
        hpr, hpi = A("hpr", [128, 32], F32), A("hpi", [128, 32], F32)
        mark = self.aoff
        t1, t2 = A("t1", [128, 4096], F32), A("t2", [128, 4096], I32)
        self.sin_of(sn1, sn1[:], ang, ang[:], 0.0, t1, t2)
        self.sin_of(cs1, cs1[:], ang, ang[:], math.pi / 2, t1, t2)
        self.tt("dve", abre[:], mag[:], cs1[:], ALU.mult, [mag, cs1], [abre])
        self.tt("dve", abim[:], mag[:], sn1[:], ALU.mult, [mag, sn1], [abim])
        self.tt("dve", den[:], lre[:], lre[:], ALU.mult, [lre], [den])
        self.tt("dve", w1[:], lim[:], lim[:], ALU.mult, [lim], [w1])
        self.tt("dve", den[:], den[:], w1[:], ALU.add, [den, w1], [den])
        self.add("dve", lambda e: e.reciprocal(den[:], den[:]), reads=[den], writes=[den])
        self.ts("dve", w1[:], abre[:], -1.0, None, ALU.add, None, [abre], [w1])
        self.tt("dve", core[:], w1[:], lre[:], ALU.mult, [w1, lre], [core])
        self.tt("dve", w2[:], abim[:], lim[:], ALU.mult, [abim, lim], [w2])
        self.tt("dve", core[:], core[:], w2[:], ALU.add, [core, w2], [core])
        self.tt("dve", core[:], core[:], den[:], ALU.mult, [core, den], [core])
        self.tt("dve", coim[:], abim[:], lre[:], ALU.mult, [abim, lre], [coim])
        self.tt("dve", w2[:], w1[:], lim[:], ALU.mult, [w1, lim], [w2])
        self.tt("dve", coim[:], coim[:], w2[:], ALU.subtract, [coim, w2], [coim])
        self.tt("dve", coim[:], coim[:], den[:], ALU.mult, [coim, den], [coim])
        self.add("pool", lambda e: e.iota(io[:], pattern=[[1, L]], base=1, channel_multiplier=0, allow_small_or_imprecise_dtypes=True), writes=[io])
        targ = A("targ", [128, 32, L], F32)
        self.tt("dve", targ[:], io[:, :].unsqueeze(1).to_broadcast([128, 32, L]), ang[:, :].unsqueeze(2).to_broadcast([128, 32, L]),
                ALU.mult, [io, ang], [targ])
        targf = targ[:, :, :].rearrange("p a b -> p (a b)")
        self.sin_of(st, st[:, :, :].rearrange("p a b -> p (a b)"), targ, targf, 0.0, t1, t2)
        self.sin_of(ct, ct[:, :, :].rearrange("p a b -> p (a b)"), targ, targf, math.pi / 2, t1, t2)
        self.P.barrier()
        self.aoff = mark
        Zre, Zim = A("Zre", [128, 32, 128], F32), A("Zim", [128, 32, 128], F32)
        self.memset("pool", Zre, Zre[:], 0.0)
        self.memset("pool", Zim, Zim[:], 0.0)
        for g in range(64):
            b, two, gl = g // 2, g % 2, g % 8
            for (Z, src) in ((Zre, d["s5_b_re"]), (Zim, d["s5_b_im"])):
                self.load(Z, Z[two * 64:(two + 1) * 64, b, gl * 16:(gl + 1) * 16], src, src.t[g], part=True)
        Wre, Wim = A("Wre", [128, 32, 128], F32), A("Wim", [128, 32, 128], F32)
        bc = lambda t: t[:, :].unsqueeze(2).to_broadcast([128, 32, 128])
        t1 = A("tz", [128, 32, 128], F32)
        tmpz = t1[:, :, :]
        self.tt("dve", Wre[:], Zre[:], bc(core), ALU.mult, [Zre, core], [Wre])
        self.tt("dve", tmpz, Zim[:], bc(coim), ALU.mult, [Zim, coim], [t1])
        self.tt("dve", Wre[:], Wre[:], tmpz, ALU.subtract, [Wre, t1], [Wre])
        self.tt("dve", Wim[:], Zim[:], bc(core), ALU.mult, [Zim, core], [Wim])
        self.tt("dve", tmpz, Zre[:], bc(coim), ALU.mult, [Zre, coim], [t1])
        self.tt("dve", Wim[:], Wim[:], tmpz, ALU.add, [Wim, t1], [Wim])
        for b in range(32):
            ps = self.psr.next()
            self.tr(ps, ps[:, 0:128], Wre[:, b, :], self.identf[:], [Wre, self.identf], first=True)
            self.tr(ps, ps[:, 128:256], Wim[:, b, :], self.identf[:], [Wim, self.identf], first=False)
            self.cp("act", XBT[:, b, :, :], ps[:, 0:256].rearrange("p (r s) -> p r s", s=128), [ps], pwrites=[XBT])
        self.P.barrier()
        Yre, Yim = Zre, Zim
        self.memset("pool", Yre, Yre[:], 0.0)
        self.memset("pool", Yim, Yim[:], 0.0)
        for g in range(64):
            b, two, gl = g // 2, g % 2, g % 8
            for (Y, src) in ((Yre, d["s5_c_re"]), (Yim, d["s5_c_im"])):
                self.load(Y, Y[gl * 16:(gl + 1) * 16, b, two * 64:(two + 1) * 64], src, src.t[g], part=True)
        for b in range(32):
            ps = self.psr.next()
            self.tr(ps, ps[:, 0:128], Yre[:, b, :], self.identf[:], [Yre, self.identf], first=True)
            self.tr(ps, ps[:, 128:256], Yim[:, b, :], self.identf[:], [Yim, self.identf], first=False)
            self.cp("act", YT[:, b, 0, :], ps[:, 0:128], [ps], pwrites=[YT])
            self.act(YT[:, b, 1, :], ps[:, 128:256], AF.Copy, [ps], pwrites=[YT], scale=-1.0)
        self.load(wg, wg[:], d["s5_w_glu_b"], d["s5_w_glu_b"].t.rearrange("(k p) n -> p k n", p=128))
        self.P.barrier()
        self.aoff = mark
        ub_r = Ring([A("ub", [128, 8, 512], BF16) for _ in range(1)])
        xs_r = Ring([A("xs", [128, 512], F32) for _ in range(4)])
        yT_r = Ring([A("yT", [128, 8, 512], F32) for _ in range(1)])
        zT_r = Ring([A("zT", [128, 8, 512], BF16) for _ in range(1)])
        oc_t = Ring([A("o8", [128, 8, 128], F32) for _ in range(8)])
        hb_r = Ring([A("hb", [128, 2, 8, 128], BF16) for _ in range(2)])
        g_t = Ring([A("gt", [128, 512], F32) for _ in range(3)])
        os_r = Ring([A("s5o", [128, 8, 512], BF16) for _ in range(1)])
        bg_s = Ring([A("bgs", [128, 256], F32) for _ in range(3)])
        bg_b = Ring([A("bgb", [128, 256], BF16) for _ in range(3)])
        n_iter = sum(((T_ + L - 1) // L) * 4 for (_, T_, _) in [(0, c.P_LEN, None)] + [(0, c.SL, i) for i in range(c.NS)])
        per_iter = (len(self.bg_jobs) + n_iter - 1) // n_iter
        print("S5 arena use", self.aoff, "of", self.arena_words, "bg jobs", len(self.bg_jobs), "per iter", per_iter)
        seqs = [(0, c.P_LEN, None)] + [(c.P_LEN + i * c.SL, c.SL, i) for i in range(c.NS)]
        for sq_i, (tok0, T_, si) in enumerate(seqs):
            if si is None:
                self.memset("dve", hpr, hpr[:], 0.0)
                self.memset("dve", hpi, hpi[:], 0.0)
            else:
                self.load(hpr, hpr[:], d["s5re0"], d["s5re0"].t[si].rearrange("(b two) n -> (two n) b", two=2), slow=True)
                self.load(hpi, hpi[:], d["s5im0"], d["s5im0"].t[si].rearrange("(b two) n -> (two n) b", two=2), slow=True)
            for m0 in range(0, T_, 512):
                nt = min(512, T_ - m0)
                ub, yT, zT = ub_r.next(), yT_r.next(), zT_r.next()
                uf = ub
                self.load(ub, ub[:, :, 0:nt], d["uT_s"], d["uT_s"].t[:, tok0 + m0:tok0 + m0 + nt].rearrange("(c p) t -> p c t", p=128))
                for s0 in range(0, nt, L):
                    ln = min(L, nt - s0)
                    for o in range(4):
                        T1, T2, T3, T4 = oc_t.next(), oc_t.next(), oc_t.next(), oc_t.next()
                        for half in range(2):
                            pxr, pxi = self.psr.next(), self.psr.next()
                            for bb in range(4):
                                b = o * 8 + half * 4 + bb
                                self.mm(pxr, pxr[:, bb * 128:bb * 128 + ln], XBT[:, b, 0, :], ub[:, b // 4, s0:s0 + ln], True, True, [XBT, ub], excl=(bb == 0))
                                self.mm(pxi, pxi[:, bb * 128:bb * 128 + ln], XBT[:, b, 1, :], ub[:, b // 4, s0:s0 + ln], True, True, [XBT, ub], excl=(bb == 0))
                            b0 = o * 8 + half * 4
                            hs = slice(half * 4, half * 4 + 4)
                            xsr, xsi = xs_r.next(), xs_r.next()
                            self.cp("act", xsr[:, :], pxr[:, :], [pxr], [xsr])
                            self.cp("act", xsi[:, :], pxi[:, :], [pxi], [xsi])
                            vr = xsr[:, :].rearrange("p (b t) -> p b t", t=128)[:, :, 0:ln]
                            vi = xsi[:, :].rearrange("p (b t) -> p b t", t=128)[:, :, 0:ln]
                            cta, sta = ct[:, b0:b0 + 4, 0:ln], st[:, b0:b0 + 4, 0:ln]
                            self.tt("dve", T1[:, hs, 0:ln], vr, cta, ALU.mult, [xsr, ct], pwrites=[T1])
                            self.tt("dve", T2[:, hs, 0:ln], vi, sta, ALU.mult, [xsi, st], pwrites=[T2])
                            self.tt("dve", T3[:, hs, 0:ln], vi, cta, ALU.mult, [xsi, ct], pwrites=[T3])
                            self.tt("dve", T4[:, hs, 0:ln], vr, sta, ALU.mult, [xsr, st], pwrites=[T4])
                        if self.debug and sq_i == 0 and m0 == 0 and s0 == 0 and o == 0:
                            for nm_, tl in (("T1", T1), ("T2", T2), ("T3", T3), ("T4", T4)):
                                dt_ = self.P.dram("dbg_%s" % nm_, [128, 8, 128], F32, "ExternalOutput")
                                self.store(dt_, dt_.t, tl, tl[:, :, :])
                            dpx = self.alloc("dpx", [128, 512], F32)
                            self.cp("dve", dpx[:], pxr[:, :], [pxr], [dpx])
                            dt_ = self.P.dram("dbg_pxr", [128, 512], F32, "ExternalOutput")
                            self.store(dt_, dt_.t, dpx, dpx[:])
                        self.tt("pool", T1[:, :, 0:ln], T1[:, :, 0:ln], T2[:, :, 0:ln], ALU.add, [T1, T2], [T1])
                        self.tt("pool", T3[:, :, 0:ln], T3[:, :, 0:ln], T4[:, :, 0:ln], ALU.subtract, [T3, T4], [T3])
                        Gr, Gi = T2, T4
                        for bb in range(8):
                            b = o * 8 + bb
                            self.add("dve", lambda e, o_=Gr[:, bb, 0:ln], a=mag[:, b:b + 1].to_broadcast([128, ln]), x_=T1[:, bb, 0:ln], i_=hpr[:, b:b + 1]:
                                     e.tensor_tensor_scan(out=o_, data0=a, data1=x_, initial=i_, op0=ALU.mult, op1=ALU.add),
                                     reads=[mag, T1, hpr], pwrites=[Gr])
                            self.add("dve", lambda e, o_=Gi[:, bb, 0:ln], a=mag[:, b:b + 1].to_broadcast([128, ln]), x_=T3[:, bb, 0:ln], i_=hpi[:, b:b + 1]:
                                     e.tensor_tensor_scan(out=o_, data0=a, data1=x_, initial=i_, op0=ALU.mult, op1=ALU.add),
                                     reads=[mag, T3, hpi], pwrites=[Gi])
                        cta, sta = ct[:, o * 8:o * 8 + 8, 0:ln], st[:, o * 8:o * 8 + 8, 0:ln]
                        Hr, Hi, U1, U2 = oc_t.next(), oc_t.next(), T1, T3
                        self.tt("pool", Hr[:, :, 0:ln], Gr[:, :, 0:ln], cta, ALU.mult, [Gr, ct], [Hr])
                        self.tt("pool", U1[:, :, 0:ln], Gi[:, :, 0:ln], sta, ALU.mult, [Gi, st], [U1])
                        self.tt("pool", Hr[:, :, 0:ln], Hr[:, :, 0:ln], U1[:, :, 0:ln], ALU.subtract, [Hr, U1], [Hr])
                        self.tt("dve", Hi[:, :, 0:ln], Gi[:, :, 0:ln], cta, ALU.mult, [Gi, ct], [Hi])
                        self.tt("dve", U2[:, :, 0:ln], Gr[:, :, 0:ln], sta, ALU.mult, [Gr, st], [U2])
                        self.tt("dve", Hi[:, :, 0:ln], Hi[:, :, 0:ln], U2[:, :, 0:ln], ALU.add, [Hi, U2], [Hi])
                        if self.debug and sq_i == 0 and m0 == 0 and s0 == 0 and o == 0:
                            for nm_, tl, shp, dty in (("XBT", XBT, [128, 32, 2, 128], BF16), ("YT", YT, [128, 32, 2, 128], BF16), ("ct", ct, [128, 32, 128], F32),
                                                      ("st", st, [128, 32, 128], F32), ("core", core, [128, 32], F32), ("mag", mag, [128, 32], F32),
                                                      ("ang", ang, [128, 32], F32), ("ub", ub, [128, 8, 512], BF16)):
                                dt_ = self.P.dram("dbg_" + nm_, shp, dty, "ExternalOutput")
                                self.store(dt_, dt_.t, tl, tl.t)
                        if self.debug and sq_i == 0 and m0 == 0 and s0 in (0, 128) and o == 0:
                            for nm_, tl in (("xr", T1), ("xi", T3), ("gr", Gr), ("gi", Gi), ("hr", Hr), ("hi", Hi)):
                                dt_ = self.P.dram("dbg_%s_%d" % (nm_, s0), [128, 8, 128], F32, "ExternalOutput")
                                self.store(dt_, dt_.t, tl, tl[:, :, :])
                            dt_ = self.P.dram("dbg_hpr_%d" % s0, [128, 32], F32, "ExternalOutput")
                            self.store(dt_, dt_.t, hpr, hpr[:, :])
                        self.cp("act", hpr[:, o * 8:o * 8 + 8], Hr[:, :, ln - 1], [Hr], pwrites=[hpr])
                        self.cp("act", hpi[:, o * 8:o * 8 + 8], Hi[:, :, ln - 1], [Hi], pwrites=[hpi])
                        hb = hb_r.next()
                        self.cp("act", hb[:, 0, :, 0:ln], Hr[:, :, 0:ln], [Hr], pwrites=[hb])
                        self.cp("act", hb[:, 1, :, 0:ln], Hi[:, :, 0:ln], [Hi], pwrites=[hb])
                        for _ in range(per_iter):
                            if self.bg_jobs:
                                self.conv_emit(self.bg_jobs.pop(0), bg_s.next(), bg_b.next(), "act", "act")
                        for fo in range(2):
                            fc = o * 2 + fo
                            ps = self.psr.next()
                            n_mm = 0
                            for bl in range(4):
                                for ri in range(2):
                                    self.mm(ps, ps[:, 0:ln], YT[:, fc * 4 + bl, ri, :], hb[:, ri, fo * 4 + bl, 0:ln], n_mm == 0, n_mm == 7, [YT, hb])
                                    n_mm += 1
                            self.stt("dve", yT[:, fc, s0:s0 + ln], uf[:, fc, s0:s0 + ln], self.s5d[:, fc, :], ps[:, 0:ln], ALU.mult, ALU.add,
                                     [uf, ps, self.s5d], pwrites=[yT])
                for fc in range(8):
                    g1, g2 = g_t.next(), g_t.next()
                    y = yT[:, fc, 0:nt]
                    self.tt("pool", g1[:, 0:nt], y, y, ALU.mult, [yT], [g1])
                    self.ts("pool", g1[:, 0:nt], g1[:, 0:nt], 0.044715, 1.0, ALU.mult, ALU.add, [g1], [g1])
                    self.tt("pool", g1[:, 0:nt], g1[:, 0:nt], y, ALU.mult, [g1, yT], [g1])
                    self.act(g2[:, 0:nt], g1[:, 0:nt], AF.Sigmoid, [g1], [g2], scale=1.5957691216057308)
                    self.tt("dve", zT[:, fc, 0:nt], g2[:, 0:nt], y, ALU.mult, [g2, yT], pwrites=[zT])
                so = os_r.next()
                for oc in range(8):
                    ps = self.psr.next()
                    for k in range(8):
                        self.mm(ps, ps[:, 0:nt], wg[:, k, oc * 128:(oc + 1) * 128], zT[:, k, 0:nt], k == 0, k == 7, [wg, zT])
                    g2 = g_t.next()
                    self.act(g2[:, 0:nt], ps[:, 0:nt], AF.Sigmoid, [ps, self.bglu], [g2], bias=self.bglu[:, oc, :])
                    self.tt("dve", so[:, oc, 0:nt], g2[:, 0:nt], zT[:, oc, 0:nt], ALU.mult, [g2, zT], pwrites=[so])
                self.store(d["mixT_s"], d["mixT_s"].t[0:1024, tok0 + m0:tok0 + m0 + nt].rearrange("(c p) t -> p c t", p=128), so, so[:, :, 0:nt])
            if sq_i == len(seqs) - 1:
                while self.bg_jobs:
                    self.conv_emit(self.bg_jobs.pop(0), bg_s.next(), bg_b.next(), "act", "act")
            self.store(d["s5re_o"], d["s5re_o"].t[sq_i].rearrange("(b two) n -> (two n) b", two=2), hpr, hpr[:], slow=True)
            self.store(d["s5im_o"], d["s5im_o"].t[sq_i].rearrange("(b two) n -> (two n) b", two=2), hpi, hpi[:], slow=True)

    def mix_ffn(self, layer, mixsrc, wout, xsrc, final):
        c, d = self.cfg, self.d
        self.phase()
        A = self.alloc
        self.slabs = Ring([A("slab", [128, 16 * 512], BF16) for _ in range(3)])
        xr = Ring([A("x1", [128, D], F32) for _ in range(4)])
        self.xnr = Ring([A("xn", [128, D], BF16) for _ in range(4)])
        self.ssr = Ring([A("ss", [128, 1], F32) for _ in range(8)])
        hT = A("hT", [128, 16, 512], BF16)
        big = A("big", [128, NJ, 512], BF16)
        gt_r = Ring([A("gt", [128, 2 * 2 + 512], F32) for _ in range(3)])
        cc_r = Ring([A("cc", [128, 512], F32) for _ in range(3)])
        cso_r = Ring([A("cso", [2, 512], F32) for _ in range(3)])
        win, wdn = d["ffn_w_in_b%d" % layer], d["ffn_w_down_b%d" % layer]
        for (tok0, nt, segs) in c.macros:
            S = nt // 128
            self.load(big, big[:, 0:16, 0:nt], mixsrc, mixsrc.t[:, tok0:tok0 + nt].rearrange("(c p) t -> p c t", p=128))
            xts = []
            for s in range(S):
                xt = xr.next()
                self.load(xt, xt[:], xsrc, xsrc.t[tok0 + s * 128:tok0 + (s + 1) * 128, :])
                xts.append(xt)
            for nb in range(4):
                sl, sv = self.slab(wout, 0, 16, nb * 512, 512)
                for s in range(S):
                    ps = self.formB(sl, lambda k, sv=sv: sv[:, k, :], 16, big, s, 512)
                    self.tt("dve", xts[s][:, nb * 512:(nb + 1) * 512], ps[:, :], xts[s][:, nb * 512:(nb + 1) * 512], ALU.add, [ps, xts[s]], [xts[s]])
            self.norm_T(xts, D, self.gffn[:, layer], hT)
            ends = [(col0 + ln - 2, seq) for (col0, ln, seq) in segs if (seq != 0 or tok0 + nt == c.P_LEN)]
            for j in range(11):
                slv, svv = self.slab(win, 0, 16, j * 512, 512)
                slg, svg = self.slab(win, 0, 16, DFF + j * 512, 512)
                for oc in range(4):
                    ch = j * 4 + oc
                    pv = self.formA(slv, svv, oc, 16, hT, nt)
                    pg = self.formA(slg, svg, oc, 16, hT, nt)
                    gt, cc = gt_r.next(), cc_r.next()
                    off = 0
                    for (col0, ln, seq) in segs:
                        gseg = gt[:, off:off + 2 + ln]
                        self.cp("act", gseg[:, 0:2], self.halo[:, layer, seq, ch, :], [self.halo], pwrites=[gt])
                        self.cp("act", gseg[:, 2:2 + ln], pg[:, col0:col0 + ln], [pg], pwrites=[gt])
                        w = self.convw[:, layer, ch, :]
                        self.ts("dve", cc[:, col0:col0 + ln], gseg[:, 2:2 + ln], w[:, 2:3], self.convb[:, layer, ch, :], ALU.mult, ALU.add,
                                [gt, self.convw, self.convb], pwrites=[cc])
                        self.stt("dve", cc[:, col0:col0 + ln], gseg[:, 1:1 + ln], w[:, 1:2], cc[:, col0:col0 + ln], ALU.mult, ALU.add,
                                 [gt, cc, self.convw], pwrites=[cc])
                        self.stt("dve", cc[:, col0:col0 + ln], gseg[:, 0:ln], w[:, 0:1], cc[:, col0:col0 + ln], ALU.mult, ALU.add,
                                 [gt, cc, self.convw], pwrites=[cc])
                        self.cp("act", self.halo[:, layer, seq, ch, :], gseg[:, ln:ln + 2], [gt], pwrites=[self.halo])
                        off += 2 + ln
                    self.act(cc[:, 0:nt], cc[:, 0:nt], AF.Silu, [cc], [cc])
                    self.tt("dve", big[:, ch, 0:nt], cc[:, 0:nt], pv[:, 0:nt], ALU.mult, [cc, pv], pwrites=[big])
                for (tcol, seq) in ends:
                    ps = self.psr.next()
                    for k in range(16):
                        self.mm(ps, ps[0:2, :], hT[:, k, tcol:tcol + 2], svg[:, k, :], k == 0, k == 15, [hT, slg])
                    cso = cso_r.next()
                    self.cp("act", cso[0:2, :], ps[0:2, :], [ps], [cso])
                    self.store(d["conv_o"], d["conv_o"].t[layer, seq, :, j * 512:(j + 1) * 512], cso, cso[0:2, :])
            for nb in range(4):
                pd = [self.psr.next() for _ in range(S)]
                for kh in range(4):
                    sl, sv = self.slab(wdn, kh * 11, 11, nb * 512, 512)
                    for s in range(S):
                        for k in range(11):
                            self.mm(pd[s], pd[s][:, :], big[:, kh * 11 + k, s * 128:(s + 1) * 128], sv[:, k, :],
                                    kh == 0 and k == 0, kh == 3 and k == 10, [big, sl])
                for s in range(S):
                    self.tt("dve", xts[s][:, nb * 512:(nb + 1) * 512], pd[s][:, :], xts[s][:, nb * 512:(nb + 1) * 512], ALU.add,
                            [pd[s], xts[s]], [xts[s]])
            if not final:
                for s in range(S):
                    self.store(d["x_s"], d["x_s"].t[tok0 + s * 128:tok0 + (s + 1) * 128, :], xts[s], xts[s][:])
            else:
                for s in range(S):
                    xt = xts[s]
                    ss, xn = self.ssr.next(), self.xnr.next()
                    self.act(xn[:, :], xt[:, :], AF.Square, [xt], [xn, ss], accum=ss[:])
                    self.act(ss[:], ss[:], AF.Sqrt, [ss, self.eps_t], [ss], bias=self.eps_t[:], scale=1.0 / D)
                    self.add("dve", lambda e, o=ss[:]: e.reciprocal(o, o), reads=[ss], writes=[ss])
                    self.stt("dve", xt[:, :], xt[:, :], ss[:, 0:1], self.gfin[:, :], ALU.mult, ALU.mult, [xt, ss, self.gfin], [xt])
                    self.store(d["y_all"], d["y_all"].t[tok0 + s * 128:tok0 + (s + 1) * 128, :], xt, xt[:])

    def mla_expand(self, ckvT, nt, kslices, wkv_slabs, stg, stv):
        d = self.d
        S = (nt + 127) // 128
        for qd in range(4):
            sl, sv = wkv_slabs(qd)
            sv4 = sv.rearrange("p k (h e) -> p k h e", e=256)
            so = stg.next()
            for hh in range(4):
                ps = self.psr.next()
                for k in range(4):
                    self.mm(ps, ps[:, 0:nt], sv4[:, k, hh, 0:128], ckvT[:, k, 0:nt], k == 0, k == 3, [sl, ckvT])
                self.cp("act" if hh % 2 else "dve", so[:, hh, 0:nt], ps[:, 0:nt], [ps], pwrites=[so])
            for (col0, ln, k0) in kslices:
                self.store(d["knT_s"], d["knT_s"].t[qd * 512:(qd + 1) * 512, k0:k0 + ln].rearrange("(c p) t -> p c t", p=128),
                           so, so[:, :, col0:col0 + ln])
            vo = stv.next()
            for s in range(S):
                ps = self.psr.next()
                for k in range(4):
                    self.mm(ps, ps[:, :].rearrange("p (h e) -> p h e", e=128), ckvT[:, k, s * 128:(s + 1) * 128], sv4[:, k, :, 128:256],
                            k == 0, k == 3, [sl, ckvT])
                self.cp("act" if s % 2 else "dve", vo[:, s, :], ps[:, :], [ps], pwrites=[vo])
            for (col0, ln, k0) in kslices:
                for s in range(S):
                    a, b = max(col0, s * 128), min(col0 + ln, (s + 1) * 128)
                    if a >= b:
                        continue
                    self.store(d["v1_s"], d["v1_s"].t[k0 + a - col0:k0 + b - col0, qd * 512:(qd + 1) * 512], vo, vo[a - s * 128:b - s * 128, s, :])

    def l1_past(self):
        c, d = self.cfg, self.d
        self.phase()
        A = self.alloc
        self.slabs = Ring([A("slab", [128, 16 * 512], BF16) for _ in range(3)])
        wkv_slabs = lambda qd: self.slab(d["mla_w_ukv_b"], 0, 4, qd * 1024, 1024)
        ckvT = A("ckvT", [128, 4, 512], BF16)
        stg = Ring([A("stg", [128, 4, 512], BF16) for _ in range(2)])
        stv = Ring([A("stv", [128, 4, 512], BF16) for _ in range(2)])
        kpb = Ring([A("kpb", [128, 64], BF16) for _ in range(2)])
        kpT = A("kpT", [64, 512], BF16)
        pf = Ring([A("pf", [128, 4, 512], F32) for _ in range(2)])
        pbf = Ring([A("pbf", [128, 512], BF16) for _ in range(4)])
        pk = Ring([A("pk", [128, 4, 64], F32) for _ in range(2)])
        for i in range(c.NS):
            k0 = c.koff(i)
            for t0 in range(0, c.PAST, 512):
                n = min(512, c.PAST - t0)
                S = n // 128
                f = pf.next()
                self.load(f, f[:, 0:S, :], d["cckv"], d["cckv"].t[i, t0:t0 + n, :].rearrange("(s p) r -> p s r", p=128))
                bts = []
                for s in range(S):
                    b = pbf.next()
                    self.cp("dve" if s % 2 else "pool", b[:], f[:, s, :], [f], [b])
                    bts.append(b)
                for cch in range(4):
                    ps = self.psr.next()
                    pb = ps.t[:, :].bitcast(BF16)
                    for s in range(S):
                        self.tr(ps, pb[:, s * 128:(s + 1) * 128], bts[s][:, cch * 128:(cch + 1) * 128], self.identb[:], [bts[s], self.identb], first=(s == 0))
                    self.cp("act", ckvT[:, cch, 0:n], pb[:, 0:n], [ps], pwrites=[ckvT])
                self.mla_expand(ckvT, n, [(0, n, k0 + t0)], wkv_slabs, stg, stv)
                kf = pk.next()
                self.load(kf, kf[:, 0:S, :], d["ckpe"], d["ckpe"].t[i, t0:t0 + n, :].rearrange("(s p) r -> p s r", p=128))
                ps = self.psr.next()
                pb = ps.t[:, :].bitcast(BF16)
                for s in range(S):
                    kb = kpb.next()
                    self.cp("dve", kb[:], kf[:, s, :], [kf], [kb])
                    self.tr(ps, pb[0:64, s * 128:(s + 1) * 128], kb[:, :], self.identb[:], [kb, self.identb], first=(s == 0))
                self.cp("act", kpT[:, 0:n], pb[0:64, 0:n], [ps], [kpT])
                self.store(d["kpeT_s"], d["kpeT_s"].t[:, k0 + t0:k0 + t0 + n], kpT, kpT[:, 0:n])

    def l1_a(self):
        c, d = self.cfg, self.d
        self.phase()
        A = self.alloc
        self.slabs = Ring([A("slab", [128, 16 * 512], BF16) for _ in range(2)])
        wkv_slabs = lambda qd: self.slab(d["mla_w_ukv_b"], 0, 4, qd * 1024, 1024)
        xr = Ring([A("x", [128, D], F32) for _ in range(4)])
        self.xnr = Ring([A("xn", [128, D], BF16) for _ in range(4)])
        self.ssr = Ring([A("ss", [128, 1], F32) for _ in range(12)])
        hT = A("hT", [128, 16, 512], BF16)
        tk_r = Ring([A("tk", [128, 1088], F32) for _ in range(4)])
        cqT, ckvT = A("cqT", [128, 4, 512], BF16), A("ckvT", [128, 4, 512], BF16)
        stg = Ring([A("stg", [128, 4, 512], BF16) for _ in range(2)])
        stv = Ring([A("stv", [128, 4, 512], BF16) for _ in range(2)])
        ckvo = Ring([A("ckvo", [128, 512], F32) for _ in range(2)])
        kpo = Ring([A("kpo", [128, 64], F32) for _ in range(2)])
        kpb = Ring([A("kpb", [128, 64], BF16) for _ in range(2)])
        kpT = A("kpT", [64, 512], BF16)
        qpb = Ring([A("qpb", [128, 1024], BF16) for _ in range(4)])
        qpT = A("qpT", [128, 8, 512], BF16)
        rt = Ring([A("rt", [128, 256], F32) for _ in range(6)])
        scale = 192.0 ** -0.5
        for (tok0, nt, segs) in c.macros:
            S = nt // 128
            koffs = self.key_cols(tok0, nt, segs)
            xts = []
            for s in range(S):
                xt = xr.next()
                self.load(xt, xt[:], d["x_s"], d["x_s"].t[tok0 + s * 128:tok0 + (s + 1) * 128, :])
                xts.append(xt)
            self.norm_T(xts, D, self.gmix[:, 1], hT)
            tks = [tk_r.next() for _ in range(S)]
            for (n0, n) in ((0, 512), (512, 512), (1024, 64)):
                sl, sv = self.slab(d["w_in_odd_b"], 0, 16, n0, n)
                for s in range(S):
                    ps = self.formB(sl, lambda k, sv=sv: sv[:, k, :], 16, hT, s, n)
                    self.cp("act" if s % 2 else "dve", tks[s][:, n0:n0 + n], ps[:, 0:n], [ps], pwrites=[tks[s]])
            cq_views = [T(self.nm("cqv"), tks[s][:, 0:512]) for s in range(S)]
            for s in range(S):
                cq_views[s].ws, cq_views[s].rs = tks[s].ws, tks[s].rs
            self.norm_T(cq_views, 512, self.qng[:], cqT)
            ckv_views = [T(self.nm("ckvv"), tks[s][:, 512:1024]) for s in range(S)]
            for s in range(S):
                ckv_views[s].ws, ckv_views[s].rs = tks[s].ws, tks[s].rs
            rstds = self.norm_T(ckv_views, 512, self.kvng[:], ckvT)
            for s in range(S):
                o = ckvo.next()
                self.stt("dve", o[:], tks[s][:, 512:1024], rstds[s][:, 0:1], self.kvgb[:], ALU.mult, ALU.mult, [tks[s], rstds[s], self.kvgb], [o])
                self.store(d["ckv_o"], d["ckv_o"].t[tok0 + s * 128:tok0 + (s + 1) * 128, :], o, o[:])
            ps = self.psr.next()
            pb = ps.t[:, :].bitcast(BF16)
            for s in range(S):
                gs = (tok0 // 128) + s
                cs_, sn_ = self.ropec[:, gs, :], self.ropes[:, gs, :]
                x1, x2 = tks[s][:, 1024:1056], tks[s][:, 1056:1088]
                o = kpo.next()
                a1, a2 = rt.next(), rt.next()
                self.tt("pool", o[:, 0:32], x1, cs_, ALU.mult, [tks[s], self.ropec], pwrites=[o])
                self.tt("pool", a1[:, 0:32], x2, sn_, ALU.mult, [tks[s], self.ropes], [a1])
                self.tt("pool", o[:, 0:32], o[:, 0:32], a1[:, 0:32], ALU.subtract, [o, a1], pwrites=[o])
                self.tt("pool", o[:, 32:64], x2, cs_, ALU.mult, [tks[s], self.ropec], pwrites=[o])
                self.tt("pool", a2[:, 0:32], x1, sn_, ALU.mult, [tks[s], self.ropes], [a2])
                self.tt("pool", o[:, 32:64], o[:, 32:64], a2[:, 0:32], ALU.add, [o, a2], pwrites=[o])
                self.store(d["kpe_o"], d["kpe_o"].t[tok0 + s * 128:tok0 + (s + 1) * 128, :], o, o[:])
                kb = kpb.next()
                self.cp("dve", kb[:], o[:], [o], [kb])
                self.tr(ps, pb[0:64, s * 128:(s + 1) * 128], kb[:, :], self.identb[:], [kb, self.identb], first=(s == 0))
            self.cp("act", kpT[:, 0:nt], pb[0:64, 0:nt], [ps], [kpT])
            for (col0, ln, k0) in koffs:
                self.store(d["kpeT_s"], d["kpeT_s"].t[:, k0:k0 + ln], kpT, kpT[:, col0:col0 + ln])
            self.mla_expand(ckvT, nt, koffs, wkv_slabs, stg, stv)
            qbs = [qpb.next() for _ in range(S)]
            for hf in range(2):
                slq, svq = self.slab(d["mla_w_uq_b"], 0, 4, hf * 1536, 1536)
                wq4 = svq.rearrange("p k (h e) -> p k h e", e=192)
                for hq in range(2):
                    so = stg.next()
                    for hh in range(4):
                        hl = hq * 4 + hh
                        ps = self.psr.next()
                        for k in range(4):
                            self.mm(ps, ps[:, 0:nt], wq4[:, k, hl, 0:128], cqT[:, k, 0:nt], k == 0, k == 3, [slq, cqT])
                        self.act(so[:, hh, 0:nt], ps[:, 0:nt], AF.Copy, [ps], pwrites=[so], scale=scale)
                    r0 = (hf * 2 + hq) * 512
                    self.store(d["qnT_s"], d["qnT_s"].t[r0:r0 + 512, tok0:tok0 + nt].rearrange("(c p) t -> p c t", p=128), so, so[:, :, 0:nt])
                for s in range(S):
                    gs = (tok0 // 128) + s
                    qb = qbs[s]
                    ps = self.psr.next()
                    for k in range(4):
                        self.mm(ps, ps[:, :].rearrange("p (h e) -> p h e", e=64), cqT[:, k, s * 128:(s + 1) * 128], wq4[:, k, :, 128:192],
                                k == 0, k == 3, [slq, cqT])
                    p3 = ps[:, :].rearrange("p (h e) -> p h e", e=64)
                    x1, x2 = p3[:, :, 0:32], p3[:, :, 32:64]
                    cb_ = self.ropec[:, gs, :].unsqueeze(1).to_broadcast([128, 8, 32])
                    sb_ = self.ropes[:, gs, :].unsqueeze(1).to_broadcast([128, 8, 32])
                    a1, a2, a3, a4 = rt.next(), rt.next(), rt.next(), rt.next()
                    v3 = lambda t_: t_[:, 0:256].rearrange("p (h e) -> p h e", e=32)
                    self.tt("dve", v3(a1), x1, cb_, ALU.mult, [ps, self.ropec], [a1])
                    self.tt("dve", v3(a2), x2, sb_, ALU.mult, [ps, self.ropes], [a2])
                    self.tt("dve", v3(a3), x2, cb_, ALU.mult, [ps, self.ropec], [a3])
                    self.tt("dve", v3(a4), x1, sb_, ALU.mult, [ps, self.ropes], [a4])
                    q3 = qb[:, hf * 512:(hf + 1) * 512].rearrange("p (h e) -> p h e", e=64)
                    self.tt("pool", v3(a1), v3(a1), v3(a2), ALU.subtract, [a1, a2], [a1])
                    self.tt("pool", v3(a3), v3(a3), v3(a4), ALU.add, [a3, a4], [a3])
                    self.act(q3[:, :, 0:32], v3(a1), AF.Copy, [a1], pwrites=[qb], scale=scale)
                    self.act(q3[:, :, 32:64], v3(a3), AF.Copy, [a3], pwrites=[qb], scale=scale)
            for s in range(S):
                qb = qbs[s]
                for pr in range(8):
                    ps = self.psr.next()
                    pb = ps.t[:, :].bitcast(BF16)
                    self.tr(ps, pb[:, 0:128], qb[:, pr * 128:(pr + 1) * 128], self.identb[:], [qb, self.identb], first=True)
                    self.cp("act" if pr % 2 else "dve", qpT[:, pr, s * 128:(s + 1) * 128], pb[:, 0:128], [ps], pwrites=[qpT])
            self.store(d["qpeT_s"], d["qpeT_s"].t[:, tok0:tok0 + nt].rearrange("(c p) t -> p c t", p=128), qpT, qpT[:, :, 0:nt])

    def l1_attn(self):
        c, d = self.cfg, self.d
        self.phase()
        A = self.alloc
        seqs = [(0, c.P_LEN, 0, c.P_LEN)]
        for i in range(c.NS):
            seqs.append((c.koff(i), c.KSEQ, c.P_LEN + i * c.SL, c.SL))
        maxk = max(c.P_LEN, c.KSEQ)
        maxq = max(c.P_LEN, c.SL)
        nkt_max = (maxk + 127) // 128
        KTr = Ring([A("KT", [128, maxk], BF16) for _ in range(2)])
        QTr = Ring([A("QT", [128, maxq], BF16) for _ in range(2)])
        QPr = Ring([A("QP", [64, maxq], BF16) for _ in range(2)])
        Vr = Ring([A("V", [128, nkt_max, 128], BF16) for _ in range(2)])
        KP = A("KP", [64, maxk], BF16)
        self.er = Ring([A("E", [128, 512], BF16) for _ in range(6)])
        self.accr = Ring([A("dacc", [128, 512], F32) for _ in range(4)])
        tmp = Ring([A("atmp", [128, 512], F32) for _ in range(3)])
        ost = Ring([A("aost", [128, 512], BF16) for _ in range(3)])
        self.deferred = None
        for (key0, nkeys, tok0, ntok) in seqs:
            nfull, rem = nkeys // 128, nkeys % 128
            self.load(KP, KP[:, 0:nkeys], d["kpeT_s"], d["kpeT_s"].t[:, key0:key0 + nkeys])
            for h in range(16):
                KT, QT, QP, V = KTr.next(), QTr.next(), QPr.next(), Vr.next()
                self.load(KT, KT[:, 0:nkeys], d["knT_s"], d["knT_s"].t[h * 128:(h + 1) * 128, key0:key0 + nkeys])
                self.load(QT, QT[:, 0:ntok], d["qnT_s"], d["qnT_s"].t[h * 128:(h + 1) * 128, tok0:tok0 + ntok])
                self.load(QP, QP[:, 0:ntok], d["qpeT_s"], d["qpeT_s"].t[h * 64:(h + 1) * 64, tok0:tok0 + ntok])
                if nfull:
                    self.load(V, V[:, 0:nfull, :], d["v1_s"],
                              d["v1_s"].t[key0:key0 + nfull * 128, h * 128:(h + 1) * 128].rearrange("(t p) d -> p t d", p=128), part=True)
                if rem:
                    self.load(V, V[0:rem, nfull, :], d["v1_s"], d["v1_s"].t[key0 + nfull * 128:key0 + nkeys, h * 128:(h + 1) * 128], part=True)
                causal = (ntok == nkeys)
                for qm in range((ntok + 511) // 512):
                    nq = min(512, ntok - qm * 512)
                    qc0 = qm * 512
                    if causal:
                        ktiles = [(kt, 128, 0, False) for kt in range(4 * qm)] + \
                                 [(4 * qm + j, 128, 128 * j, True) for j in range(nq // 128)]
                    else:
                        ktiles = [(kt, 128, 0, False) for kt in range(nfull)] + ([(nfull, rem, 0, False)] if rem else [])
                    maps = [[(lambda kt, nk, KT=KT: KT[:, kt * 128:kt * 128 + nk],
                              lambda q0, n, QT=QT, qc0=qc0: QT[:, qc0 + q0:qc0 + q0 + n], [KT, QT]),
                             (lambda kt, nk: KP[:, kt * 128:kt * 128 + nk],
                              lambda q0, n, QP=QP, qc0=qc0: QP[:, qc0 + q0:qc0 + q0 + n], [KP, QP])]]
                    vfn = lambda kt, nk, V=V: (V[0:nk, kt, :], [V])

                    def fin(O, Dn, h=h, tok0=tok0, qc0=qc0, nq=nq):
                        r = tmp.next()
                        self.act(r[:, 0:nq], Dn[0][:, 0:nq], AF.Ln, [Dn[0]], [r])
                        self.act(r[:, 0:nq], r[:, 0:nq], AF.Exp, [r], [r], scale=-1.0)
                        ob = ost.next()
                        self.tt("dve", ob[:, 0:nq], O[0][:, 0:nq], r[:, 0:nq], ALU.mult, [O[0], r], [ob])
                        self.store(d["oT_s"], d["oT_s"].t[h * 128:(h + 1) * 128, tok0 + qc0:tok0 + qc0 + nq], ob, ob[:, 0:nq])
                    self.attention(maps, vfn, nq, ktiles, fin, ["pe"])

    def build(self):
        d_stop = self.stop_after
        self.declare()
        self.setup_consts()
        self.setup_params()
        steps = [("prologue", self.prologue), ("l0_a", self.l0_a), ("l0_past", self.l0_past), ("l0_s5", self.l0_s5),
                 ("l0_attn", self.l0_attn),
                 ("l0_ffn", lambda: self.mix_ffn(0, self.d["mixT_s"], self.d["w_out_even_b"], self.d["x_all"], False)),
                 ("l1_past", self.l1_past), ("l1_a", self.l1_a), ("l1_attn", self.l1_attn),
                 ("l1_ffn", lambda: self.mix_ffn(1, self.d["oT_s"], self.d["w_out_odd_b"], self.d["x_s"], True))]
        for name, fn in steps:
            fn()
            if d_stop == name:
                break
        self.P.emit()
        self.P.close()


def nc_free_bytes(nc):
    return int(nc.sbuf_bytes_remaining)


IN_KEYS_W = ["norm_mix", "norm_ffn", "norm_final", "w_in_even", "w_out_even", "s5_a_re", "s5_a_im", "s5_b_re", "s5_b_im",
             "s5_c_re", "s5_c_im", "s5_d", "s5_log_dt", "s5_w_glu", "s5_b_glu", "diff_lambda_q1", "diff_lambda_k1",
             "diff_lambda_q2", "diff_lambda_k2", "diff_subln", "w_in_odd", "mla_q_norm", "mla_kv_norm", "mla_w_uq",
             "mla_w_ukv", "w_out_odd", "ffn_w_in", "ffn_conv_w", "ffn_conv_b", "ffn_w_down"]


def core_inputs(inputs, cfg, pb, sbs, wshapes):
    f = lambda a: np.ascontiguousarray(np.asarray(a, dtype=np.float32))
    m = {}
    m["x_all"] = f(np.concatenate([inputs["x_prompt"][pb]] + [inputs["x_sample"][i] for i in sbs], axis=0))
    m["s5re0"] = f(np.stack([inputs["state_s5_re"][0, i] for i in sbs]))
    m["s5im0"] = f(np.stack([inputs["state_s5_im"][0, i] for i in sbs]))
    m["ck"] = f(np.stack([inputs["cache_diff_k"][0, i].reshape(cfg.PAST, 1024) for i in sbs]))
    m["cv"] = f(np.stack([inputs["cache_diff_v"][0, i].reshape(cfg.PAST, 1024) for i in sbs]))
    m["cckv"] = f(np.stack([inputs["cache_mla_ckv"][0, i] for i in sbs]))
    m["ckpe"] = f(np.stack([inputs["cache_mla_kpe"][0, i] for i in sbs]))
    m["convst"] = f(np.stack([np.stack([inputs["state_ffn_conv"][l, i] for i in sbs]) for l in range(2)]))
    for k in IN_KEYS_W:
        m[k] = f(np.asarray(inputs[k]).reshape(wshapes[k]))
    return m


_CACHE = {}


def get_program(cfg_key, debug=False, stop_after=None):
    key = (cfg_key, debug, stop_after)
    if key not in _CACHE:
        cfg = Cfg(*cfg_key)
        nc = bass.Bass("TRN2", target_bir_lowering=False)
        b = Builder(nc, cfg, debug=debug, stop_after=stop_after)
        b.build()
        _CACHE[key] = (nc, cfg, b)
    return _CACHE[key]


def kernel(**inputs):
    B, SEQ = inputs["x_prompt"].shape[0], inputs["x_prompt"].shape[1]
    DB, DS = inputs["x_sample"].shape[0], inputs["x_sample"].shape[1]
    PAST = inputs["cache_diff_k"].shape[2]
    ncores = 8
    ns = DB // ncores
    nc, cfg, b = get_program((SEQ, PAST, ns, DS))
    in_maps = []
    for c in range(ncores):
        in_maps.append(core_inputs(inputs, cfg, c % B, [c * ns + i for i in range(ns)], b.wshapes))
    res = run_bass_kernel_spmd(nc, in_maps, core_ids=list(range(ncores)))
    R = res.results
    P_LEN = cfg.P_LEN
    f32 = np.float32

    def prompt(name, shape_tail, sl=slice(0, P_LEN)):
        return np.stack([np.asarray(R[bb][name][sl]).reshape(shape_tail) for bb in range(B)]).astype(f32)

    def sample(name, shape_tail):
        out = []
        for c in range(ncores):
            for i in range(ns):
                t0 = P_LEN + i * DS
                out.append(np.asarray(R[c][name][t0:t0 + DS]).reshape(shape_tail))
        return np.stack(out).astype(f32)

    y_prompt = prompt("y_all", (P_LEN, D))
    y_sample = sample("y_all", (DS, D))
    s5rp = np.stack([R[bb]["s5re_o"][0] for bb in range(B)])[None].astype(f32)
    s5ip = np.stack([R[bb]["s5im_o"][0] for bb in range(B)])[None].astype(f32)
    s5rs = np.stack([R[c]["s5re_o"][1 + i] for c in range(ncores) for i in range(ns)])[None].astype(f32)
    s5is = np.stack([R[c]["s5im_o"][1 + i] for c in range(ncores) for i in range(ns)])[None].astype(f32)
    dkp = prompt("dk_o", (P_LEN, 8, 128))[None]
    dvp = prompt("dv_o", (P_LEN, 8, 128))[None]
    dks = sample("dk_o", (DS, 8, 128))[None]
    dvs = sample("dv_o", (DS, 8, 128))[None]
    ckp = prompt("ckv_o", (P_LEN, 512))[None]
    kpp = prompt("kpe_o", (P_LEN, 64))[None]
    cks = sample("ckv_o", (DS, 512))[None]
    kps = sample("kpe_o", (DS, 64))[None]
    cvp = np.stack([np.stack([R[bb]["conv_o"][l, 0] for bb in range(B)]) for l in range(2)]).astype(f32)
    cvs = np.stack([np.stack([R[c]["conv_o"][l, 1 + i] for c in range(ncores) for i in range(ns)]) for l in range(2)]).astype(f32)
    return (y_prompt, y_sample, s5rp, s5ip, s5rs, s5is, dkp, dvp, dks, dvs, ckp, kpp, cks, kps, cvp, cvs)
```

```python
import math
import numpy as np
import concourse.bass as bass
import concourse.mybir as mybir
from concourse.bass_utils import run_bass_kernel_spmd

F32 = mybir.dt.float32
BF16 = mybir.dt.bfloat16
I32 = mybir.dt.int32
ALU = mybir.AluOpType
AF = mybir.ActivationFunctionType
AX = mybir.AxisListType

ENGS = ("pe", "act", "dve", "pool", "sp")
RECYCLE_SEMS = True

D = 2048
DFF = 5632
NJ = DFF // 128
EPS = 1e-6
TWO_PI = 2.0 * math.pi


class T:
    __slots__ = ("name", "t", "ws", "rs", "sem", "cnt", "transient", "xw", "sem_sw", "cnt_sw")

    def __init__(self, name, t=None, transient=False):
        self.name = name
        self.t = t
        self.ws = {}
        self.rs = {}
        self.sem = None
        self.cnt = 0
        self.transient = transient
        self.xw = None
        self.sem_sw = None
        self.cnt_sw = 0

    def __getitem__(self, idx):
        return self.t[idx]


class _Frozen:
    __slots__ = ("sem",)

    def __init__(self, sem):
        self.sem = sem


class Op:
    __slots__ = ("eng", "fn", "deps", "signal", "tick", "dma", "sem", "semval", "sem_t", "key")

    def __init__(self, eng, fn, dma):
        self.eng = eng
        self.fn = fn
        self.deps = None
        self.signal = False
        self.tick = 0
        self.dma = dma
        self.sem = None
        self.semval = 0
        self.sem_t = None
        self.key = eng


class Prog:
    def __init__(self, nc):
        self.nc = nc
        self.ops = {e: [] for e in ENGS}
        self.stack = []
        self.engsem = {}
        self.last = {}
        self.semtiles = []
        self.pending_bar = {}
        self.nops = 0
        self.free_sems = []
        self.swtiles = []

    def sb(self, name, shape, dtype):
        cm = self.nc.sbuf_tensor(name, list(shape), dtype)
        h = cm.__enter__()
        self.stack.append(cm)
        return T(name, h)

    def ps(self, name, shape, dtype):
        cm = self.nc.psum_tensor(name, list(shape), dtype)
        h = cm.__enter__()
        self.stack.append(cm)
        return T(name, h)

    def new_sem(self, name):
        cm = self.nc.semaphore(name)
        h = cm.__enter__()
        self.stack.append(cm)
        return h

    def dram(self, name, shape, dtype, kind="Internal"):
        h = self.nc.dram_tensor(name, list(shape), dtype, kind=kind)
        return T(name, h.ap())

    def add(self, eng, fn, reads=(), writes=(), pwrites=(), dma=False, semtile=None):
        op = Op(eng, fn, dma)
        deps = []
        for t in reads:
            deps.extend(t.ws.values())
        for t in writes:
            deps.extend(t.ws.values())
            deps.extend(t.rs.values())
        for t in pwrites:
            deps.extend(t.rs.values())
            if t.xw is not None:
                deps.append(t.xw)
        res = []
        seen = set()
        for p in deps:
            if p is op or id(p) in seen:
                continue
            seen.add(id(p))
            if p.dma:
                if p.key.startswith("dmasw"):
                    res.append((p.sem, p.sem_t.cnt_sw))
                elif p.sem_t.sem is p.sem:
                    res.append((p.sem, p.sem_t.cnt))
                else:
                    res.append((p.sem, p.semval))
            else:
                if p.eng == eng and eng == "pe":
                    continue
                p.signal = True
                res.append(p)
        if eng in self.pending_bar:
            for p in self.pending_bar.pop(eng):
                if isinstance(p, tuple):
                    res.append(p)
                elif not (p.eng == eng and eng == "pe"):
                    p.signal = True
                    res.append(p)
        op.deps = res
        if dma and eng == "pool":
            st = semtile
            if st.sem_sw is None:
                st.sem_sw = self.new_sem("w_" + st.name)
                self.swtiles.append(st)
            st.cnt_sw += 16
            op.sem = st.sem_sw
            op.sem_t = st
            op.semval = st.cnt_sw
            op.key = "dmasw%d" % id(st)
        elif dma:
            st = semtile
            if st.sem is None:
                if self.free_sems and RECYCLE_SEMS:
                    st.sem, st.cnt = self.free_sems.pop()
                else:
                    st.sem = self.new_sem("s_" + st.name)
                self.semtiles.append(st)
            st.cnt += 16
            op.sem = st.sem
            op.sem_t = st
            op.semval = st.cnt
            op.key = "dma%d" % id(st)
        else:
            self.last[eng] = op
        k = op.key
        for t in reads:
            t.rs[k] = op
        for t in writes:
            t.ws[k] = op
            t.xw = op
        for t in pwrites:
            t.ws[k] = op
        self.ops[eng].append(op)
        self.nops += 1
        return op

    def barrier(self):
        deps = list(self.last.values()) + [(st.sem, st.cnt) for st in self.semtiles] + [(st.sem_sw, st.cnt_sw) for st in self.swtiles]
        for e in ENGS:
            self.pending_bar[e] = self.pending_bar.get(e, []) + deps
        keep = []
        for st in self.semtiles:
            if st.transient:
                self.free_sems.append((st.sem, st.cnt))
                st.sem = None
                st.cnt = 0
            else:
                keep.append(st)
        self.semtiles = keep

    def emit(self):
        nc = self.nc
        for e in ENGS:
            if e != "sp":
                self.engsem[e] = self.new_sem("eng_" + e)
        for e in ENGS:
            c = 0
            for op in self.ops[e]:
                if op.signal:
                    c += 1
                    op.tick = c
        prog = self

        def run(e, engobj):
            waited = {}
            for op in prog.ops[e]:
                for d in op.deps:
                    if isinstance(d, tuple):
                        sem, val = d
                    else:
                        sem, val = prog.engsem[d.eng], d.tick
                    key = id(sem)
                    if waited.get(key, 0) >= val:
                        continue
                    waited[key] = val
                    engobj.wait_ge(sem, val)
                ins = op.fn(engobj)
                if op.dma:
                    ins.then_inc(op.sem, 16)
                elif op.signal:
                    ins.then_inc(prog.engsem[e], 1)
            done = {}
            for op in prog.ops[e]:
                if op.dma:
                    done[id(op.sem)] = (op.sem, max(op.semval, done.get(id(op.sem), (None, 0))[1]))
            for sem, val in done.values():
                engobj.wait_ge(sem, val)

        with nc.Block() as block:
            @block.sync
            def _(eng):
                run("sp", eng)

            @block.scalar
            def _(eng):
                run("act", eng)

            @block.vector
            def _(eng):
                run("dve", eng)

            @block.gpsimd
            def _(eng):
                run("pool", eng)

            @block.tensor
            def _(eng):
                run("pe", eng)

    def close(self):
        while self.stack:
            self.stack.pop().__exit__(None, None, None)


class Ring:
    def __init__(self, tiles):
        self.tiles = tiles
        self.i = 0

    def next(self):
        t = self.tiles[self.i % len(self.tiles)]
        self.i += 1
        return t


class Cfg:
    def __init__(self, p_len=4096, past=2048, ns=2, sl=64):
        self.P_LEN = p_len
        self.PAST = past
        self.NS = ns
        self.SL = sl
        self.NTOK = p_len + ns * sl
        self.KSEQ = past + sl
        self.NKEY = p_len + ns * self.KSEQ
        self.macros = []
        for m in range(p_len // 512):
            self.macros.append((m * 512, 512, [(0, 512, 0)]))
        assert ns * sl == 128
        self.macros.append((p_len, 128, [(i * sl, sl, 1 + i) for i in range(ns)]))
        self.NSEQ = 1 + ns

    def koff(self, i):
        return self.P_LEN + i * self.KSEQ


class Builder:
    def __init__(self, nc, cfg, debug=False, stop_after=None):
        self.nc = nc
        self.cfg = cfg
        self.debug = debug
        self.stop_after = stop_after
        self.P = Prog(nc)
        self.uid = 0

    def add(self, *a, **k):
        return self.P.add(*a, **k)

    def nm(self, s):
        self.uid += 1
        return "%s_%d" % (s, self.uid)

    def alloc(self, name, shape, dtype):
        n = int(np.prod(shape[1:]))
        nb = n * (2 if dtype == BF16 else 4)
        nw = (nb + 3) // 4
        assert self.aoff + nw <= self.arena_words, ("arena overflow", name, self.aoff, nw)
        ap = self.arena.t[0:shape[0], self.aoff:self.aoff + nw]
        self.aoff += nw
        if dtype != F32:
            ap = ap.bitcast(dtype)
        if len(shape) == 3:
            ap = ap.rearrange("p (a b) -> p a b", b=shape[2])
        elif len(shape) == 4:
            ap = ap.rearrange("p (a b c) -> p a b c", b=shape[2], c=shape[3])
        return T(self.nm(name), ap, transient=True)

    def phase(self):
        self.P.barrier()
        self.aoff = 0
        self.psr = Ring(self.psum)

    def dma(self, eng, out, in_, reads=(), writes=(), pwrites=(), semtile=None, slow=False):
        if slow:
            fn = lambda e, o=out, i=in_: e.dma_start(out=o, in_=i, allow_slow_non_contiguous=True)
        else:
            fn = lambda e, o=out, i=in_: e.dma_start(out=o, in_=i)
        return self.add(eng, fn, reads=reads, writes=writes, pwrites=pwrites, dma=True, semtile=semtile)

    def load(self, dst_t, dst_ap, src_t, src_ap, slow=False, part=False):
        return self.dma("sp", dst_ap, src_ap, reads=[src_t], writes=([] if part else [dst_t]),
                        pwrites=([dst_t] if part else []), semtile=dst_t, slow=slow)

    def store(self, dst_t, dst_ap, src_t, src_ap, slow=False, eng="pool"):
        return self.dma(eng, dst_ap, src_ap, reads=[src_t], pwrites=[dst_t], semtile=src_t, slow=slow)

    def mm(self, ps_t, ps_ap, lhsT, rhs, start, stop, reads, excl=None):
        fn = lambda e, o=ps_ap, l=lhsT, r=rhs, s=start, p=stop: e.matmul(o, lhsT=l, rhs=r, start=s, stop=p)
        if excl is None:
            excl = start
        if excl:
            return self.add("pe", fn, reads=reads, writes=[ps_t])
        return self.add("pe", fn, reads=reads, pwrites=[ps_t])

    def tr(self, ps_t, ps_ap, in_ap, ident_ap, reads, first=True):
        fn = lambda e, o=ps_ap, i=in_ap, d=ident_ap: e.transpose(o, i, d)
        if first:
            return self.add("pe", fn, reads=reads, writes=[ps_t])
        return self.add("pe", fn, reads=reads, pwrites=[ps_t])

    def act(self, out_ap, in_ap, func, reads, writes=(), pwrites=(), bias=None, scale=None, accum=None):
        kw = {}
        if bias is not None:
            kw["bias"] = bias
        if scale is not None:
            kw["scale"] = scale
        if accum is not None:
            kw["accum_out"] = accum
        return self.add("act", lambda e, o=out_ap, i=in_ap, f=func, kw=kw: e.activation(out=o, in_=i, func=f, **kw),
                        reads=reads, writes=writes, pwrites=pwrites)

    def ts(self, eng, out_ap, in_ap, s1, s2, op0, op1, reads, writes=(), pwrites=()):
        if op1 is None:
            fn = lambda e, o=out_ap, i=in_ap, a=s1, p0=op0: e.tensor_scalar(o, i, a, None, p0)
        else:
            fn = lambda e, o=out_ap, i=in_ap, a=s1, b=s2, p0=op0, p1=op1: e.tensor_scalar(o, i, a, b, p0, p1)
        return self.add(eng, fn, reads=reads, writes=writes, pwrites=pwrites)

    def tt(self, eng, out_ap, a_ap, b_ap, op, reads, writes=(), pwrites=()):
        return self.add(eng, lambda e, o=out_ap, a=a_ap, b=b_ap, p=op: e.tensor_tensor(out=o, in0=a, in1=b, op=p),
                        reads=reads, writes=writes, pwrites=pwrites)

    def stt(self, eng, out_ap, in0, scalar, in1, op0, op1, reads, writes=(), pwrites=()):
        return self.add(eng, lambda e, o=out_ap, a=in0, s=scalar, b=in1, p0=op0, p1=op1:
                        e.scalar_tensor_tensor(out=o, in0=a, scalar=s, in1=b, op0=p0, op1=p1),
                        reads=reads, writes=writes, pwrites=pwrites)

    def cp(self, eng, out_ap, in_ap, reads, writes=(), pwrites=()):
        if eng == "act":
            return self.add("act", lambda e, o=out_ap, i=in_ap: e.copy(out=o, in_=i), reads=reads, writes=writes, pwrites=pwrites)
        return self.add(eng, lambda e, o=out_ap, i=in_ap: e.tensor_copy(out=o, in_=i), reads=reads, writes=writes, pwrites=pwrites)

    def memset(self, eng, t, ap, val, part=False):
        return self.add(eng, lambda e, o=ap, v=val: e.memset(o, v), writes=([] if part else [t]), pwrites=([t] if part else []))

    def declare(self):
        P, c = self.P, self.cfg
        IN, OUT = "ExternalInput", "ExternalOutput"
        SCR = OUT if self.debug else "Internal"
        d = {}
        d["x_all"] = P.dram("x_all", [c.NTOK, D], F32, IN)
        d["s5re0"] = P.dram("s5re0", [c.NS, 64, 64], F32, IN)
        d["s5im0"] = P.dram("s5im0", [c.NS, 64, 64], F32, IN)
        d["ck"] = P.dram("ck", [c.NS, c.PAST, 1024], F32, IN)
        d["cv"] = P.dram("cv", [c.NS, c.PAST, 1024], F32, IN)
        d["cckv"] = P.dram("cckv", [c.NS, c.PAST, 512], F32, IN)
        d["ckpe"] = P.dram("ckpe", [c.NS, c.PAST, 64], F32, IN)
        d["convst"] = P.dram("convst", [2, c.NS, 2, DFF], F32, IN)
        wshapes = {
            "norm_mix": [2, D], "norm_ffn": [2, D], "norm_final": [D],
            "w_in_even": [D, 4096], "w_out_even": [D, D],
            "s5_a_re": [64, 64], "s5_a_im": [64, 64], "s5_b_re": [64, 64, 16], "s5_b_im": [64, 64, 16],
            "s5_c_re": [64, 16, 64], "s5_c_im": [64, 16, 64], "s5_d": [1024], "s5_log_dt": [64],
            "s5_w_glu": [1024, 1024], "s5_b_glu": [1024],
            "diff_lambda_q1": [64], "diff_lambda_k1": [64], "diff_lambda_q2": [64], "diff_lambda_k2": [64],
            "diff_subln": [128],
            "w_in_odd": [D, 1088], "mla_q_norm": [512], "mla_kv_norm": [512],
            "mla_w_uq": [512, 3072], "mla_w_ukv": [512, 4096], "w_out_odd": [D, D],
            "ffn_w_in": [2, D, 2 * DFF], "ffn_conv_w": [2, 3, DFF], "ffn_conv_b": [2, DFF],
            "ffn_w_down": [2, DFF, D],
        }
        self.wshapes = wshapes
        for k, s in wshapes.items():
            d[k] = P.dram(k, s, F32, IN)
        d["y_all"] = P.dram("y_all", [c.NTOK, D], F32, OUT)
        d["s5re_o"] = P.dram("s5re_o", [c.NSEQ, 64, 64], F32, OUT)
        d["s5im_o"] = P.dram("s5im_o", [c.NSEQ, 64, 64], F32, OUT)
        d["dk_o"] = P.dram("dk_o", [c.NTOK, 1024], F32, OUT)
        d["dv_o"] = P.dram("dv_o", [c.NTOK, 1024], F32, OUT)
        d["ckv_o"] = P.dram("ckv_o", [c.NTOK, 512], F32, OUT)
        d["kpe_o"] = P.dram("kpe_o", [c.NTOK, 64], F32, OUT)
        d["conv_o"] = P.dram("conv_o", [2, c.NSEQ, 2, DFF], F32, OUT)
        for k in ("w_in_even", "w_out_even", "s5_w_glu", "w_in_odd", "mla_w_uq", "mla_w_ukv", "w_out_odd"):
            d[k + "_b"] = P.dram(k + "_b", wshapes[k], BF16, "Internal")
        for l in range(2):
            d["ffn_w_in_b%d" % l] = P.dram("ffn_w_in_b%d" % l, [D, 2 * DFF], BF16, "Internal")
            d["ffn_w_down_b%d" % l] = P.dram("ffn_w_down_b%d" % l, [DFF, D], BF16, "Internal")
        d["uT_s"] = P.dram("uT_s", [1024, c.NTOK], BF16, SCR)
        d["qT_s"] = P.dram("qT_s", [1024, c.NTOK], BF16, SCR)
        d["kT_s"] = P.dram("kT_s", [1024, c.NKEY], BF16, SCR)
        d["v_s"] = P.dram("v_s", [c.NKEY, 1024], BF16, SCR)
        d["mixT_s"] = P.dram("mixT_s", [D, c.NTOK], BF16, SCR)
        d["x_s"] = P.dram("x_s", [c.NTOK, D], F32, SCR)
        d["qnT_s"] = P.dram("qnT_s", [D, c.NTOK], BF16, SCR)
        d["qpeT_s"] = P.dram("qpeT_s", [1024, c.NTOK], BF16, SCR)
        d["knT_s"] = P.dram("knT_s", [D, c.NKEY], BF16, SCR)
        d["kpeT_s"] = P.dram("kpeT_s", [64, c.NKEY], BF16, SCR)
        d["v1_s"] = P.dram("v1_s", [c.NKEY, D], BF16, SCR)
        d["oT_s"] = P.dram("oT_s", [D, c.NTOK], BF16, SCR)
        self.d = d

    def setup_consts(self):
        P, c, d = self.P, self.cfg, self.d
        self.psum = [P.ps("ps%d" % i, [128, 512], F32) for i in range(8)]
        self.psr = Ring(self.psum)
        sb = P.sb
        self.identf = sb("identf", [128, 128], F32)
        self.identb = sb("identb", [128, 128], BF16)
        self.onesf = sb("onesf", [128, 128], F32)
        self.onesb = sb("onesb", [128, 128], BF16)
        self.eps_t = sb("eps_t", [128, 1], F32)
        self.mask = sb("maskd", [128, 512], BF16)
        self.memset("pool", self.identf, self.identf[:], 1.0)
        self.add("pool", lambda e: e.affine_select(out=self.identf[:], in_=self.identf[:], pattern=[[1, 128]],
                                                   compare_op=ALU.is_equal, fill=0.0, base=0, channel_multiplier=-1),
                 reads=[self.identf], writes=[self.identf])
        self.cp("dve", self.identb[:], self.identf[:], [self.identf], [self.identb])
        self.memset("dve", self.onesf, self.onesf[:], 1.0)
        self.memset("dve", self.onesb, self.onesb[:], 1.0)
        self.memset("dve", self.eps_t, self.eps_t[:], EPS)
        self.memset("dve", self.mask, self.mask[:], 1.0)
        self.memset("dve", self.mask, self.mask[64:128, 0:64], 0.0, part=True)

    def load_featmajor(self, dst_t, dst_ap3, src_t, src_rows_ap, R, F, stage_t):
        nch = F // 128
        self.load(stage_t, stage_t[0:R, 0:F], src_t, src_rows_ap)
        per = 512 // R
        c0 = 0
        while c0 < nch:
            n = min(per, nch - c0)
            ps = self.psr.next()
            for i in range(n):
                cc = c0 + i
                self.tr(ps, ps[:, i * R:(i + 1) * R], stage_t[0:R, cc * 128:(cc + 1) * 128], self.identf[0:R, 0:R],
                        [stage_t, self.identf], first=(i == 0))
            self.cp("dve", dst_ap3[:, c0:c0 + n, :], ps[:, 0:n * R].rearrange("p (c r) -> p c r", r=R), [ps], pwrites=[dst_t])
            c0 += n

    def setup_params(self):
        P, c, d = self.P, self.cfg, self.d
        sb = P.sb
        nsub = c.NTOK // 128
        self.nsub = nsub
        self.gmix = sb("gmix", [128, 2, 16, 1], F32)
        self.gffn = sb("gffn", [128, 2, 16, 1], F32)
        self.gfin = sb("gfin", [128, D], F32)
        self.s5d = sb("s5d", [128, 8, 1], F32)
        self.bglu = sb("bglu", [128, 8, 1], F32)
        self.qng = sb("qng", [128, 4, 1], F32)
        self.kvng = sb("kvng", [128, 4, 1], F32)
        self.kvgb = sb("kvgb", [128, 512], F32)
        self.subg = sb("subg", [128, 1, 1], F32)
        self.convw = sb("convw", [128, 2, NJ, 3], F32)
        self.convb = sb("convb", [128, 2, NJ, 1], F32)
        self.halo = sb("halo", [128, 2, c.NSEQ, NJ, 2], F32)
        self.neglam = sb("neglam", [128, 1], F32)
        self.ropes = sb("ropesin", [128, nsub, 32], F32)
        self.ropec = sb("ropecos", [128, nsub, 32], F32)
        self.arena_words = (nc_free_bytes(self.nc) - 1024) // 4
        self.arena = sb("arena", [128, self.arena_words], F32)
        self.aoff = 0
        self.phase()
        A = self.alloc
        stage = A("pstage", [4, DFF], F32)
        for l in range(2):
            self.load_featmajor(self.gmix, self.gmix[:, l], d["norm_mix"], d["norm_mix"].t[l:l + 1, :], 1, D, stage)
            self.load_featmajor(self.gffn, self.gffn[:, l], d["norm_ffn"], d["norm_ffn"].t[l:l + 1, :], 1, D, stage)
        self.load(self.gfin, self.gfin[:], d["norm_final"], d["norm_final"].t.partition_broadcast(128))
        self.load_featmajor(self.s5d, self.s5d[:], d["s5_d"], d["s5_d"].t.unsqueeze(0), 1, 1024, stage)
        self.load_featmajor(self.bglu, self.bglu[:], d["s5_b_glu"], d["s5_b_glu"].t.unsqueeze(0), 1, 1024, stage)
        self.load_featmajor(self.qng, self.qng[:], d["mla_q_norm"], d["mla_q_norm"].t.unsqueeze(0), 1, 512, stage)
        self.load_featmajor(self.kvng, self.kvng[:], d["mla_kv_norm"], d["mla_kv_norm"].t.unsqueeze(0), 1, 512, stage)
        self.load(self.kvgb, self.kvgb[:], d["mla_kv_norm"], d["mla_kv_norm"].t.partition_broadcast(128))
        self.load_featmajor(self.subg, self.subg[:], d["diff_subln"], d["diff_subln"].t.unsqueeze(0), 1, 128, stage)
        self.memset("pool", self.halo, self.halo[:], 0.0)
        for l in range(2):
            self.load_featmajor(self.convw, self.convw[:, l], d["ffn_conv_w"], d["ffn_conv_w"].t[l], 3, DFF, stage)
            self.load_featmajor(self.convb, self.convb[:, l], d["ffn_conv_b"], d["ffn_conv_b"].t[l:l + 1, :], 1, DFF, stage)
            for i in range(c.NS):
                self.load_featmajor(self.halo, self.halo[:, l, 1 + i], d["convst"], d["convst"].t[l, i], 2, DFF, stage)
        lam_init = 0.8 - 0.6 * math.exp(-0.3 * 0)
        lq = A("lamtmp", [128, 4, 64], F32)
        for i, k in enumerate(("diff_lambda_q1", "diff_lambda_k1", "diff_lambda_q2", "diff_lambda_k2")):
            self.load(lq, lq[:, i, :], d[k], d[k].t.partition_broadcast(128), part=True)
        lsum = A("lamsum", [128, 2], F32)
        lprod = A("lamprod", [128, 2, 64], F32)
        self.tt("dve", lprod[:, 0, :], lq[:, 0, :], lq[:, 1, :], ALU.mult, [lq], pwrites=[lprod])
        self.tt("dve", lprod[:, 1, :], lq[:, 2, :], lq[:, 3, :], ALU.mult, [lq], pwrites=[lprod])
        self.add("dve", lambda e: e.reduce_sum(out=lsum[:], in_=lprod[:], axis=AX.X), reads=[lprod], writes=[lsum])
        self.act(lsum[:], lsum[:], AF.Exp, [lsum], [lsum])
        self.tt("dve", self.neglam[:], lsum[:, 1:2], lsum[:, 0:1], ALU.subtract, [lsum], [self.neglam])
        self.ts("dve", self.neglam[:], self.neglam[:], -lam_init, None, ALU.add, None, [self.neglam], [self.neglam])
        self.ts("dve", self.subg[:, 0, :], self.subg[:, 0, :], 1.0 - lam_init, None, ALU.mult, None, [self.subg], [self.subg])
        pos = A("ropepos", [128, nsub], F32)
        npr = c.P_LEN // 128
        self.add("pool", lambda e: e.iota(pos[:, 0:npr], pattern=[[128, npr]], base=0, channel_multiplier=1,
                                          allow_small_or_imprecise_dtypes=True), writes=[pos])
        self.add("pool", lambda e: e.iota(pos[0:64, npr:npr + 1], pattern=[[0, 1]], base=c.PAST, channel_multiplier=1,
                                          allow_small_or_imprecise_dtypes=True), pwrites=[pos])
        self.add("pool", lambda e: e.iota(pos[64:128, npr:npr + 1], pattern=[[0, 1]], base=c.PAST, channel_multiplier=1,
                                          allow_small_or_imprecise_dtypes=True), pwrites=[pos])
        invf = A("ropeinv", [128, 32], F32)
        self.add("pool", lambda e: e.iota(invf[:], pattern=[[1, 32]], base=0, channel_multiplier=0,
                                          allow_small_or_imprecise_dtypes=True), writes=[invf])
        self.act(invf[:], invf[:], AF.Exp, [invf], [invf], scale=-math.log(10000.0) / 32.0)
        ang = A("ropeang", [128, nsub, 32], F32)
        self.tt("dve", ang[:], pos[:, :].unsqueeze(2).to_broadcast([128, nsub, 32]),
                invf[:, :].unsqueeze(1).to_broadcast([128, nsub, 32]), ALU.mult, [pos, invf], [ang])
        t1 = A("ropet1", [128, nsub * 32], F32)
        t2 = A("ropet2", [128, nsub * 32], I32)
        angf = ang[:, :, :].rearrange("p a b -> p (a b)")
        self.sin_of(self.ropes, self.ropes[:, :, :].rearrange("p a b -> p (a b)"), ang, angf, 0.0, t1, t2)
        self.sin_of(self.ropec, self.ropec[:, :, :].rearrange("p a b -> p (a b)"), ang, angf, math.pi / 2, t1, t2)

    def sin_of(self, out_t, out_ap, arg_t, arg_ap, shift, t1, t2):
        n = out_ap.shape[-1]
        a1, a2 = t1[:, 0:n], t2[:, 0:n]
        self.ts("dve", a1, arg_ap, 1.0 / TWO_PI, shift / TWO_PI, ALU.mult, ALU.add, [arg_t], [t1])
        self.cp("dve", a2, a1, [t1], [t2])
        self.cp("dve", a1, a2, [t2], [t1])
        self.stt("dve", a1, a1, -TWO_PI, arg_ap, ALU.mult, ALU.add, [t1, arg_t], [t1])
        self.ts("dve", a1, a1, shift, None, ALU.add, None, [t1], [t1])
        self.ts("dve", a1, a1, -math.pi, math.pi, ALU.max, ALU.min, [t1], [t1])
        self.act(out_ap, a1, AF.Sin, [t1], [out_t])

    def conv_jobs(self, keys, CB):
        d = self.d
        jobs = []
        specs = []
        for k in keys:
            if k.startswith("ffn_w_in"):
                l = int(k[-1]); specs.append((d["ffn_w_in"], d["ffn_w_in"].t[l], d["ffn_w_in_b%d" % l], [D, 2 * DFF]))
            elif k.startswith("ffn_w_down"):
                l = int(k[-1]); specs.append((d["ffn_w_down"], d["ffn_w_down"].t[l], d["ffn_w_down_b%d" % l], [DFF, D]))
            else:
                specs.append((d[k], d[k].t, d[k + "_b"], self.wshapes[k]))
        for src_t, src_ap, dst_t, (K, N) in specs:
            for r in range(K // 128):
                for c0 in range(0, N, CB):
                    jobs.append((src_t, src_ap, dst_t, r, c0, min(CB, N - c0)))
        return jobs

    def conv_emit(self, job, s, b, cast_eng, store_eng):
        src_t, src_ap, dst_t, r, c0, n = job
        self.load(s, s[:, 0:n], src_t, src_ap[r * 128:(r + 1) * 128, c0:c0 + n])
        self.cp(cast_eng, b[:, 0:n], s[:, 0:n], [s], [b])
        self.store(dst_t, dst_t.t[r * 128:(r + 1) * 128, c0:c0 + n], b, b[:, 0:n], eng=store_eng)

    def prologue(self):
        self.phase()
        CB = 2048
        st = Ring([self.alloc("wst", [128, CB], F32) for _ in range(4)])
        bt = Ring([self.alloc("wbt", [128, CB], BF16) for _ in range(4)])
        for i, job in enumerate(self.conv_jobs(["w_in_even", "s5_w_glu"], CB)):
            self.conv_emit(job, st.next(), bt.next(), ("dve", "act")[i % 2], "pool")
        self.bg_jobs = self.conv_jobs(["w_out_even", "ffn_w_in0", "ffn_w_down0", "w_in_odd", "mla_w_uq", "mla_w_ukv", "w_out_odd",
                                       "ffn_w_in1", "ffn_w_down1"], 256)

    def slab(self, wt, k0, kc, n0, n):
        s = self.slabs.next()
        v = s.t[:, 0:kc * n].rearrange("p (k n) -> p k n", n=n)
        self.load(s, v, wt, wt.t[k0 * 128:(k0 + kc) * 128, n0:n0 + n].rearrange("(k p) n -> p k n", p=128))
        return s, v

    def norm_T(self, xts, Dw, gain_ap3, hT, evac_scale=None):
        nch = Dw // 128
        S = len(xts)
        rstds, xns = [], []
        for s, xt in enumerate(xts):
            ss = self.ssr.next()
            xn = self.xnr.next()
            self.act(xn[:, 0:Dw], xt[:, 0:Dw], AF.Square, [xt], [xn, ss], accum=ss[:])
            self.act(ss[:], ss[:], AF.Sqrt, [ss, self.eps_t], [ss], bias=self.eps_t[:], scale=1.0 / Dw)
            self.add("dve", lambda e, o=ss[:]: e.reciprocal(o, o), reads=[ss], writes=[ss])
            if s % 2 == 0:
                self.act(xn[:, 0:Dw], xt[:, 0:Dw], AF.Copy, [xt, ss], [xn], scale=ss[:])
            else:
                self.ts("dve", xn[:, 0:Dw], xt[:, 0:Dw], ss[:], None, ALU.mult, None, [xt, ss], [xn])
            rstds.append(ss)
            xns.append(xn)
        for c in range(nch):
            ps = self.psr.next()
            pb = ps.t[:, :].bitcast(BF16)
            for s in range(S):
                self.tr(ps, pb[:, s * 128:(s + 1) * 128], xns[s][:, c * 128:(c + 1) * 128], self.identb[:],
                        [xns[s], self.identb], first=(s == 0))
            if gain_ap3 is None:
                self.cp("dve" if c % 2 else "act", hT[:, c, 0:S * 128], pb[:, 0:S * 128], [ps], pwrites=[hT])
            elif c % 2 == 0:
                self.ts("dve", hT[:, c, 0:S * 128], pb[:, 0:S * 128], gain_ap3[:, c, :], None, ALU.mult, None,
                        [ps], pwrites=[hT])
            else:
                self.act(hT[:, c, 0:S * 128], pb[:, 0:S * 128], AF.Copy, [ps], pwrites=[hT], scale=gain_ap3[:, c, :])
        return rstds

    def formA(self, sl_t, sl_v, oc, kc, hT, nt):
        ps = self.psr.next()
        for k in range(kc):
            self.mm(ps, ps[:, 0:nt], sl_v[:, k, oc * 128:(oc + 1) * 128], hT[:, k, 0:nt], k == 0, k == kc - 1, [sl_t, hT])
        return ps

    def formB(self, sl_t, rhs_fn, kc, hT, s, n):
        ps = self.psr.next()
        for k in range(kc):
            self.mm(ps, ps[:, 0:n], hT[:, k, s * 128:(s + 1) * 128], rhs_fn(k), k == 0, k == kc - 1, [sl_t, hT])
        return ps

    def l0_a(self):
        c, d = self.cfg, self.d
        self.phase()
        self.slabs = Ring([self.alloc("slab", [128, 16 * 512], BF16) for _ in range(3)])
        xr = Ring([self.alloc("x", [128, D], F32) for _ in range(5)])
        self.xnr = Ring([self.alloc("xn", [128, D], BF16) for _ in range(4)])
        self.ssr = Ring([self.alloc("ss", [128, 1], F32) for _ in range(8)])
        hTr = Ring([self.alloc("hT", [128, 16, 512], BF16) for _ in range(2)])
        stf = Ring([self.alloc("stf", [128, 4, 512], F32) for _ in range(2)])
        stb = Ring([self.alloc("stb", [128, 4, 512], BF16) for _ in range(3)])
        for (tok0, nt, segs) in c.macros:
            S = nt // 128
            xts = []
            for s in range(S):
                xt = xr.next()
                self.load(xt, xt[:], d["x_all"], d["x_all"].t[tok0 + s * 128: tok0 + (s + 1) * 128, :])
                xts.append(xt)
            hT = hTr.next()
            self.norm_T(xts, D, self.gmix[:, 0], hT)
            koffs = self.key_cols(tok0, nt, segs)
            for si in range(8):
                sl, sv = self.slab(d["w_in_even_b"], 0, 16, si * 512, 512)
                if si < 6:
                    so = stb.next()
                    for oc in range(4):
                        ps = self.formA(sl, sv, oc, 16, hT, nt)
                        if si < 2:
                            self.cp("act" if oc % 2 else "dve", so[:, oc, 0:nt], ps[:, 0:nt], [ps], pwrites=[so])
                        elif si < 4:
                            self.act(so[:, oc, 0:nt], ps[:, 0:nt], AF.Copy, [ps], pwrites=[so], scale=0.125)
                        else:
                            self.cp("act" if oc % 2 else "dve", so[:, oc, 0:nt], ps[:, 0:nt], [ps], pwrites=[so])
                    if si < 2:
                        self.store(d["uT_s"], d["uT_s"].t[si * 512:(si + 1) * 512, tok0:tok0 + nt].rearrange("(c p) t -> p c t", p=128),
                                   so, so[:, :, 0:nt])
                    elif si < 4:
                        r0 = (si - 2) * 512
                        self.store(d["qT_s"], d["qT_s"].t[r0:r0 + 512, tok0:tok0 + nt].rearrange("(c p) t -> p c t", p=128),
                                   so, so[:, :, 0:nt])
                    else:
                        r0 = (si - 4) * 512
                        for (col0, ln, k0) in koffs:
                            self.store(d["kT_s"], d["kT_s"].t[r0:r0 + 512, k0:k0 + ln].rearrange("(c p) t -> p c t", p=128),
                                       so, so[:, :, col0:col0 + ln])
                if si >= 4:
                    so = stf.next()
                    for s in range(S):
                        ps = self.formB(sl, lambda k, sv=sv: sv[:, k, :], 16, hT, s, 512)
                        self.cp("act" if s % 2 else "dve", so[:, s, :], ps[:, :], [ps], pwrites=[so])
                    dst = d["dk_o"] if si < 6 else d["dv_o"]
                    cb = (si - 4) % 2 * 512
                    self.store(dst, dst.t[tok0:tok0 + nt, cb:cb + 512].rearrange("(s p) n -> p s n", p=128), so, so[:, 0:S, :])
                    if si >= 6:
                        sb2 = stb.next()
                        self.cp("pool", sb2[:, 0:S, :], so[:, 0:S, :], [so], pwrites=[sb2])
                        for (col0, ln, k0) in koffs:
                            for s in range(S):
                                a, b = max(col0, s * 128), min(col0 + ln, (s + 1) * 128)
                                if a >= b:
                                    continue
                                self.store(d["v_s"], d["v_s"].t[k0 + a - col0:k0 + b - col0, cb:cb + 512],
                                           sb2, sb2[a - s * 128:b - s * 128, s, :])

    def key_cols(self, tok0, nt, segs):
        c = self.cfg
        out = []
        for (col0, ln, seq) in segs:
            if seq == 0:
                out.append((col0, ln, tok0 + col0))
            else:
                out.append((col0, ln, c.koff(seq - 1) + c.PAST))
        return out

    def l0_past(self):
        c, d = self.cfg, self.d
        self.phase()
        fr = Ring([self.alloc("pf", [128, 1024], F32) for _ in range(3)])
        br = Ring([self.alloc("pb", [128, 1024], BF16) for _ in range(3)])
        ktr = Ring([self.alloc("pkt", [128, 8, 512], BF16) for _ in range(2)])
        for i in range(c.NS):
            k0 = c.koff(i)
            for t0 in range(0, c.PAST, 512):
                nsub = min(4, (c.PAST - t0) // 128)
                kt = ktr.next()
                for s in range(nsub):
                    r0 = t0 + s * 128
                    f, b = fr.next(), br.next()
                    self.load(f, f[:], d["ck"], d["ck"].t[i, r0:r0 + 128, :])
                    self.cp("dve", b[:], f[:], [f], [b])
                    for hp in range(2):
                        ps = self.psr.next()
                        pb = ps.t[:, :].bitcast(BF16)
                        for hh in range(4):
                            h = hp * 4 + hh
                            self.tr(ps, pb[:, hh * 128:(hh + 1) * 128], b[:, h * 128:(h + 1) * 128], self.identb[:],
                                    [b, self.identb], first=(hh == 0))
                        self.cp("act", kt[:, hp * 4:(hp + 1) * 4, s * 128:(s + 1) * 128],
                                pb[:, 0:512].rearrange("p (h t) -> p h t", t=128), [ps], pwrites=[kt])
                    f2, b2 = fr.next(), br.next()
                    self.load(f2, f2[:], d["cv"], d["cv"].t[i, r0:r0 + 128, :])
                    self.cp("pool", b2[:], f2[:], [f2], [b2])
                    self.store(d["v_s"], d["v_s"].t[k0 + r0:k0 + r0 + 128, :], b2, b2[:])
                n = nsub * 128
                self.store(d["kT_s"], d["kT_s"].t[:, k0 + t0:k0 + t0 + n].rearrange("(h p) t -> p h t", p=128), kt, kt[:, :, 0:n])

    def attention(self, maps, vfn, nq, ktiles, fin, den_modes):
        M = len(maps)
        O = [self.psr.next() for _ in range(M)]
        Dpe = {m: self.psr.next() for m in range(M) if den_modes[m] == "pe"}
        held = O + list(Dpe.values())
        sring = Ring([b for b in self.psum if all(b is not a for a in held)])
        accs = [self.accr.next() for _ in range(M)]
        S = {}
        nt_ = len(ktiles)

        def emit_S(ti):
            kt, nk, q0, diag = ktiles[ti]
            n = nq - q0
            for m in range(M):
                ps = sring.next()
                ops = maps[m]
                for oi, (lf, rf, rd) in enumerate(ops):
                    self.mm(ps, ps[0:nk, q0:nq], lf(kt, nk), rf(q0, n), oi == 0, oi == len(ops) - 1, rd)
                S[(ti, m)] = ps

        LA = 3 if M == 1 else 1
        self.cur_sring = sring
        for t0_ in range(min(LA, nt_)):
            emit_S(t0_)
        if self.deferred is not None:
            fn_, self.deferred = self.deferred, None
            fn_()
        for ti, (kt, nk, q0, diag) in enumerate(ktiles):
            if ti + LA < nt_:
                emit_S(ti + LA)
            n = nq - q0
            for m in range(M):
                ps = S.pop((ti, m))
                e = self.er.next()
                self.act(e[0:nk, q0:nq], ps[0:nk, q0:nq], AF.Exp, [ps], [e])
                if diag:
                    self.tt("pool", e[0:nk, q0:nq], e[0:nk, q0:nq], self.mask[0:nk, 0:n], ALU.mult, [e, self.mask], [e])
                va, vr = vfn(kt, nk)
                self.mm(O[m], O[m][:, q0:nq], va, e[0:nk, q0:nq], ti == 0, ti == nt_ - 1, [e] + vr)
                eng = "dve"
                acc = accs[m]
                if den_modes[m] == "pe":
                    self.mm(Dpe[m], Dpe[m][:, q0:nq], self.onesb[0:nk, :], e[0:nk, q0:nq], ti == 0, ti == nt_ - 1, [e, self.onesb])
                elif ti == 0:
                    self.cp(eng, acc[:, 0:nq], e[:, 0:nq], [e], [acc])
                else:
                    self.tt(eng, acc[0:nk, q0:nq], acc[0:nk, q0:nq], e[0:nk, q0:nq], ALU.add, [acc, e], [acc])
        Dn = []
        for m in range(M):
            if den_modes[m] == "pe":
                Dn.append(Dpe[m])
                continue
            ps = sring.next()
            self.mm(ps, ps[:, 0:nq], self.onesf[:, :], accs[m][:, 0:nq], True, True, [accs[m], self.onesf])
            Dn.append(ps)
        self.deferred = fin(O, Dn)

    def l0_attn(self):
        c, d = self.cfg, self.d
        self.phase()
        seqs = [(0, c.P_LEN, 0, c.P_LEN)]
        for i in range(c.NS):
            seqs.append((c.koff(i), c.KSEQ, c.P_LEN + i * c.SL, c.SL))
        maxk = max(c.P_LEN, c.KSEQ)
        nkt_max = (maxk + 127) // 128
        KTr = Ring([self.alloc("KT", [128, maxk], BF16) for _ in range(2)])
        QTr = Ring([self.alloc("QT", [128, max(c.P_LEN, c.SL)], BF16) for _ in range(2)])
        Vr = Ring([self.alloc("V", [128, nkt_max, 128], BF16) for _ in range(2)])
        self.er = Ring([self.alloc("E", [128, 512], BF16) for _ in range(6)])
        self.accr = Ring([self.alloc("dacc", [128, 512], F32) for _ in range(4)])
        tmp = Ring([self.alloc("atmp", [128, 512], F32) for _ in range(12)])
        ost = Ring([self.alloc("aost", [128, 512], BF16) for _ in range(3)])
        self.deferred = None
        for (key0, nkeys, tok0, ntok) in seqs:
            nfull, rem = nkeys // 128, nkeys % 128
            for h in range(8):
                KT, QT, V = KTr.next(), QTr.next(), Vr.next()
                self.load(KT, KT[:, 0:nkeys], d["kT_s"], d["kT_s"].t[h * 128:(h + 1) * 128, key0:key0 + nkeys])
                self.load(QT, QT[:, 0:ntok], d["qT_s"], d["qT_s"].t[h * 128:(h + 1) * 128, tok0:tok0 + ntok])
                if nfull:
                    self.load(V, V[:, 0:nfull, :], d["v_s"],
                              d["v_s"].t[key0:key0 + nfull * 128, h * 128:(h + 1) * 128].rearrange("(t p) d -> p t d", p=128), part=True)
                if rem:
                    self.load(V, V[0:rem, nfull, :], d["v_s"], d["v_s"].t[key0 + nfull * 128:key0 + nkeys, h * 128:(h + 1) * 128], part=True)
                causal = (ntok == nkeys)
                for qm in range((ntok + 511) // 512):
                    nq = min(512, ntok - qm * 512)
                    qc0 = qm * 512
                    if causal:
                        ktiles = [(kt, 128, 0, False) for kt in range(4 * qm)] + \
                                 [(4 * qm + j, 128, 128 * j, True) for j in range(nq // 128)]
                    else:
                        ktiles = [(kt, 128, 0, False) for kt in range(nfull)] + ([(nfull, rem, 0, False)] if rem else [])
                    maps = []
                    for m in range(2):
                        pr = slice(m * 64, (m + 1) * 64)
                        maps.append([(lambda kt, nk, KT=KT, pr=pr: KT[pr, kt * 128:kt * 128 + nk],
                                      lambda q0, n, QT=QT, pr=pr, qc0=qc0: QT[pr, qc0 + q0:qc0 + q0 + n], [KT, QT])])
                    vfn = lambda kt, nk, V=V: (V[0:nk, kt, :], [V])

                    def fin(O, Dn, h=h, tok0=tok0, qc0=qc0, nq=nq):
                        r = [tmp.next(), tmp.next()]
                        for m in range(2):
                            self.act(r[m][:, 0:nq], Dn[m][:, 0:nq], AF.Ln, [Dn[m]], [r[m]])
                            self.act(r[m][:, 0:nq], r[m][:, 0:nq], AF.Exp, [r[m]], [r[m]], scale=-1.0)
                        o1, o2 = tmp.next(), tmp.next()
                        self.tt("dve", o1[:, 0:nq], O[0][:, 0:nq], r[0][:, 0:nq], ALU.mult, [O[0], r[0]], [o1])
                        self.tt("dve", o2[:, 0:nq], O[1][:, 0:nq], r[1][:, 0:nq], ALU.mult, [O[1], r[1]], [o2])
                        self.stt("dve", o1[:, 0:nq], o2[:, 0:nq], self.neglam[:, 0:1], o1[:, 0:nq], ALU.mult, ALU.add,
                                 [o1, o2, self.neglam], [o1])
                        sq = tmp.next()
                        self.tt("pool", sq[:, 0:nq], o1[:, 0:nq], o1[:, 0:nq], ALU.mult, [o1], [sq])

                        def fin_b():
                            ps = self.cur_sring.next()
                            self.mm(ps, ps[:, 0:nq], self.onesf[:, :], sq[:, 0:nq], True, True, [sq, self.onesf])
                            rs = tmp.next()
                            self.act(rs[:, 0:nq], ps[:, 0:nq], AF.Ln, [ps, self.eps_t], [rs], bias=self.eps_t[:], scale=1.0 / 128)
                            self.act(rs[:, 0:nq], rs[:, 0:nq], AF.Exp, [rs], [rs], scale=-0.5)
                            ob = ost.next()
                            self.stt("dve", ob[:, 0:nq], o1[:, 0:nq], self.subg[:, 0, :], rs[:, 0:nq], ALU.mult, ALU.mult,
                                     [o1, rs, self.subg], [ob])
                            self.store(d["mixT_s"], d["mixT_s"].t[1024 + h * 128:1024 + (h + 1) * 128, tok0 + qc0:tok0 + qc0 + nq],
                                       ob, ob[:, 0:nq])
                        return fin_b
                    self.attention(maps, vfn, nq, ktiles, fin, ["dve", "pe"])
        if self.deferred is not None:
            self.cur_sring = self.psr
            fn_, self.deferred = self.deferred, None
            fn_()

    def l0_s5(self):
        c, d = self.cfg, self.d
        self.phase()
        A = self.alloc
        L = 128
        lre, lim, dtv = A("lre", [128, 32], F32), A("lim", [128, 32], F32), A("dtv", [128, 32], F32)
        self.load(lre, lre[:], d["s5_a_re"], d["s5_a_re"].t.rearrange("(b two) n -> (two n) b", two=2), slow=True)
        self.load(lim, lim[:], d["s5_a_im"], d["s5_a_im"].t.rearrange("(b two) n -> (two n) b", two=2), slow=True)
        ldt = d["s5_log_dt"].t.rearrange("(b two) -> two b", two=2)
        for two in range(2):
            self.load(dtv, dtv[two * 64:(two + 1) * 64, :], d["s5_log_dt"], ldt[two].partition_broadcast(64), part=True, slow=True)
        self.ts("dve", lre[:], lre[:], -1e-4, None, ALU.min, None, [lre], [lre])
        self.act(dtv[:], dtv[:], AF.Exp, [dtv], [dtv])
        mag, ang = A("mag", [128, 32], F32), A("ang", [128, 32], F32)
        self.tt("dve", mag[:], lre[:], dtv[:], ALU.mult, [lre, dtv], [mag])
        self.act(mag[:], mag[:], AF.Exp, [mag], [mag])
        self.tt("dve", ang[:], lim[:], dtv[:], ALU.mult, [lim, dtv], [ang])
        cs1, sn1 = A("cs1", [128, 32], F32), A("sn1", [128, 32], F32)
        abre, abim = A("abre", [128, 32], F32), A("abim", [128, 32], F32)
        den, w1, w2 = A("den", [128, 32], F32), A("w1", [128, 32], F32), A("w2", [128, 32], F32)
        core, coim = A("core", [128, 32], F32), A("coim", [128, 32], F32)
        io = A("iota", [128, L], F32)
        ct, st = A("ct", [128, 32, L], F32), A("st", [128, 32, L], F32)
        XBT = A("XBT", [128, 32, 2, 128], BF16)
        YT = A("YT", [128, 32, 2, 128], BF16)
        wg = A("wglu", [128, 8, 1024], BF16)
        hpr, hpi = A("hpr", [128, 32], F32), A("hpi", [128, 32], F32)
        mark = self.aoff
        t1, t2 = A("t1", [128, 4096], F32), A("t2", [128, 4096], I32)
        self.sin_of(sn1, sn1[:], ang, ang[:], 0.0, t1, t2)
        self.sin_of(cs1, cs1[:], ang, ang[:], math.pi / 2, t1, t2)
        self.tt("dve", abre[:], mag[:], cs1[:], ALU.mult, [mag, cs1], [abre])
        self.tt("dve", abim[:], mag[:], sn1[:], ALU.mult, [mag, sn1], [abim])
        self.tt("dve", den[:], lre[:], lre[:], ALU.mult, [lre], [den])
        self.tt("dve", w1[:], lim[:], lim[:], ALU.mult, [lim], [w1])
        self.tt("dve", den[:], den[:], w1[:], ALU.add, [den, w1], [den])
        self.add("dve", lambda e: e.reciprocal(den[:], den[:]), reads=[den], writes=[den])
        self.ts("dve", w1[:], abre[:], -1.0, None, ALU.add, None, [abre], [w1])
        self.tt("dve", core[:], w1[:], lre[:], ALU.mult, [w1, lre], [core])
        self.tt("dve", w2[:], abim[:], lim[:], ALU.mult, [abim, lim], [w2])
        self.tt("dve", core[:], core[:], w2[:], ALU.add, [core, w2], [core])
        self.tt("dve", core[:], core[:], den[:], ALU.mult, [core, den], [core])
        self.tt("dve", coim[:], abim[:], lre[:], ALU.mult, [abim, lre], [coim])
        self.tt("dve", w2[:], w1[:], lim[:], ALU.mult, [w1, lim], [w2])
        self.tt("dve", coim[:], coim[:], w2[:], ALU.subtract, [coim, w2], [coim])
        self.tt("dve", coim[:], coim[:], den[:], ALU.mult, [coim, den], [coim])
        self.add("pool", lambda e: e.iota(io[:], pattern=[[1, L]], base=1, channel_multiplier=0, allow_small_or_imprecise_dtypes=True), writes=[io])
        targ = A("targ", [128, 32, L], F32)
        self.tt("dve", targ[:], io[:, :].unsqueeze(1).to_broadcast([128, 32, L]), ang[:, :].unsqueeze(2).to_broadcast([128, 32, L]),
                ALU.mult, [io, ang], [targ])
        targf = targ[:, :, :].rearrange("p a b -> p (a b)")
        self.sin_of(st, st[:, :, :].rearrange("p a b -> p (a b)"), targ, targf, 0.0, t1, t2)
        self.sin_of(ct, ct[:, :, :].rearrange("p a b -> p (a b)"), targ, targf, math.pi / 2, t1, t2)
        self.P.barrier()
        self.aoff = mark
        Zre, Zim = A("Zre", [128, 32, 128], F32), A("Zim", [128, 32, 128], F32)
        self.memset("pool", Zre, Zre[:], 0.0)
        self.memset("pool", Zim, Zim[:], 0.0)
        for g in range(64):
            b, two, gl = g // 2, g % 2, g % 8
            for (Z, src) in ((Zre, d["s5_b_re"]), (Zim, d["s5_b_im"])):
                self.load(Z, Z[two * 64:(two + 1) * 64, b, gl * 16:(gl + 1) * 16], src, src.t[g], part=True)
        Wre, Wim = A("Wre", [128, 32, 128], F32), A("Wim", [128, 32, 128], F32)
        bc = lambda t: t[:, :].unsqueeze(2).to_broadcast([128, 32, 128])
        t1 = A("tz", [128, 32, 128], F32)
        tmpz = t1[:, :, :]
        self.tt("dve", Wre[:], Zre[:], bc(core), ALU.mult, [Zre, core], [Wre])
        self.tt("dve", tmpz, Zim[:], bc(coim), ALU.mult, [Zim, coim], [t1])
        self.tt("dve", Wre[:], Wre[:], tmpz, ALU.subtract, [Wre, t1], [Wre])
        self.tt("dve", Wim[:], Zim[:], bc(core), ALU.mult, [Zim, core], [Wim])
        self.tt("dve", tmpz, Zre[:], bc(coim), ALU.mult, [Zre, coim], [t1])
        self.tt("dve", Wim[:], Wim[:], tmpz, ALU.add, [Wim, t1], [Wim])
        for b in range(32):
            ps = self.psr.next()
            self.tr(ps, ps[:, 0:128], Wre[:, b, :], self.identf[:], [Wre, self.identf], first=True)
            self.tr(ps, ps[:, 128:256], Wim[:, b, :], self.identf[:], [Wim, self.identf], first=False)
            self.cp("act", XBT[:, b, :, :], ps[:, 0:256].rearrange("p (r s) -> p r s", s=128), [ps], pwrites=[XBT])
        self.P.barrier()
        Yre, Yim = Zre, Zim
        self.memset("pool", Yre, Yre[:], 0.0)
        self.memset("pool", Yim, Yim[:], 0.0)
        for g in range(64):
            b, two, gl = g // 2, g % 2, g % 8
            for (Y, src) in ((Yre, d["s5_c_re"]), (Yim, d["s5_c_im"])):
                self.load(Y, Y[gl * 16:(gl + 1) * 16, b, two * 64:(two + 1) * 64], src, src.t[g], part=True)
        for b in range(32):
            ps = self.psr.next()
            self.tr(ps, ps[:, 0:128], Yre[:, b, :], self.identf[:], [Yre, self.identf], first=True)
            self.tr(ps, ps[:, 128:256], Yim[:, b, :], self.identf[:], [Yim, self.identf], first=False)
            self.cp("act", YT[:, b, 0, :], ps[:, 0:128], [ps], pwrites=[YT])
            self.act(YT[:, b, 1, :], ps[:, 128:256], AF.Copy, [ps], pwrites=[YT], scale=-1.0)
        self.load(wg, wg[:], d["s5_w_glu_b"], d["s5_w_glu_b"].t.rearrange("(k p) n -> p k n", p=128))
        self.P.barrier()
        self.aoff = mark
        ub_r = Ring([A("ub", [128, 8, 512], BF16) for _ in range(2)])
        yT_r = Ring([A("yT", [128, 8, 512], F32) for _ in range(1)])
        zT_r = Ring([A("zT", [128, 8, 512], BF16) for _ in range(1)])
        oc_t = Ring([A("o8", [128, 8, 128], F32) for _ in range(8)])
        hb_r = Ring([A("hb", [128, 2, 8, 128], BF16) for _ in range(2)])
        g_t = Ring([A("gt", [128, 512], F32) for _ in range(3)])
        os_r = Ring([A("s5o", [128, 8, 512], BF16) for _ in range(1)])
        bg_s = Ring([A("bgs", [128, 256], F32) for _ in range(3)])
        bg_b = Ring([A("bgb", [128, 256], BF16) for _ in range(3)])
        n_iter = sum(((T_ + L - 1) // L) * 4 for (_, T_, _) in [(0, c.P_LEN, None)] + [(0, c.SL, i) for i in range(c.NS)])
        per_iter = (len(self.bg_jobs) + n_iter - 1) // n_iter
        print("S5 arena use", self.aoff, "of", self.arena_words, "bg jobs", len(self.bg_jobs), "per iter", per_iter)
        seqs = [(0, c.P_LEN, None)] + [(c.P_LEN + i * c.SL, c.SL, i) for i in range(c.NS)]
        for sq_i, (tok0, T_, si) in enumerate(seqs):
            if si is None:
                self.memset("dve", hpr, hpr[:], 0.0)
                self.memset("dve", hpi, hpi[:], 0.0)
            else:
                self.load(hpr, hpr[:], d["s5re0"], d["s5re0"].t[si].rearrange("(b two) n -> (two n) b", two=2), slow=True)
                self.load(hpi, hpi[:], d["s5im0"], d["s5im0"].t[si].rearrange("(b two) n -> (two n) b", two=2), slow=True)
            for m0 in range(0, T_, 512):
                nt = min(512, T_ - m0)
                ub, yT, zT = ub_r.next(), yT_r.next(), zT_r.next()
                uf = ub
                self.load(ub, ub[:, :, 0:nt], d["uT_s"], d["uT_s"].t[:, tok0 + m0:tok0 + m0 + nt].rearrange("(c p) t -> p c t", p=128))
                for s0 in range(0, nt, L):
                    ln = min(L, nt - s0)
                    for o in range(4):
                        T1, T2, T3, T4 = oc_t.next(), oc_t.next(), oc_t.next(), oc_t.next()
                        for half in range(2):
                            pxr, pxi = self.psr.next(), self.psr.next()
                            for bb in range(4):
                                b = o * 8 + half * 4 + bb
                                self.mm(pxr, pxr[:, bb * 128:bb * 128 + ln], XBT[:, b, 0, :], ub[:, b // 4, s0:s0 + ln], True, True, [XBT, ub], excl=(bb == 0))
                                self.mm(pxi, pxi[:, bb * 128:bb * 128 + ln], XBT[:, b, 1, :], ub[:, b // 4, s0:s0 + ln], True, True, [XBT, ub], excl=(bb == 0))
                            b0 = o * 8 + half * 4
                            hs = slice(half * 4, half * 4 + 4)
                            vr = pxr[:, :].rearrange("p (b t) -> p b t", t=128)[:, :, 0:ln]
                            vi = pxi[:, :].rearrange("p (b t) -> p b t", t=128)[:, :, 0:ln]
                            cta, sta = ct[:, b0:b0 + 4, 0:ln], st[:, b0:b0 + 4, 0:ln]
                            self.tt("dve", T1[:, hs, 0:ln], vr, cta, ALU.mult, [pxr, ct], pwrites=[T1])
                            self.tt("dve", T2[:, hs, 0:ln], vi, sta, ALU.mult, [pxi, st], pwrites=[T2])
                            self.tt("dve", T3[:, hs, 0:ln], vi, cta, ALU.mult, [pxi, ct], pwrites=[T3])
                            self.tt("dve", T4[:, hs, 0:ln], vr, sta, ALU.mult, [pxr, st], pwrites=[T4])
                        if self.debug and sq_i == 0 and m0 == 0 and s0 == 0 and o == 0:
                            for nm_, tl in (("T1", T1), ("T2", T2), ("T3", T3), ("T4", T4)):
                                dt_ = self.P.dram("dbg_%s" % nm_, [128, 8, 128], F32, "ExternalOutput")
                                self.store(dt_, dt_.t, tl, tl[:, :, :])
                            dpx = self.alloc("dpx", [128, 512], F32)
                            self.cp("dve", dpx[:], pxr[:, :], [pxr], [dpx])
                            dt_ = self.P.dram("dbg_pxr", [128, 512], F32, "ExternalOutput")
                            self.store(dt_, dt_.t, dpx, dpx[:])
                        self.tt("pool", T1[:, :, 0:ln], T1[:, :, 0:ln], T2[:, :, 0:ln], ALU.add, [T1, T2], [T1])
                        self.tt("pool", T3[:, :, 0:ln], T3[:, :, 0:ln], T4[:, :, 0:ln], ALU.subtract, [T3, T4], [T3])
                        Gr, Gi = T2, T4
                        for bb in range(8):
                            b = o * 8 + bb
                            self.add("dve", lambda e, o_=Gr[:, bb, 0:ln], a=mag[:, b:b + 1].to_broadcast([128, ln]), x_=T1[:, bb, 0:ln], i_=hpr[:, b:b + 1]:
                                     e.tensor_tensor_scan(out=o_, data0=a, data1=x_, initial=i_, op0=ALU.mult, op1=ALU.add),
                                     reads=[mag, T1, hpr], pwrites=[Gr])
                            self.add("dve", lambda e, o_=Gi[:, bb, 0:ln], a=mag[:, b:b + 1].to_broadcast([128, ln]), x_=T3[:, bb, 0:ln], i_=hpi[:, b:b + 1]:
                                     e.tensor_tensor_scan(out=o_, data0=a, data1=x_, initial=i_, op0=ALU.mult, op1=ALU.add),
                                     reads=[mag, T3, hpi], pwrites=[Gi])
                        cta, sta = ct[:, o * 8:o * 8 + 8, 0:ln], st[:, o * 8:o * 8 + 8, 0:ln]
                        Hr, Hi, U1, U2 = oc_t.next(), oc_t.next(), T1, T3
                        self.tt("pool", Hr[:, :, 0:ln], Gr[:, :, 0:ln], cta, ALU.mult, [Gr, ct], [Hr])
                        self.tt("pool", U1[:, :, 0:ln], Gi[:, :, 0:ln], sta, ALU.mult, [Gi, st], [U1])
                        self.tt("pool", Hr[:, :, 0:ln], Hr[:, :, 0:ln], U1[:, :, 0:ln], ALU.subtract, [Hr, U1], [Hr])
                        self.tt("dve", Hi[:, :, 0:ln], Gi[:, :, 0:ln], cta, ALU.mult, [Gi, ct], [Hi])
                        self.tt("dve", U2[:, :, 0:ln], Gr[:, :, 0:ln], sta, ALU.mult, [Gr, st], [U2])
                        self.tt("dve", Hi[:, :, 0:ln], Hi[:, :, 0:ln], U2[:, :, 0:ln], ALU.add, [Hi, U2], [Hi])
                        if self.debug and sq_i == 0 and m0 == 0 and s0 == 0 and o == 0:
                            for nm_, tl, shp, dty in (("XBT", XBT, [128, 32, 2, 128], BF16), ("YT", YT, [128, 32, 2, 128], BF16), ("ct", ct, [128, 32, 128], F32),
                                                      ("st", st, [128, 32, 128], F32), ("core", core, [128, 32], F32), ("mag", mag, [128, 32], F32),
                                                      ("ang", ang, [128, 32], F32), ("ub", ub, [128, 8, 512], BF16)):
                                dt_ = self.P.dram("dbg_" + nm_, shp, dty, "ExternalOutput")
                                self.store(dt_, dt_.t, tl, tl.t)
                        if self.debug and sq_i == 0 and m0 == 0 and s0 in (0, 128) and o == 0:
                            for nm_, tl in (("xr", T1), ("xi", T3), ("gr", Gr), ("gi", Gi), ("hr", Hr), ("hi", Hi)):
                                dt_ = self.P.dram("dbg_%s_%d" % (nm_, s0), [128, 8, 128], F32, "ExternalOutput")
                                self.store(dt_, dt_.t, tl, tl[:, :, :])
                            dt_ = self.P.dram("dbg_hpr_%d" % s0, [128, 32], F32, "ExternalOutput")
                            self.store(dt_, dt_.t, hpr, hpr[:, :])
                        self.cp("act", hpr[:, o * 8:o * 8 + 8], Hr[:, :, ln - 1], [Hr], pwrites=[hpr])
                        self.cp("act", hpi[:, o * 8:o * 8 + 8], Hi[:, :, ln - 1], [Hi], pwrites=[hpi])
                        hb = hb_r.next()
                        self.cp("act", hb[:, 0, :, 0:ln], Hr[:, :, 0:ln], [Hr], pwrites=[hb])
                        self.cp("act", hb[:, 1, :, 0:ln], Hi[:, :, 0:ln], [Hi], pwrites=[hb])
                        for _ in range(per_iter):
                            if self.bg_jobs:
                                self.conv_emit(self.bg_jobs.pop(0), bg_s.next(), bg_b.next(), "act", "act")
                        for fo in range(2):
                            fc = o * 2 + fo
                            ps = self.psr.next()
                            n_mm = 0
                            for bl in range(4):
                                for ri in range(2):
                                    self.mm(ps, ps[:, 0:ln], YT[:, fc * 4 + bl, ri, :], hb[:, ri, fo * 4 + bl, 0:ln], n_mm == 0, n_mm == 7, [YT, hb])
                                    n_mm += 1
                            self.stt("dve", yT[:, fc, s0:s0 + ln], uf[:, fc, s0:s0 + ln], self.s5d[:, fc, :], ps[:, 0:ln], ALU.mult, ALU.add,
                                     [uf, ps, self.s5d], pwrites=[yT])
                for fc in range(8):
                    g1, g2 = g_t.next(), g_t.next()
                    y = yT[:, fc, 0:nt]
                    self.tt("pool", g1[:, 0:nt], y, y, ALU.mult, [yT], [g1])
                    self.ts("pool", g1[:, 0:nt], g1[:, 0:nt], 0.044715, 1.0, ALU.mult, ALU.add, [g1], [g1])
                    self.tt("pool", g1[:, 0:nt], g1[:, 0:nt], y, ALU.mult, [g1, yT], [g1])
                    self.act(g2[:, 0:nt], g1[:, 0:nt], AF.Sigmoid, [g1], [g2], scale=1.5957691216057308)
                    self.tt("dve", zT[:, fc, 0:nt], g2[:, 0:nt], y, ALU.mult, [g2, yT], pwrites=[zT])
                so = os_r.next()
                for oc in range(8):
                    ps = self.psr.next()
                    for k in range(8):
                        self.mm(ps, ps[:, 0:nt], wg[:, k, oc * 128:(oc + 1) * 128], zT[:, k, 0:nt], k == 0, k == 7, [wg, zT])
                    g2 = g_t.next()
                    self.act(g2[:, 0:nt], ps[:, 0:nt], AF.Sigmoid, [ps, self.bglu], [g2], bias=self.bglu[:, oc, :])
                    self.tt("dve", so[:, oc, 0:nt], g2[:, 0:nt], zT[:, oc, 0:nt], ALU.mult, [g2, zT], pwrites=[so])
                self.store(d["mixT_s"], d["mixT_s"].t[0:1024, tok0 + m0:tok0 + m0 + nt].rearrange("(c p) t -> p c t", p=128), so, so[:, :, 0:nt])
            if sq_i == len(seqs) - 1:
                while self.bg_jobs:
                    self.conv_emit(self.bg_jobs.pop(0), bg_s.next(), bg_b.next(), "act", "act")
            self.store(d["s5re_o"], d["s5re_o"].t[sq_i].rearrange("(b two) n -> (two n) b", two=2), hpr, hpr[:], slow=True)
            self.store(d["s5im_o"], d["s5im_o"].t[sq_i].rearrange("(b two) n -> (two n) b", two=2), hpi, hpi[:], slow=True)

    def mix_ffn(self, layer, mixsrc, wout, xsrc, final):
        c, d = self.cfg, self.d
        self.phase()
        A = self.alloc
        self.slabs = Ring([A("slab", [128, 16 * 512], BF16) for _ in range(3)])
        xr = Ring([A("x1", [128, D], F32) for _ in range(4)])
        self.xnr = Ring([A("xn", [128, D], BF16) for _ in range(4)])
        self.ssr = Ring([A("ss", [128, 1], F32) for _ in range(8)])
        hT = A("hT", [128, 16, 512], BF16)
        big = A("big", [128, NJ, 512], BF16)
        gt_r = Ring([A("gt", [128, 2 * 2 + 512], F32) for _ in range(3)])
        cc_r = Ring([A("cc", [128, 512], F32) for _ in range(3)])
        cso_r = Ring([A("cso", [2, 512], F32) for _ in range(3)])
        win, wdn = d["ffn_w_in_b%d" % layer], d["ffn_w_down_b%d" % layer]
        for (tok0, nt, segs) in c.macros:
            S = nt // 128
            self.load(big, big[:, 0:16, 0:nt], mixsrc, mixsrc.t[:, tok0:tok0 + nt].rearrange("(c p) t -> p c t", p=128))
            xts = []
            for s in range(S):
                xt = xr.next()
                self.load(xt, xt[:], xsrc, xsrc.t[tok0 + s * 128:tok0 + (s + 1) * 128, :])
                xts.append(xt)
            for nb in range(4):
                sl, sv = self.slab(wout, 0, 16, nb * 512, 512)
                for s in range(S):
                    ps = self.formB(sl, lambda k, sv=sv: sv[:, k, :], 16, big, s, 512)
                    self.tt("dve", xts[s][:, nb * 512:(nb + 1) * 512], ps[:, :], xts[s][:, nb * 512:(nb + 1) * 512], ALU.add, [ps, xts[s]], [xts[s]])
            self.norm_T(xts, D, self.gffn[:, layer], hT)
            ends = [(col0 + ln - 2, seq) for (col0, ln, seq) in segs if (seq != 0 or tok0 + nt == c.P_LEN)]
            for j in range(11):
                slv, svv = self.slab(win, 0, 16, j * 512, 512)
                slg, svg = self.slab(win, 0, 16, DFF + j * 512, 512)
                for oc in range(4):
                    ch = j * 4 + oc
                    pv = self.formA(slv, svv, oc, 16, hT, nt)
                    pg = self.formA(slg, svg, oc, 16, hT, nt)
                    gt, cc = gt_r.next(), cc_r.next()
                    off = 0
                    for (col0, ln, seq) in segs:
                        gseg = gt[:, off:off + 2 + ln]
                        self.cp("act", gseg[:, 0:2], self.halo[:, layer, seq, ch, :], [self.halo], pwrites=[gt])
                        self.cp("act", gseg[:, 2:2 + ln], pg[:, col0:col0 + ln], [pg], pwrites=[gt])
                        w = self.convw[:, layer, ch, :]
                        self.ts("dve", cc[:, col0:col0 + ln], gseg[:, 2:2 + ln], w[:, 2:3], self.convb[:, layer, ch, :], ALU.mult, ALU.add,
                                [gt, self.convw, self.convb], pwrites=[cc])
                        self.stt("dve", cc[:, col0:col0 + ln], gseg[:, 1:1 + ln], w[:, 1:2], cc[:, col0:col0 + ln], ALU.mult, ALU.add,
                                 [gt, cc, self.convw], pwrites=[cc])
                        self.stt("dve", cc[:, col0:col0 + ln], gseg[:, 0:ln], w[:, 0:1], cc[:, col0:col0 + ln], ALU.mult, ALU.add,
                                 [gt, cc, self.convw], pwrites=[cc])
                        self.cp("act", self.halo[:, layer, seq, ch, :], gseg[:, ln:ln + 2], [gt], pwrites=[self.halo])
                        off += 2 + ln
                    self.act(cc[:, 0:nt], cc[:, 0:nt], AF.Silu, [cc], [cc])
                    self.tt("dve", big[:, ch, 0:nt], cc[:, 0:nt], pv[:, 0:nt], ALU.mult, [cc, pv], pwrites=[big])
                for (tcol, seq) in ends:
                    ps = self.psr.next()
                    for k in range(16):
                        self.mm(ps, ps[0:2, :], hT[:, k, tcol:tcol + 2], svg[:, k, :], k == 0, k == 15, [hT, slg])
                    cso = cso_r.next()
                    self.cp("act", cso[0:2, :], ps[0:2, :], [ps], [cso])
                    self.store(d["conv_o"], d["conv_o"].t[layer, seq, :, j * 512:(j + 1) * 512], cso, cso[0:2, :])
            for nb in range(4):
                pd = [self.psr.next() for _ in range(S)]
                for kh in range(4):
                    sl, sv = self.slab(wdn, kh * 11, 11, nb * 512, 512)
                    for s in range(S):
                        for k in range(11):
                            self.mm(pd[s], pd[s][:, :], big[:, kh * 11 + k, s * 128:(s + 1) * 128], sv[:, k, :],
                                    kh == 0 and k == 0, kh == 3 and k == 10, [big, sl])
                for s in range(S):
                    self.tt("dve", xts[s][:, nb * 512:(nb + 1) * 512], pd[s][:, :], xts[s][:, nb * 512:(nb + 1) * 512], ALU.add,
                            [pd[s], xts[s]], [xts[s]])
            if not final:
                for s in range(S):
                    self.store(d["x_s"], d["x_s"].t[tok0 + s * 128:tok0 + (s + 1) * 128, :], xts[s], xts[s][:])
            else:
                for s in range(S):
                    xt = xts[s]
                    ss, xn = self.ssr.next(), self.xnr.next()
                    self.act(xn[:, :], xt[:, :], AF.Square, [xt], [xn, ss], accum=ss[:])
                    self.act(ss[:], ss[:], AF.Sqrt, [ss, self.eps_t], [ss], bias=self.eps_t[:], scale=1.0 / D)
                    self.add("dve", lambda e, o=ss[:]: e.reciprocal(o, o), reads=[ss], writes=[ss])
                    self.stt("dve", xt[:, :], xt[:, :], ss[:, 0:1], self.gfin[:, :], ALU.mult, ALU.mult, [xt, ss, self.gfin], [xt])
                    self.store(d["y_all"], d["y_all"].t[tok0 + s * 128:tok0 + (s + 1) * 128, :], xt, xt[:])

    def mla_expand(self, ckvT, nt, kslices, wkv_slabs, stg, stv):
        d = self.d
        S = (nt + 127) // 128
        for qd in range(4):
            sl, sv = wkv_slabs(qd)
            sv4 = sv.rearrange("p k (h e) -> p k h e", e=256)
            so = stg.next()
            for hh in range(4):
                ps = self.psr.next()
                for k in range(4):
                    self.mm(ps, ps[:, 0:nt], sv4[:, k, hh, 0:128], ckvT[:, k, 0:nt], k == 0, k == 3, [sl, ckvT])
                self.cp("act" if hh % 2 else "dve", so[:, hh, 0:nt], ps[:, 0:nt], [ps], pwrites=[so])
            for (col0, ln, k0) in kslices:
                self.store(d["knT_s"], d["knT_s"].t[qd * 512:(qd + 1) * 512, k0:k0 + ln].rearrange("(c p) t -> p c t", p=128),
                           so, so[:, :, col0:col0 + ln])
            vo = stv.next()
            for s in range(S):
                ps = self.psr.next()
                for k in range(4):
                    self.mm(ps, ps[:, :].rearrange("p (h e) -> p h e", e=128), ckvT[:, k, s * 128:(s + 1) * 128], sv4[:, k, :, 128:256],
                            k == 0, k == 3, [sl, ckvT])
                self.cp("act" if s % 2 else "dve", vo[:, s, :], ps[:, :], [ps], pwrites=[vo])
            for (col0, ln, k0) in kslices:
                for s in range(S):
                    a, b = max(col0, s * 128), min(col0 + ln, (s + 1) * 128)
                    if a >= b:
                        continue
                    self.store(d["v1_s"], d["v1_s"].t[k0 + a - col0:k0 + b - col0, qd * 512:(qd + 1) * 512], vo, vo[a - s * 128:b - s * 128, s, :])

    def l1_past(self):
        c, d = self.cfg, self.d
        self.phase()
        A = self.alloc
        self.slabs = Ring([A("slab", [128, 16 * 512], BF16) for _ in range(3)])
        wkv_slabs = lambda qd: self.slab(d["mla_w_ukv_b"], 0, 4, qd * 1024, 1024)
        ckvT = A("ckvT", [128, 4, 512], BF16)
        stg = Ring([A("stg", [128, 4, 512], BF16) for _ in range(2)])
        stv = Ring([A("stv", [128, 4, 512], BF16) for _ in range(2)])
        kpb = Ring([A("kpb", [128, 64], BF16) for _ in range(2)])
        kpT = A("kpT", [64, 512], BF16)
        pf = Ring([A("pf", [128, 4, 512], F32) for _ in range(2)])
        pbf = Ring([A("pbf", [128, 512], BF16) for _ in range(4)])
        pk = Ring([A("pk", [128, 4, 64], F32) for _ in range(2)])
        for i in range(c.NS):
            k0 = c.koff(i)
            for t0 in range(0, c.PAST, 512):
                n = min(512, c.PAST - t0)
                S = n // 128
                f = pf.next()
                self.load(f, f[:, 0:S, :], d["cckv"], d["cckv"].t[i, t0:t0 + n, :].rearrange("(s p) r -> p s r", p=128))
                bts = []
                for s in range(S):
                    b = pbf.next()
                    self.cp("dve" if s % 2 else "pool", b[:], f[:, s, :], [f], [b])
                    bts.append(b)
                for cch in range(4):
                    ps = self.psr.next()
                    pb = ps.t[:, :].bitcast(BF16)
                    for s in range(S):
                        self.tr(ps, pb[:, s * 128:(s + 1) * 128], bts[s][:, cch * 128:(cch + 1) * 128], self.identb[:], [bts[s], self.identb], first=(s == 0))
                    self.cp("act", ckvT[:, cch, 0:n], pb[:, 0:n], [ps], pwrites=[ckvT])
                self.mla_expand(ckvT, n, [(0, n, k0 + t0)], wkv_slabs, stg, stv)
                kf = pk.next()
                self.load(kf, kf[:, 0:S, :], d["ckpe"], d["ckpe"].t[i, t0:t0 + n, :].rearrange("(s p) r -> p s r", p=128))
                ps = self.psr.next()
                pb = ps.t[:, :].bitcast(BF16)
                for s in range(S):
                    kb = kpb.next()
                    self.cp("dve", kb[:], kf[:, s, :], [kf], [kb])
                    self.tr(ps, pb[0:64, s * 128:(s + 1) * 128], kb[:, :], self.identb[:], [kb, self.identb], first=(s == 0))
                self.cp("act", kpT[:, 0:n], pb[0:64, 0:n], [ps], [kpT])
                self.store(d["kpeT_s"], d["kpeT_s"].t[:, k0 + t0:k0 + t0 + n], kpT, kpT[:, 0:n])

    def l1_a(self):
        c, d = self.cfg, self.d
        self.phase()
        A = self.alloc
        self.slabs = Ring([A("slab", [128, 16 * 512], BF16) for _ in range(2)])
        wkv_slabs = lambda qd: self.slab(d["mla_w_ukv_b"], 0, 4, qd * 1024, 1024)
        xr = Ring([A("x", [128, D], F32) for _ in range(4)])
        self.xnr = Ring([A("xn", [128, D], BF16) for _ in range(4)])
        self.ssr = Ring([A("ss", [128, 1], F32) for _ in range(12)])
        hT = A("hT", [128, 16, 512], BF16)
        tk_r = Ring([A("tk", [128, 1088], F32) for _ in range(4)])
        cqT, ckvT = A("cqT", [128, 4, 512], BF16), A("ckvT", [128, 4, 512], BF16)
        stg = Ring([A("stg", [128, 4, 512], BF16) for _ in range(2)])
        stv = Ring([A("stv", [128, 4, 512], BF16) for _ in range(2)])
        ckvo = Ring([A("ckvo", [128, 512], F32) for _ in range(2)])
        kpo = Ring([A("kpo", [128, 64], F32) for _ in range(2)])
        kpb = Ring([A("kpb", [128, 64], BF16) for _ in range(2)])
        kpT = A("kpT", [64, 512], BF16)
        qpb = Ring([A("qpb", [128, 1024], BF16) for _ in range(4)])
        qpT = A("qpT", [128, 8, 512], BF16)
        rt = Ring([A("rt", [128, 256], F32) for _ in range(6)])
        scale = 192.0 ** -0.5
        for (tok0, nt, segs) in c.macros:
            S = nt // 128
            koffs = self.key_cols(tok0, nt, segs)
            xts = []
            for s in range(S):
                xt = xr.next()
                self.load(xt, xt[:], d["x_s"], d["x_s"].t[tok0 + s * 128:tok0 + (s + 1) * 128, :])
                xts.append(xt)
            self.norm_T(xts, D, self.gmix[:, 1], hT)
            tks = [tk_r.next() for _ in range(S)]
            for (n0, n) in ((0, 512), (512, 512), (1024, 64)):
                sl, sv = self.slab(d["w_in_odd_b"], 0, 16, n0, n)
                for s in range(S):
                    ps = self.formB(sl, lambda k, sv=sv: sv[:, k, :], 16, hT, s, n)
                    self.cp("act" if s % 2 else "dve", tks[s][:, n0:n0 + n], ps[:, 0:n], [ps], pwrites=[tks[s]])
            cq_views = [T(self.nm("cqv"), tks[s][:, 0:512]) for s in range(S)]
            for s in range(S):
                cq_views[s].ws, cq_views[s].rs = tks[s].ws, tks[s].rs
            self.norm_T(cq_views, 512, self.qng[:], cqT)
            ckv_views = [T(self.nm("ckvv"), tks[s][:, 512:1024]) for s in range(S)]
            for s in range(S):
                ckv_views[s].ws, ckv_views[s].rs = tks[s].ws, tks[s].rs
            rstds = self.norm_T(ckv_views, 512, self.kvng[:], ckvT)
            for s in range(S):
                o = ckvo.next()
                self.stt("dve", o[:], tks[s][:, 512:1024], rstds[s][:, 0:1], self.kvgb[:], ALU.mult, ALU.mult, [tks[s], rstds[s], self.kvgb], [o])
                self.store(d["ckv_o"], d["ckv_o"].t[tok0 + s * 128:tok0 + (s + 1) * 128, :], o, o[:])
            ps = self.psr.next()
            pb = ps.t[:, :].bitcast(BF16)
            for s in range(S):
                gs = (tok0 // 128) + s
                cs_, sn_ = self.ropec[:, gs, :], self.ropes[:, gs, :]
                x1, x2 = tks[s][:, 1024:1056], tks[s][:, 1056:1088]
                o = kpo.next()
                a1, a2 = rt.next(), rt.next()
                self.tt("pool", o[:, 0:32], x1, cs_, ALU.mult, [tks[s], self.ropec], pwrites=[o])
                self.tt("pool", a1[:, 0:32], x2, sn_, ALU.mult, [tks[s], self.ropes], [a1])
                self.tt("pool", o[:, 0:32], o[:, 0:32], a1[:, 0:32], ALU.subtract, [o, a1], pwrites=[o])
                self.tt("pool", o[:, 32:64], x2, cs_, ALU.mult, [tks[s], self.ropec], pwrites=[o])
                self.tt("pool", a2[:, 0:32], x1, sn_, ALU.mult, [tks[s], self.ropes], [a2])
                self.tt("pool", o[:, 32:64], o[:, 32:64], a2[:, 0:32], ALU.add, [o, a2], pwrites=[o])
                self.store(d["kpe_o"], d["kpe_o"].t[tok0 + s * 128:tok0 + (s + 1) * 128, :], o, o[:])
                kb = kpb.next()
                self.cp("dve", kb[:], o[:], [o], [kb])
                self.tr(ps, pb[0:64, s * 128:(s + 1) * 128], kb[:, :], self.identb[:], [kb, self.identb], first=(s == 0))
            self.cp("act", kpT[:, 0:nt], pb[0:64, 0:nt], [ps], [kpT])
            for (col0, ln, k0) in koffs:
                self.store(d["kpeT_s"], d["kpeT_s"].t[:, k0:k0 + ln], kpT, kpT[:, col0:col0 + ln])
            self.mla_expand(ckvT, nt, koffs, wkv_slabs, stg, stv)
            qbs = [qpb.next() for _ in range(S)]
            for hf in range(2):
                slq, svq = self.slab(d["mla_w_uq_b"], 0, 4, hf * 1536, 1536)
                wq4 = svq.rearrange("p k (h e) -> p k h e", e=192)
                for hq in range(2):
                    so = stg.next()
                    for hh in range(4):
                        hl = hq * 4 + hh
                        ps = self.psr.next()
                        for k in range(4):
                            self.mm(ps, ps[:, 0:nt], wq4[:, k, hl, 0:128], cqT[:, k, 0:nt], k == 0, k == 3, [slq, cqT])
                        self.act(so[:, hh, 0:nt], ps[:, 0:nt], AF.Copy, [ps], pwrites=[so], scale=scale)
                    r0 = (hf * 2 + hq) * 512
                    self.store(d["qnT_s"], d["qnT_s"].t[r0:r0 + 512, tok0:tok0 + nt].rearrange("(c p) t -> p c t", p=128), so, so[:, :, 0:nt])
                for s in range(S):
                    gs = (tok0 // 128) + s
                    qb = qbs[s]
                    ps = self.psr.next()
                    for k in range(4):
                        self.mm(ps, ps[:, :].rearrange("p (h e) -> p h e", e=64), cqT[:, k, s * 128:(s + 1) * 128], wq4[:, k, :, 128:192],
                                k == 0, k == 3, [slq, cqT])
                    p3 = ps[:, :].rearrange("p (h e) -> p h e", e=64)
                    x1, x2 = p3[:, :, 0:32], p3[:, :, 32:64]
                    cb_ = self.ropec[:, gs, :].unsqueeze(1).to_broadcast([128, 8, 32])
                    sb_ = self.ropes[:, gs, :].unsqueeze(1).to_broadcast([128, 8, 32])
                    a1, a2, a3, a4 = rt.next(), rt.next(), rt.next(), rt.next()
                    v3 = lambda t_: t_[:, 0:256].rearrange("p (h e) -> p h e", e=32)
                    self.tt("dve", v3(a1), x1, cb_, ALU.mult, [ps, self.ropec], [a1])
                    self.tt("dve", v3(a2), x2, sb_, ALU.mult, [ps, self.ropes], [a2])
                    self.tt("dve", v3(a3), x2, cb_, ALU.mult, [ps, self.ropec], [a3])
                    self.tt("dve", v3(a4), x1, sb_, ALU.mult, [ps, self.ropes], [a4])
                    q3 = qb[:, hf * 512:(hf + 1) * 512].rearrange("p (h e) -> p h e", e=64)
                    self.tt("pool", v3(a1), v3(a1), v3(a2), ALU.subtract, [a1, a2], [a1])
                    self.tt("pool", v3(a3), v3(a3), v3(a4), ALU.add, [a3, a4], [a3])
                    self.act(q3[:, :, 0:32], v3(a1), AF.Copy, [a1], pwrites=[qb], scale=scale)
                    self.act(q3[:, :, 32:64], v3(a3), AF.Copy, [a3], pwrites=[qb], scale=scale)
            for s in range(S):
                qb = qbs[s]
                for pr in range(8):
                    ps = self.psr.next()
                    pb = ps.t[:, :].bitcast(BF16)
                    self.tr(ps, pb[:, 0:128], qb[:, pr * 128:(pr + 1) * 128], self.identb[:], [qb, self.identb], first=True)
                    self.cp("act" if pr % 2 else "dve", qpT[:, pr, s * 128:(s + 1) * 128], pb[:, 0:128], [ps], pwrites=[qpT])
            self.store(d["qpeT_s"], d["qpeT_s"].t[:, tok0:tok0 + nt].rearrange("(c p) t -> p c t", p=128), qpT, qpT[:, :, 0:nt])

    def l1_attn(self):
        c, d = self.cfg, self.d
        self.phase()
        A = self.alloc
        seqs = [(0, c.P_LEN, 0, c.P_LEN)]
        for i in range(c.NS):
            seqs.append((c.koff(i), c.KSEQ, c.P_LEN + i * c.SL, c.SL))
        maxk = max(c.P_LEN, c.KSEQ)
        maxq = max(c.P_LEN, c.SL)
        nkt_max = (maxk + 127) // 128
        KTr = Ring([A("KT", [128, maxk], BF16) for _ in range(2)])
        QTr = Ring([A("QT", [128, maxq], BF16) for _ in range(2)])
        QPr = Ring([A("QP", [64, maxq], BF16) for _ in range(2)])
        Vr = Ring([A("V", [128, nkt_max, 128], BF16) for _ in range(2)])
        KP = A("KP", [64, maxk], BF16)
        self.er = Ring([A("E", [128, 512], BF16) for _ in range(6)])
        self.accr = Ring([A("dacc", [128, 512], F32) for _ in range(4)])
        tmp = Ring([A("atmp", [128, 512], F32) for _ in range(3)])
        ost = Ring([A("aost", [128, 512], BF16) for _ in range(3)])
        self.deferred = None
        for (key0, nkeys, tok0, ntok) in seqs:
            nfull, rem = nkeys // 128, nkeys % 128
            self.load(KP, KP[:, 0:nkeys], d["kpeT_s"], d["kpeT_s"].t[:, key0:key0 + nkeys])
            for h in range(16):
                KT, QT, QP, V = KTr.next(), QTr.next(), QPr.next(), Vr.next()
                self.load(KT, KT[:, 0:nkeys], d["knT_s"], d["knT_s"].t[h * 128:(h + 1) * 128, key0:key0 + nkeys])
                self.load(QT, QT[:, 0:ntok], d["qnT_s"], d["qnT_s"].t[h * 128:(h + 1) * 128, tok0:tok0 + ntok])
                self.load(QP, QP[:, 0:ntok], d["qpeT_s"], d["qpeT_s"].t[h * 64:(h + 1) * 64, tok0:tok0 + ntok])
                if nfull:
                    self.load(V, V[:, 0:nfull, :], d["v1_s"],
                              d["v1_s"].t[key0:key0 + nfull * 128, h * 128:(h + 1) * 128].rearrange("(t p) d -> p t d", p=128), part=True)
                if rem:
                    self.load(V, V[0:rem, nfull, :], d["v1_s"], d["v1_s"].t[key0 + nfull * 128:key0 + nkeys, h * 128:(h + 1) * 128], part=True)
                causal = (ntok == nkeys)
                for qm in range((ntok + 511) // 512):
                    nq = min(512, ntok - qm * 512)
                    qc0 = qm * 512
                    if causal:
                        ktiles = [(kt, 128, 0, False) for kt in range(4 * qm)] + \
                                 [(4 * qm + j, 128, 128 * j, True) for j in range(nq // 128)]
                    else:
                        ktiles = [(kt, 128, 0, False) for kt in range(nfull)] + ([(nfull, rem, 0, False)] if rem else [])
                    maps = [[(lambda kt, nk, KT=KT: KT[:, kt * 128:kt * 128 + nk],
                              lambda q0, n, QT=QT, qc0=qc0: QT[:, qc0 + q0:qc0 + q0 + n], [KT, QT]),
                             (lambda kt, nk: KP[:, kt * 128:kt * 128 + nk],
                              lambda q0, n, QP=QP, qc0=qc0: QP[:, qc0 + q0:qc0 + q0 + n], [KP, QP])]]
                    vfn = lambda kt, nk, V=V: (V[0:nk, kt, :], [V])

                    def fin(O, Dn, h=h, tok0=tok0, qc0=qc0, nq=nq):
                        r = tmp.next()
                        self.act(r[:, 0:nq], Dn[0][:, 0:nq], AF.Ln, [Dn[0]], [r])
                        self.act(r[:, 0:nq], r[:, 0:nq], AF.Exp, [r], [r], scale=-1.0)
                        ob = ost.next()
                        self.tt("dve", ob[:, 0:nq], O[0][:, 0:nq], r[:, 0:nq], ALU.mult, [O[0], r], [ob])
                        self.store(d["oT_s"], d["oT_s"].t[h * 128:(h + 1) * 128, tok0 + qc0:tok0 + qc0 + nq], ob, ob[:, 0:nq])
                    self.attention(maps, vfn, nq, ktiles, fin, ["pe"])

    def build(self):
        d_stop = self.stop_after
        self.declare()
        self.setup_consts()
        self.setup_params()
        steps = [("prologue", self.prologue), ("l0_a", self.l0_a), ("l0_past", self.l0_past), ("l0_s5", self.l0_s5),
                 ("l0_attn", self.l0_attn),
                 ("l0_ffn", lambda: self.mix_ffn(0, self.d["mixT_s"], self.d["w_out_even_b"], self.d["x_all"], False)),
                 ("l1_past", self.l1_past), ("l1_a", self.l1_a), ("l1_attn", self.l1_attn),
                 ("l1_ffn", lambda: self.mix_ffn(1, self.d["oT_s"], self.d["w_out_odd_b"], self.d["x_s"], True))]
        for name, fn in steps:
            fn()
            if d_stop == name:
                break
        self.P.emit()
        self.P.close()


def nc_free_bytes(nc):
    return int(nc.sbuf_bytes_remaining)


IN_KEYS_W = ["norm_mix", "norm_ffn", "norm_final", "w_in_even", "w_out_even", "s5_a_re", "s5_a_im", "s5_b_re", "s5_b_im",
             "s5_c_re", "s5_c_im", "s5_d", "s5_log_dt", "s5_w_glu", "s5_b_glu", "diff_lambda_q1", "diff_lambda_k1",
             "diff_lambda_q2", "diff_lambda_k2", "diff_subln", "w_in_odd", "mla_q_norm", "mla_kv_norm", "mla_w_uq",
             "mla_w_ukv", "w_out_odd", "ffn_w_in", "ffn_conv_w", "ffn_conv_b", "ffn_w_down"]


def core_inputs(inputs, cfg, pb, sbs, wshapes):
    f = lambda a: np.ascontiguousarray(np.asarray(a, dtype=np.float32))
    m = {}
    m["x_all"] = f(np.concatenate([inputs["x_prompt"][pb]] + [inputs["x_sample"][i] for i in sbs], axis=0))
    m["s5re0"] = f(np.stack([inputs["state_s5_re"][0, i] for i in sbs]))
    m["s5im0"] = f(np.stack([inputs["state_s5_im"][0, i] for i in sbs]))
    m["ck"] = f(np.stack([inputs["cache_diff_k"][0, i].reshape(cfg.PAST, 1024) for i in sbs]))
    m["cv"] = f(np.stack([inputs["cache_diff_v"][0, i].reshape(cfg.PAST, 1024) for i in sbs]))
    m["cckv"] = f(np.stack([inputs["cache_mla_ckv"][0, i] for i in sbs]))
    m["ckpe"] = f(np.stack([inputs["cache_mla_kpe"][0, i] for i in sbs]))
    m["convst"] = f(np.stack([np.stack([inputs["state_ffn_conv"][l, i] for i in sbs]) for l in range(2)]))
    for k in IN_KEYS_W:
        m[k] = f(np.asarray(inputs[k]).reshape(wshapes[k]))
    return m


_CACHE = {}


def get_program(cfg_key, debug=False, stop_after=None):
    key = (cfg_key, debug, stop_after)
    if key not in _CACHE:
        cfg = Cfg(*cfg_key)
        nc = bass.Bass("TRN2", target_bir_lowering=False)
        b = Builder(nc, cfg, debug=debug, stop_after=stop_after)
        b.build()
        _CACHE[key] = (nc, cfg, b)
    return _CACHE[key]


def kernel(**inputs):
    B, SEQ = inputs["x_prompt"].shape[0], inputs["x_prompt"].shape[1]
    DB, DS = inputs["x_sample"].shape[0], inputs["x_sample"].shape[1]
    PAST = inputs["cache_diff_k"].shape[2]
    ncores = 8
    ns = DB // ncores
    nc, cfg, b = get_program((SEQ, PAST, ns, DS))
    in_maps = []
    for c in range(ncores):
        in_maps.append(core_inputs(inputs, cfg, c % B, [c * ns + i for i in range(ns)], b.wshapes))
    res = run_bass_kernel_spmd(nc, in_maps, core_ids=list(range(ncores)))
    R = res.results
    P_LEN = cfg.P_LEN
    f32 = np.float32

    def prompt(name, shape_tail, sl=slice(0, P_LEN)):
        return np.stack([np.asarray(R[bb][name][sl]).reshape(shape_tail) for bb in range(B)]).astype(f32)

    def sample(name, shape_tail):
        out = []
        for c in range(ncores):
            for i in range(ns):
                t0 = P_LEN + i * DS
                out.append(np.asarray(R[c][name][t0:t0 + DS]).reshape(shape_tail))
        return np.stack(out).astype(f32)

    y_prompt = prompt("y_all", (P_LEN, D))
    y_sample = sample("y_all", (DS, D))
    s5rp = np.stack([R[bb]["s5re_o"][0] for bb in range(B)])[None].astype(f32)
    s5ip = np.stack([R[bb]["s5im_o"][0] for bb in range(B)])[None].astype(f32)
    s5rs = np.stack([R[c]["s5re_o"][1 + i] for c in range(ncores) for i in range(ns)])[None].astype(f32)
    s5is = np.stack([R[c]["s5im_o"][1 + i] for c in range(ncores) for i in range(ns)])[None].astype(f32)
    dkp = prompt("dk_o", (P_LEN, 8, 128))[None]
    dvp = prompt("dv_o", (P_LEN, 8, 128))[None]
    dks = sample("dk_o", (DS, 8, 128))[None]
    dvs = sample("dv_o", (DS, 8, 128))[None]
    ckp = prompt("ckv_o", (P_LEN, 512))[None]
    kpp = prompt("kpe_o", (P_LEN, 64))[None]
    cks = sample("ckv_o", (DS, 512))[None]
    kps = sample("kpe_o", (DS, 64))[None]
    cvp = np.stack([np.stack([R[bb]["conv_o"][l, 0] for bb in range(B)]) for l in range(2)]).astype(f32)
    cvs = np.stack([np.stack([R[c]["conv_o"][l, 1 + i] for c in range(ncores) for i in range(ns)]) for l in range(2)]).astype(f32)
    return (y_prompt, y_sample, s5rp, s5ip, s5rs, s5is, dkp, dvp, dks, dvs, ckp, kpp, cks, kps, cvp, cvs)
```
